# Optimizing a Trainium2 kernel written in Bass

```python
import math
import jax
import jax.numpy as jnp
from jax import lax
import numpy as np

D_MODEL = 1024
BATCH = 8
SEQ = 4096
DEPTH = 2

HEAD_DIM = 64
ATTN_WIDTH = 3 * D_MODEL // 8
ATTN_HEADS = ATTN_WIDTH // HEAD_DIM
ATTN_GROUPS = ((128, 1), (512, 4), (2048, 16))
ATTN_HEADS_PER_GROUP = ATTN_HEADS // len(ATTN_GROUPS)
RWKV_WIDTH = 3 * D_MODEL // 8
RWKV_HEADS = RWKV_WIDTH // HEAD_DIM
DECAY_LORA = 64
AAA_LORA = 64
GATE_LORA = 128
RWKV_IN = 3 * RWKV_WIDTH + DECAY_LORA + AAA_LORA + GATE_LORA
RWKV_SPLITS = (RWKV_WIDTH, 2 * RWKV_WIDTH, 3 * RWKV_WIDTH,
               3 * RWKV_WIDTH + DECAY_LORA, 3 * RWKV_WIDTH + DECAY_LORA + AAA_LORA)
GN_EPS = 64e-5
SSM_WIDTH = D_MODEL // 4
SSM_GROUP_CH = 16
SSM_GROUPS = SSM_WIDTH // SSM_GROUP_CH
SSM_STATE = 64
STEP_MIN = 1e-3
STEP_MAX = 1e-1
N_BRANCH = 3
ATTN_IN = 3 * ATTN_WIDTH
RWKV_OFF = ATTN_IN
SSM_OFF = RWKV_OFF + RWKV_IN
GATE_OFF = SSM_OFF + SSM_WIDTH
N_IN = GATE_OFF + N_BRANCH * D_MODEL
FFN_HIDDEN = ((8 * D_MODEL + 3 * 256 - 1) // (3 * 256)) * 256
NORM_EPS = 1e-6

kernel_name = "hybrid_dilated_rwkv7_s5_block"


def rmsnorm(z, gain):
    zf = z.astype(jnp.float32)
    y = zf * lax.rsqrt(jnp.mean(zf * zf, axis=-1, keepdims=True) + NORM_EPS) * gain.astype(jnp.float32)
    return y.astype(z.dtype)


def token_shift(z):
    return jnp.pad(z, ((0, 0), (1, 0), (0, 0)))[:, :-1]


def dilated_window_attention(q, k, v, window, dilation):
    bsz, s, h, e = q.shape
    n_back = window // dilation
    span = n_back * dilation
    s_pad = -(-s // span) * span
    nb = s_pad // span

    def blocks(z):
        z = jnp.pad(z, ((0, 0), (0, s_pad - s), (0, 0), (0, 0)))
        return z.reshape(bsz, nb, n_back, dilation, h, e)

    def with_prev(z):
        prev = jnp.pad(z, ((0, 0), (1, 0), (0, 0), (0, 0), (0, 0), (0, 0)))[:, :-1]
        return jnp.concatenate([prev, z], axis=2)

    qb = blocks(q).astype(jnp.float32)
    kb = with_prev(blocks(k)).astype(jnp.float32)
    vb = with_prev(blocks(v)).astype(jnp.float32)
    scores = jnp.einsum('bnqrhe,bnkrhe->bnrhqk', qb, kb) * (e ** -0.5)
    qi = jnp.arange(n_back)[:, None]
    kj = jnp.arange(2 * n_back)[None, :]
    band = (kj >= qi) & (kj <= qi + n_back)
    has_prev = (jnp.arange(nb) > 0)[:, None, None] | (jnp.arange(2 * n_back) >= n_back)[None, None, :]
    mask = band[None] & has_prev
    scores = jnp.where(mask[None, :, None, None], scores, -jnp.inf)
    mx = jnp.max(scores, axis=-1, keepdims=True)
    p = jnp.exp(scores - mx)
    den = jnp.sum(p, axis=-1, keepdims=True)
    out = jnp.einsum('bnrhqk,bnkrhe->bnqrhe', p / den, vb)
    lse = (mx + jnp.log(den))[..., 0]
    out = out.reshape(bsz, s_pad, h, e)[:, :s]
    lse = jnp.transpose(lse, (0, 1, 4, 2, 3)).reshape(bsz, s_pad, h)[:, :s]
    return out, lse


def dilated_mixture_attention(p_attn):
    bsz, s, _ = p_attn.shape
    q, k, v = (t.reshape(bsz, s, ATTN_HEADS, HEAD_DIM) for t in jnp.split(p_attn, 3, axis=-1))
    outs, lses = [], []
    for g, (window, dilation) in enumerate(ATTN_GROUPS):
        hs = slice(g * ATTN_HEADS_PER_GROUP, (g + 1) * ATTN_HEADS_PER_GROUP)
        o, lse = dilated_window_attention(q[:, :, hs], k[:, :, hs], v[:, :, hs], window, dilation)
        outs.append(o)
        lses.append(lse)
    alpha = jax.nn.softmax(jnp.stack(lses, axis=0), axis=0)
    o = jnp.concatenate([o_g * alpha[g][..., None] for g, o_g in enumerate(outs)], axis=2)
    return o.reshape(bsz, s, ATTN_WIDTH)


def wkv7_scan(r, w, k, v, a, b):
    bsz, _, h, e = r.shape

    def step(state, inp):
        r_t, w_t, k_t, v_t, a_t, b_t = inp
        sa = jnp.einsum('bhij,bhj->bhi', state, a_t)
        state = (state * w_t[:, :, None, :] + sa[..., None] * b_t[:, :, None, :]
                 + v_t[..., None] * k_t[:, :, None, :])
        return state, jnp.einsum('bhij,bhj->bhi', state, r_t)

    xs = tuple(jnp.moveaxis(t, 1, 0) for t in (r, w, k, v, a, b))
    _, y = lax.scan(step, jnp.zeros((bsz, h, e, e), jnp.float32), xs)
    return jnp.moveaxis(y, 0, 1)


def rwkv7_time_mix(p_rwkv, shift_mix, w0, w2, a0, a2, g2, k_k, k_a, r_k, ln_w, ln_b):
    bsz, s, _ = p_rwkv.shape
    z = p_rwkv.astype(jnp.float32)
    z = z + (token_shift(z) - z) * shift_mix
    r, k, v, xw, xa, xg = jnp.split(z, RWKV_SPLITS, axis=-1)
    w = -jax.nn.softplus(-(w0 + jnp.tanh(xw) @ w2)) - 0.5
    decay = jnp.exp(-jnp.exp(w))
    a = jax.nn.sigmoid(a0 + xa @ a2)
    g = jax.nn.sigmoid(xg) @ g2

    def heads(t):
        return t.reshape(bsz, s, RWKV_HEADS, HEAD_DIM)

    kk = heads(k * k_k)
    kk = kk / jnp.maximum(jnp.linalg.norm(kk, axis=-1, keepdims=True), 1e-12)
    k = k * (1.0 + (a - 1.0) * k_a)
    r_h, k_h, v_h, a_h = heads(r), heads(k), heads(v), heads(a)
    y = wkv7_scan(r_h, heads(decay), k_h, v_h, -kk, kk * a_h)
    mu = jnp.mean(y, axis=-1, keepdims=True)
    var = jnp.mean(jnp.square(y - mu), axis=-1, keepdims=True)
    y = ((y - mu) * lax.rsqrt(var + GN_EPS)).reshape(bsz, s, RWKV_WIDTH) * ln_w + ln_b
    bonus = jnp.sum(r_h * k_h * r_k, axis=-1, keepdims=True) * v_h
    return (y + bonus.reshape(bsz, s, RWKV_WIDTH)) * g


def _complex_linear_combine(e1, e2):
    a1r, a1i, b1r, b1i = e1
    a2r, a2i, b2r, b2i = e2
    return (a2r * a1r - a2i * a1i, a2r * a1i + a2i * a1r,
            a2r * b1r - a2i * b1i + b2r, a2r * b1i + a2i * b1r + b2i)


def s5_glu(p_ssm, a_re, a_im, log_step, b_re, b_im, c_re, c_im, d_skip, w_val, w_gate):
    bsz, s, _ = p_ssm.shape
    u = p_ssm.astype(jnp.float32).reshape(bsz, s, SSM_GROUPS, SSM_GROUP_CH)
    lam_re = a_re.astype(jnp.float32)
    lam_im = a_im.astype(jnp.float32)
    step = jnp.exp(log_step.astype(jnp.float32))[:, None]
    mag = jnp.exp(lam_re * step)
    ang = lam_im * step
    abar_re, abar_im = mag * jnp.cos(ang), mag * jnp.sin(ang)
    inv = 1.0 / (lam_re * lam_re + lam_im * lam_im)
    f_re = ((abar_re - 1.0) * lam_re + abar_im * lam_im) * inv
    f_im = (abar_im * lam_re - (abar_re - 1.0) * lam_im) * inv
    bbar_re = f_re[..., None] * b_re - f_im[..., None] * b_im
    bbar_im = f_re[..., None] * b_im + f_im[..., None] * b_re
    bu_re = jnp.einsum('bsgc,gpc->bsgp', u, bbar_re)
    bu_im = jnp.einsum('bsgc,gpc->bsgp', u, bbar_im)
    shape = bu_re.shape
    _, _, x_re, x_im = lax.associative_scan(
        _complex_linear_combine,
        (jnp.broadcast_to(abar_re, shape), jnp.broadcast_to(abar_im, shape), bu_re, bu_im),
        axis=1)
    y = (jnp.einsum('bsgp,gcp->bsgc', x_re, c_re) - jnp.einsum('bsgp,gcp->bsgc', x_im, c_im)
         + d_skip.reshape(SSM_GROUPS, SSM_GROUP_CH) * u)
    zg = jax.nn.gelu(y.reshape(bsz, s, SSM_WIDTH))
    return (zg @ w_val) * jax.nn.sigmoid(zg @ w_gate)


def swiglu(u, w_gate_up, w_down):
    a, b = jnp.split(u @ w_gate_up, 2, axis=-1)
    return (jax.nn.silu(a) * b) @ w_down


def setup_inputs(seed: int = 0) -> dict:
    key = jax.random.key(seed)
    ks = iter(jax.random.split(key, 40))
    f32 = jnp.float32

    def nrm(shape, scale):
        return jax.random.normal(next(ks), shape, f32) * scale

    def uni(shape, lo, hi):
        return jax.random.uniform(next(ks), shape, f32, lo, hi)

    L, D = DEPTH, D_MODEL
    G, P, C = SSM_GROUPS, SSM_STATE, SSM_GROUP_CH
    return {
        "x": nrm((BATCH, SEQ, D), 1.0),
        "norm_mix": 1.0 + nrm((L, D), 0.02),
        "w_in": nrm((L, D, N_IN), D ** -0.5),
        "rwkv_shift_mix": uni((L, RWKV_IN), 0.0, 1.0),
        "rwkv_w0": uni((L, RWKV_WIDTH), -6.0, -1.0),
        "rwkv_w2": nrm((L, DECAY_LORA, RWKV_WIDTH), 0.1 * DECAY_LORA ** -0.5),
        "rwkv_a0": nrm((L, RWKV_WIDTH), 0.1),
        "rwkv_a2": nrm((L, AAA_LORA, RWKV_WIDTH), 0.5 * AAA_LORA ** -0.5),
        "rwkv_g2": nrm((L, GATE_LORA, RWKV_WIDTH), GATE_LORA ** -0.5),
        "rwkv_k_k": 0.85 + nrm((L, RWKV_WIDTH), 0.02),
        "rwkv_k_a": 1.0 + nrm((L, RWKV_WIDTH), 0.02),
        "rwkv_r_k": nrm((L, RWKV_HEADS, HEAD_DIM), 0.1),
        "rwkv_ln_w": 1.0 + nrm((L, RWKV_WIDTH), 0.02),
        "rwkv_ln_b": nrm((L, RWKV_WIDTH), 0.02),
        "ssm_a_re": -0.5 + nrm((L, G, P), 0.01),
        "ssm_a_im": jnp.pi * jnp.arange(P, dtype=f32) + nrm((L, G, P), 0.01),
        "ssm_log_step": uni((L, G), math.log(STEP_MIN), math.log(STEP_MAX)),
        "ssm_b_re": nrm((L, G, P, C), (2 * C) ** -0.5),
        "ssm_b_im": nrm((L, G, P, C), (2 * C) ** -0.5),
        "ssm_c_re": nrm((L, G, C, P), P ** -0.5),
        "ssm_c_im": nrm((L, G, C, P), P ** -0.5),
        "ssm_d": nrm((L, SSM_WIDTH), 1.0),
        "ssm_glu_val": nrm((L, SSM_WIDTH, SSM_WIDTH), SSM_WIDTH ** -0.5),
        "ssm_glu_gate": nrm((L, SSM_WIDTH, SSM_WIDTH), SSM_WIDTH ** -0.5),
        "w_branch_attn": nrm((L, ATTN_WIDTH, D), ATTN_WIDTH ** -0.5),
        "w_branch_rwkv": nrm((L, RWKV_WIDTH, D), RWKV_WIDTH ** -0.5),
        "w_branch_ssm": nrm((L, SSM_WIDTH, D), SSM_WIDTH ** -0.5),
        "w_out": nrm((L, D, D), D ** -0.5),
        "norm_ffn": 1.0 + nrm((L, D), 0.02),
        "ffn_w_gate_up": nrm((L, D, 2 * FFN_HIDDEN), D ** -0.5),
        "ffn_w_down": nrm((L, FFN_HIDDEN, D), FFN_HIDDEN ** -0.5),
        "norm_final": 1.0 + nrm((D,), 0.02),
    }


def reference(x, norm_mix, w_in, rwkv_shift_mix, rwkv_w0, rwkv_w2, rwkv_a0, rwkv_a2, rwkv_g2,
              rwkv_k_k, rwkv_k_a, rwkv_r_k, rwkv_ln_w, rwkv_ln_b, ssm_a_re, ssm_a_im, ssm_log_step,
              ssm_b_re, ssm_b_im, ssm_c_re, ssm_c_im, ssm_d, ssm_glu_val, ssm_glu_gate,
              w_branch_attn, w_branch_rwkv, w_branch_ssm, w_out, norm_ffn, ffn_w_gate_up,
              ffn_w_down, norm_final):
    h = x
    for l in range(DEPTH):
        u = rmsnorm(h, norm_mix[l])
        p = u @ w_in[l]
        bsz, s, _ = p.shape
        y_attn = dilated_mixture_attention(p[..., :ATTN_IN]) @ w_branch_attn[l]
        y_rwkv = rwkv7_time_mix(p[..., RWKV_OFF:SSM_OFF], rwkv_shift_mix[l], rwkv_w0[l], rwkv_w2[l],
                                rwkv_a0[l], rwkv_a2[l], rwkv_g2[l], rwkv_k_k[l], rwkv_k_a[l],
                                rwkv_r_k[l], rwkv_ln_w[l], rwkv_ln_b[l]) @ w_branch_rwkv[l]
        y_ssm = s5_glu(p[..., SSM_OFF:GATE_OFF], ssm_a_re[l], ssm_a_im[l], ssm_log_step[l],
                       ssm_b_re[l], ssm_b_im[l], ssm_c_re[l], ssm_c_im[l], ssm_d[l],
                       ssm_glu_val[l], ssm_glu_gate[l]) @ w_branch_ssm[l]
        gates = jax.nn.sigmoid(p[..., GATE_OFF:].astype(jnp.float32)).reshape(bsz, s, N_BRANCH, D_MODEL)
        merged = gates[:, :, 0] * y_attn + gates[:, :, 1] * y_rwkv + gates[:, :, 2] * y_ssm
        h = h + (merged @ w_out[l]).astype(h.dtype)
        u = rmsnorm(h, norm_ffn[l])
        h = h + swiglu(u, ffn_w_gate_up[l], ffn_w_down[l]).astype(h.dtype)
    return rmsnorm(h, norm_final)
```

```python
import os
import numpy as np
from contextlib import ExitStack
import concourse.bass as bass
import concourse.mybir as mybir
from concourse.bass_utils import run_bass_kernel_spmd

F32 = mybir.dt.float32
BF16 = mybir.dt.bfloat16
I32 = mybir.dt.int32
AF = mybir.ActivationFunctionType
ALU = mybir.AluOpType
AX = mybir.AxisListType

SES_ENG = set(os.environ.get("SES", "dve").split(","))
import os
OP_LIMIT = int(os.environ.get('OP_LIMIT', '0'))
NDS = 8

S = 4096
D = 1024
NL = 2
N_IN = 5888
FH = 2816
NCH = 46


class Prog:
    ENG = ("pe", "dve", "act", "pool", "sp")

    def __init__(self, nc, stack):
        self.nc = nc
        self.stack = stack
        self.sem = {e: stack.enter_context(nc.semaphore("s_" + e)) for e in self.ENG}
        self.cnt = {e: 0 for e in self.ENG}
        self.dq = ("sp", "act", "pool")
        self.dsem = {q: [stack.enter_context(nc.semaphore("d_%s%d" % (q, i))) for i in range(NDS)] for q in self.dq}
        self.dcnt = {q: [0] * NDS for q in self.dq}
        self.drr = {q: 0 for q in self.dq}
        self.semobj = {}
        for e in self.ENG:
            self.semobj["s_" + e] = self.sem[e]
        for q in self.dq:
            for i in range(NDS):
                self.semobj["d_%s%d" % (q, i)] = self.dsem[q][i]
        self.waited = {e: {} for e in self.ENG}
        self.ops = {e: [] for e in self.ENG}
        self.buf = {}
        self.nalloc = 0
        self.nins = 0

    def sb(self, shape, dt, name=None):
        self.nalloc += 1
        return self.stack.enter_context(self.nc.sbuf_tensor(name or ("t%d" % self.nalloc), list(shape), dt))

    def ps(self, shape, dt, name=None):
        self.nalloc += 1
        return self.stack.enter_context(self.nc.psum_tensor(name or ("p%d" % self.nalloc), list(shape), dt))

    def _deps(self, reads, writes):
        deps = {}

        def add(tok):
            if tok is None:
                return
            s, v = tok
            if deps.get(s, 0) < v:
                deps[s] = v

        for k in reads:
            b = self.buf.get(k)
            if b:
                add(b["w"])
        for k in writes:
            b = self.buf.get(k)
            if b:
                add(b["w"])
                for s, v in b["r"].items():
                    add((s, v))
        return deps

    def _commit(self, tok, reads, writes):
        for k in writes:
            self.buf[k] = {"w": tok, "r": {}}
        for k in reads:
            if k in writes:
                continue
            b = self.buf.setdefault(k, {"w": None, "r": {}})
            if b["r"].get(tok[0], 0) < tok[1]:
                b["r"][tok[0]] = tok[1]

    def _waits(self, e, deps):
        waits = []
        own = "s_" + e
        for s, v in deps.items():
            if s == own and (e not in SES_ENG):
                continue
            if self.waited[e].get(s, 0) >= v:
                continue
            self.waited[e][s] = v
            waits.append((s, v))
        return waits

    def op(self, e, fn, reads=(), writes=()):
        self.total = getattr(self, "total", 0) + 1
        if OP_LIMIT and self.total > OP_LIMIT:
            return None
        deps = self._deps(reads, writes)
        waits = self._waits(e, deps)
        self.cnt[e] += 1
        tok = ("s_" + e, self.cnt[e])
        self.ops[e].append((waits, fn, tok, 1))
        self._commit(tok, reads, writes)
        return tok

    def dma(self, q, out, in_, reads=(), writes=(), **kw):
        self.total = getattr(self, "total", 0) + 1
        if OP_LIMIT and self.total > OP_LIMIT:
            return None
        deps = self._deps(reads, writes)
        i = self.drr[q]
        self.drr[q] = (i + 1) % NDS
        sname = "d_%s%d" % (q, i)
        prev = self.dcnt[q][i]
        if prev > 0 and deps.get(sname, 0) < prev:
            deps[sname] = prev
        waits = self._waits(q, deps)
        self.dcnt[q][i] = prev + 16
        tok = (sname, prev + 16)

        def fn(eng, out=out, in_=in_, kw=kw):
            return eng.dma_start(out=out, in_=in_, **kw)

        self.ops[q].append((waits, fn, tok, 16))
        self._commit(tok, reads, writes)
        return tok

    def wait_all(self, e):
        deps = {}
        for en in self.ENG:
            if self.cnt[en] > 0:
                deps["s_" + en] = self.cnt[en]
        for q in self.dq:
            for i in range(NDS):
                if self.dcnt[q][i] > 0:
                    deps["d_%s%d" % (q, i)] = self.dcnt[q][i]
        own = "s_" + e
        waits = [(s, v) for s, v in deps.items() if s != own and self.waited[e].get(s, 0) < v]
        for s, v in waits:
            self.waited[e][s] = v
        self.ops[e].append((waits, None, None, 0))

    def emit(self):
        for e in self.ENG:
            self.wait_all(e)
        nc = self.nc
        with nc.Block() as block:
            def mk(e):
                def body(eng):
                    for waits, fn, tok, inc in self.ops[e]:
                        for s, v in waits:
                            eng.wait_ge(self.semobj[s], v)
                        if fn is None:
                            continue
                        ins = fn(eng)
                        ins.then_inc(self.semobj[tok[0]], inc)
                        self.nins += 1
                return body
            block.tensor(mk("pe"))
            block.vector(mk("dve"))
            block.scalar(mk("act"))
            block.gpsimd(mk("pool"))
            block.sync(mk("sp"))
        self.ops = {e: [] for e in self.ENG}
        self.buf = {}

    def mm(self, out, lhsT, rhs, start, stop, reads, writes):
        return self.op("pe", lambda e: e.matmul(out, lhsT, rhs, start=start, stop=stop), reads, writes)

    def tr(self, out, in_, ident, reads, writes):
        return self.op("pe", lambda e: e.transpose(out, in_, ident), reads, writes)

    def act(self, out, in_, func, reads, writes, eng="act", **kw):
        return self.op(eng, lambda e: e.activation(out=out, in_=in_, func=func, **kw), reads, writes)

    def tt(self, eng, out, in0, in1, op, reads, writes):
        return self.op(eng, lambda e: e.tensor_tensor(out=out, in0=in0, in1=in1, op=op), reads, writes)

    def ts(self, eng, out, in0, s1, s2, op0, op1, reads, writes):
        if op1 is None:
            return self.op(eng, lambda e: e.tensor_scalar(out=out, in0=in0, scalar1=s1, scalar2=None, op0=op0), reads, writes)
        return self.op(eng, lambda e: e.tensor_scalar(out=out, in0=in0, scalar1=s1, scalar2=s2, op0=op0, op1=op1), reads, writes)

    def stt(self, out, in0, scalar, in1, op0, op1, reads, writes):
        return self.op("dve", lambda e: e.scalar_tensor_tensor(out=out, in0=in0, scalar=scalar, in1=in1, op0=op0, op1=op1), reads, writes)

    def copy(self, eng, out, in_, reads, writes):
        if eng == "act":
            return self.op(eng, lambda e: e.copy(out=out, in_=in_), reads, writes)
        return self.op(eng, lambda e: e.tensor_copy(out=out, in_=in_), reads, writes)

    def memset(self, eng, ap, val, writes):
        return self.op(eng, lambda e: e.memset(ap, val), (), writes)


def pipeline(gens, depth):
    gens = list(gens)
    live = []
    nxt = 0
    while nxt < len(gens) or live:
        if nxt < len(gens) and len(live) < depth:
            live.append(gens[nxt])
            nxt += 1
        for g_ in list(live):
            try:
                next(g_)
            except StopIteration:
                live.remove(g_)


def make_ident(P, dt=BF16):
    ones = P.sb([128, 128], dt)
    ident = P.sb([128, 128], dt)
    P.memset("pool", ones[:], 1.0, ["ident_ones"])
    P.op("pool", lambda e: e.affine_select(out=ident[:], in_=ones[:], pattern=[[-1, 128]], compare_op=ALU.is_equal,
                                           fill=0.0, base=0, channel_multiplier=1), ["ident_ones"], ["ident"])
    return ident


def phase_norm(P, src, gain_ap, uT_d):
    ident = make_ident(P)
    gT = P.sb([128, 8], F32)
    P.dma("sp", gT[:], gain_ap.rearrange("(k p) -> p k", p=128), writes=["gT"], allow_slow_non_contiguous=True)
    eps = P.sb([128, 1], F32)
    P.memset("dve", eps[:], 1e-6, ["eps"])
    hb = [P.sb([128, 4, 1024], F32) for _ in range(2)]
    ub = [P.sb([128, 4, 1024], BF16) for _ in range(2)]
    junk = P.sb([128, 1024], BF16)
    ss = [P.sb([128, 4], F32) for _ in range(2)]
    rs = [P.sb([128, 4], F32) for _ in range(2)]
    pst = P.ps([128, 8, 512], BF16)
    uTt = [P.sb([128, 8, 512], BF16) for _ in range(2)]
    for tt in range(8):
        b = tt % 2
        P.dma("sp", hb[b][:], src[tt * 512:(tt + 1) * 512, :].rearrange("(s p) d -> p s d", p=128), writes=[("h", b)])
        for s in range(4):
            P.act(junk[:], hb[b][:, s, :], AF.Square, [("h", b)], ["junk", ("ss", b, s)], accum_out=ss[b][:, s:s + 1])
        P.act(rs[b][:], ss[b][:], AF.Sqrt, [("ss", b, s) for s in range(4)] + ["eps"], [("rs", b)], scale=1.0 / D, bias=eps[:])
        P.op("dve", lambda e, o=rs[b]: e.reciprocal(out=o[:], in_=o[:]), [], [("rs", b)])
        for s in range(4):
            P.ts("dve", ub[b][:, s, :], hb[b][:, s, :], rs[b][:, s:s + 1], None, ALU.mult, None, [("h", b), ("rs", b)], [("ub", b, s)])
        for s in range(4):
            for k in range(8):
                P.tr(pst[:, k, s * 128:(s + 1) * 128], ub[b][:, s, k * 128:(k + 1) * 128], ident[:], [("ub", b, s), "ident"], [("pst", k // 2)])
        for k in range(8):
            P.act(uTt[b][:, k, :], pst[:, k, :], AF.Copy, ["gT"], [("pst", k // 2), ("uTt", b)], scale=gT[:, k:k + 1])
        P.dma("sp", uT_d[tt], uTt[b][:], reads=[("uTt", b)], writes=[("uT_d", tt)])


def phase_proj(P, uT_d, w_in_l, mix_ap, pT_d, zr_d):
    uT = P.sb([128, 8, S], BF16)
    for t_ in range(8):
        P.dma("sp" if t_ % 2 == 0 else "act", uT[:, :, t_ * 512:(t_ + 1) * 512],
              uT_d[t_], writes=[("uT", t_)])
    mixT = P.sb([128, 11], F32)
    ommT = P.sb([128, 11], F32)
    P.dma("sp", mixT[:], mix_ap.rearrange("(c p) -> p c", p=128), writes=["mixT"], allow_slow_non_contiguous=True)
    P.ts("dve", ommT[:], mixT[:], -1.0, 1.0, ALU.mult, ALU.add, ["mixT"], ["ommT"])
    wb = [P.sb([128, 8, 512], BF16) for _ in range(2)]
    stage = [P.sb([128, S], BF16) for _ in range(2)]
    zf = P.sb([128, S + 1], F32)
    zm = P.sb([128, S], F32)
    P.memset("dve", zf[:, 0:1], 0.0, [("zf", 0)])
    banks = [P.ps([128, 512], F32) for _ in range(6)]
    nb = 0
    ngroups = (N_IN + 511) // 512
    for cg in range(ngroups):
        c0 = cg * 512
        ncol = min(512, N_IN - c0)
        w = wb[cg % 2]
        P.dma("pool", w[:, :, 0:ncol], w_in_l[:, c0:c0 + ncol].rearrange("(k p) c -> p k c", p=128), writes=[("wb", cg % 2)])
        for cc in range(ncol // 128):
            c = cg * 4 + cc
            is_r = 9 <= c < 20
            st = stage[c % 2]
            for tt in range(8):
                bk = nb % 6
                nb += 1
                for k in range(8):
                    P.mm(banks[bk][:], w[:, k, cc * 128:(cc + 1) * 128], uT[:, k, tt * 512:(tt + 1) * 512], k == 0, k == 7,
                         [("wb", cg % 2), ("uT", tt)], [("bank", bk)])
                if is_r:
                    P.copy("act", zf[:, 1 + tt * 512:1 + (tt + 1) * 512], banks[bk][:], [("bank", bk)], [("zf", 1 + tt)])
                elif c < 3:
                    P.act(st[:, tt * 512:(tt + 1) * 512], banks[bk][:], AF.Copy, [("bank", bk)], [("stage", c % 2, tt)], scale=0.125)
                elif c >= 22:
                    P.act(st[:, tt * 512:(tt + 1) * 512], banks[bk][:], AF.Sigmoid, [("bank", bk)], [("stage", c % 2, tt)])
                else:
                    P.copy("act", st[:, tt * 512:(tt + 1) * 512], banks[bk][:], [("bank", bk)], [("stage", c % 2, tt)])
            if is_r:
                j = c - 9
                zfk = [("zf", i) for i in range(9)]
                P.ts("dve", zm[:], zf[:, 1:S + 1], ommT[:, j:j + 1], None, ALU.mult, None, zfk + ["ommT"], ["zm"])
                P.stt(zm[:], zf[:, 0:S], mixT[:, j:j + 1], zm[:], ALU.mult, ALU.add, zfk + ["mixT"], ["zm"])
                P.dma("sp", zr_d[j], zm[:], reads=["zm"], writes=[("zr_d", j)])
            else:
                P.dma("sp", pT_d[c], st[:], reads=[("stage", c % 2, tt) for tt in range(8)], writes=[("pT_d", c)])


GROUPS = ((128, 1), (512, 4), (2048, 16))


def sl(st, n, d):
    return slice(st, st + (n - 1) * d + 1, d)


def phase_attn(P, pT_d, ao_d):
    ident = make_ident(P)
    qT = P.sb([128, 3, S], BF16)
    kT = P.sb([128, 3, S], BF16)
    vT = P.sb([128, 3, S], BF16)
    for g in range(3):
        P.dma("sp", qT[:, g, :], pT_d[g], writes=[("qT", g)])
        P.dma("act", kT[:, g, :], pT_d[3 + g], writes=[("kT", g)])
        P.dma("sp", vT[:, g, :], pT_d[6 + g], writes=[("vT", g)])
    ones = P.sb([128, 256], BF16)
    mask = P.sb([128, 256], BF16)
    P.memset("pool", ones[:], 1.0, ["mones"])
    P.op("pool", lambda e: e.affine_select(out=mask[:, 0:128], in_=ones[:, 0:128], pattern=[[1, 128]], compare_op=ALU.is_ge,
                                           fill=0.0, base=0, channel_multiplier=-1), ["mones"], ["mask0"])
    P.op("pool", lambda e: e.affine_select(out=mask[:, 128:256], in_=ones[:, 128:256], pattern=[[-1, 128]], compare_op=ALU.is_ge,
                                           fill=0.0, base=0, channel_multiplier=1), ["mones"], ["mask1"])
    vdil = [P.sb([128, 32, 2, 65], BF16) for _ in range(3)]
    pvt_ = [P.ps([128, 1024], BF16) for _ in range(2)]
    pvt = [t[:, 0:128] for t in pvt_]
    sc = [P.ps([128, 512], F32) for _ in range(4)]
    ob_ = [P.ps([128, 512], F32) for _ in range(2)]
    ob = [t[:, 0:130].rearrange("p (j e) -> p j e", j=2) for t in ob_]
    ntr = 0
    for g, (win, d) in enumerate(GROUPS):
        P.memset("dve", vdil[g][:], 1.0, [("vdil", g, b) for b in range(32)])
        nblk = 32 // d
        for r in range(d):
            for n in range(nblk):
                b = r * nblk + n
                st = r + 128 * d * n
                pb = ntr % 2
                ntr += 1
                P.tr(pvt[pb], vT[:, g, sl(st, 128, d)], ident[:], [("vT", g), "ident"], [("pvt", pb)])
                P.copy("dve", vdil[g][:, b, :, 0:64], pvt[pb].rearrange("p (j e) -> p j e", j=2), [("pvt", pb)], [("vdil", g, b)])
    NPT = 6
    pt = [[P.sb([128, 256], BF16) for _ in range(NPT)] for _ in range(2)]
    osb = [P.sb([128, 130], F32) for _ in range(2)]

    def unit(g, d, r, n, nblk, ui):
        b = r * nblk + n
        st = r + 128 * d * n
        nq = 256 if n < nblk - 1 else 128
        o = ui % 2
        pi = ui % NPT
        pp = (ui - 1) % NPT
        for j in range(2):
            s_ = (2 * ui + j) % 4
            P.mm(sc[s_][:, 0:nq], kT[64 * j:64 * j + 64, g, sl(st, 128, d)], qT[64 * j:64 * j + 64, g, sl(st, nq, d)],
                 True, True, [("kT", g), ("qT", g)], [("sc", s_)])
        yield
        for j in range(2):
            s_ = (2 * ui + j) % 4
            P.act(pt[j][pi][:, 0:nq], sc[s_][:, 0:nq], AF.Exp, [("sc", s_)], [("pt", j, pi)])
        yield
        for j in range(2):
            P.tt("pool", pt[j][pi][:, 0:nq], pt[j][pi][:, 0:nq], mask[:, 0:nq], ALU.mult, ["mask0", "mask1"], [("pt", j, pi)])
        yield
        for j in range(2):
            P.mm(ob[o][:, j, :], pt[j][pi][:, 0:128], vdil[g][:, b, j, :], True, n == 0,
                 [("pt", j, pi), ("vdil", g, b)], [("ob", o)])
            if n > 0:
                P.mm(ob[o][:, j, :], pt[j][pp][:, 128:256], vdil[g][:, b - 1, j, :], False, True,
                     [("pt", j, pp), ("vdil", g, b - 1)], [("ob", o)])
        yield
        P.copy("dve", osb[o][:], ob_[o][:, 0:130], [("ob", o)], [("osb", o)])
        P.dma("sp", ao_d[g, sl(st, 128, d), :], osb[o][:], reads=[("osb", o)], writes=[("ao_d", g, b)])
        yield

    units = []
    for g, (win, d) in enumerate(GROUPS):
        nblk = 32 // d
        for r in range(d):
            for n in range(nblk):
                units.append(unit(g, d, r, n, nblk, len(units)))
    pipeline(units, 5)


def phase_attn_combine(P, ao_d, brT_d):
    ident = make_ident(P)
    stage = P.sb([128, 3, S], BF16)
    at = [P.sb([128, 3, 130], F32) for _ in range(2)]
    den = [P.sb([128, 2], F32) for _ in range(2)]
    on = [P.sb([128, 3, 2, 64], BF16) for _ in range(2)]
    pst_ = [P.ps([128, 1024], BF16) for _ in range(2)]
    pst = [t[:, 0:384].rearrange("p (g e) -> p g e", g=3) for t in pst_]
    for t in range(32):
        b = t % 2
        P.dma("sp", at[b][:], ao_d[:, t * 128:(t + 1) * 128, :].rearrange("g p e -> p g e"), writes=[("at", b)])
        a4 = at[b][:].rearrange("p g (j e) -> p g j e", j=2)
        P.op("dve", lambda e, o=den[b], i=a4: e.tensor_reduce(out=o[:], in_=i[:, :, :, 64].rearrange("p g j -> p j g"), axis=AX.X, op=ALU.add),
             [("at", b)], [("den", b)])
        P.op("dve", lambda e, o=den[b]: e.reciprocal(out=o[:], in_=o[:]), [], [("den", b)])
        for j in range(2):
            P.ts("dve", on[b][:, :, j, :], a4[:, :, j, 0:64], den[b][:, j:j + 1], None, ALU.mult, None, [("at", b), ("den", b)], [("on", b, j)])
        onf = on[b][:].rearrange("p g j e -> p (g j e)")
        for g in range(3):
            P.tr(pst[b][:, g, :], onf[:, g * 128:(g + 1) * 128], ident[:], [("on", b, 0), ("on", b, 1), "ident"], [("pst", b)])
        P.copy("act", stage[:, :, t * 128:(t + 1) * 128], pst[b], [("pst", b)], [("stage", t)])
    for g in range(3):
        P.dma("sp", brT_d[g], stage[:, g, :], reads=[("stage", t) for t in range(32)], writes=[("brT_d", g)])


TWO_PI = 6.283185307179586
GELU_C = 2.0 * 0.7978845608028654


def phase_s5(P, pT_d, prm, brT_d):
    identf = make_ident(P, F32)
    uS = P.sb([128, 2, S], BF16)
    P.dma("sp", uS[:, 0, :], pT_d[20], writes=[("uS", 0)])
    P.dma("act", uS[:, 1, :], pT_d[21], writes=[("uS", 1)])
    lre = P.sb([128, 8], F32)
    lim = P.sb([128, 8], F32)
    lst = P.sb([128, 8], F32)
    P.dma("sp", lre[:], prm["a_re"].rearrange("(gp gl) n -> (gl n) gp", gl=2), writes=["lre"], allow_slow_non_contiguous=True)
    P.dma("act", lim[:], prm["a_im"].rearrange("(gp gl) n -> (gl n) gp", gl=2), writes=["lim"], allow_slow_non_contiguous=True)
    for gl in range(2):
        P.dma("sp", lst[gl * 64:(gl + 1) * 64, :], prm["log_step"][gl::2].partition_broadcast(64), writes=[("lst", gl)], allow_slow_non_contiguous=True)
    dT = P.sb([128, 2], F32)
    P.dma("sp", dT[:], prm["d"].rearrange("(cs p) -> p cs", p=128), writes=["dT"], allow_slow_non_contiguous=True)
    Bst_re = P.sb([128, 8, 16], F32)
    Bst_im = P.sb([128, 8, 16], F32)
    P.dma("sp", Bst_re[:], prm["b_re"].rearrange("(gp gl) n c -> (gl n) gp c", gl=2), writes=["Bst_re"])
    P.dma("act", Bst_im[:], prm["b_im"].rearrange("(gp gl) n c -> (gl n) gp c", gl=2), writes=["Bst_im"])
    wv = P.sb([128, 2, 256], BF16)
    wg = P.sb([128, 2, 256], BF16)
    P.dma("pool", wv[:], prm["glu_val"].rearrange("(k p) c -> p k c", p=128), writes=["wv"])
    P.dma("pool", wg[:], prm["glu_gate"].rearrange("(k p) c -> p k c", p=128), writes=["wg"])
    Cin = [[P.sb([128, 128], F32) for _ in range(2)] for _ in range(2)]
    for ri, nm in enumerate(("c_re", "c_im")):
        for cs in range(2):
            P.memset("pool", Cin[ri][cs][:], 0.0, [("Cin", ri, cs)])
            for i in range(4):
                for gl in range(2):
                    g = 2 * (4 * cs + i) + gl
                    P.dma("sp" if gl == 0 else "act", Cin[ri][cs][32 * i + 16 * gl:32 * i + 16 * gl + 16, 64 * gl:64 * gl + 64], prm[nm][g],
                          writes=[("Cin", ri, cs)])

    sm = {}

    def small(name):
        sm[name] = P.sb([128, 8], F32)
        return sm[name]

    for nm in ("stp", "lr", "mag", "angt", "phs", "phc", "s1", "c1", "are", "aim", "den", "am1", "fre", "fim", "nfim", "t1", "t2", "nsT"):
        small(nm)
    ti8 = P.sb([128, 8], I32)
    K_ = lambda n: "sm_" + n
    P.act(sm["stp"][:], lst[:], AF.Exp, [("lst", 0), ("lst", 1)], [K_("stp")])
    P.tt("dve", sm["lr"][:], lre[:], sm["stp"][:], ALU.mult, ["lre", K_("stp")], [K_("lr")])
    P.act(sm["mag"][:], sm["lr"][:], AF.Exp, [K_("lr")], [K_("mag")])
    P.tt("dve", sm["angt"][:], lim[:], sm["stp"][:], ALU.mult, ["lim", K_("stp")], [K_("angt")])
    P.ts("dve", sm["angt"][:], sm["angt"][:], 1.0 / TWO_PI, None, ALU.mult, None, [], [K_("angt")])
    for ph, off, outn in (("phs", 4.0, "s1"), ("phc", 4.25, "c1")):
        P.ts("dve", sm[ph][:], sm["angt"][:], off, None, ALU.add, None, [K_("angt")], [K_(ph)])
        P.copy("dve", ti8[:], sm[ph][:], [K_(ph)], ["ti8"])
        P.tt("dve", sm[ph][:], sm[ph][:], ti8[:], ALU.subtract, ["ti8"], [K_(ph)])
        P.act(sm[outn][:], sm[ph][:], AF.Sin, [K_(ph)], [K_(outn)], scale=TWO_PI)
    P.tt("dve", sm["are"][:], sm["mag"][:], sm["c1"][:], ALU.mult, [K_("mag"), K_("c1")], [K_("are")])
    P.tt("dve", sm["aim"][:], sm["mag"][:], sm["s1"][:], ALU.mult, [K_("mag"), K_("s1")], [K_("aim")])
    P.tt("dve", sm["den"][:], lre[:], lre[:], ALU.mult, ["lre"], [K_("den")])
    P.tt("dve", sm["t1"][:], lim[:], lim[:], ALU.mult, ["lim"], [K_("t1")])
    P.tt("dve", sm["den"][:], sm["den"][:], sm["t1"][:], ALU.add, [K_("t1")], [K_("den")])
    P.op("dve", lambda e: e.reciprocal(out=sm["den"][:], in_=sm["den"][:]), [], [K_("den")])
    P.ts("dve", sm["am1"][:], sm["are"][:], -1.0, None, ALU.add, None, [K_("are")], [K_("am1")])
    P.tt("dve", sm["t1"][:], sm["am1"][:], lre[:], ALU.mult, [K_("am1"), "lre"], [K_("t1")])
    P.tt("dve", sm["t2"][:], sm["aim"][:], lim[:], ALU.mult, [K_("aim"), "lim"], [K_("t2")])
    P.tt("dve", sm["t1"][:], sm["t1"][:], sm["t2"][:], ALU.add, [K_("t2")], [K_("t1")])
    P.tt("dve", sm["fre"][:], sm["t1"][:], sm["den"][:], ALU.mult, [K_("t1"), K_("den")], [K_("fre")])
    P.tt("dve", sm["t1"][:], sm["aim"][:], lre[:], ALU.mult, [K_("aim"), "lre"], [K_("t1")])
    P.tt("dve", sm["t2"][:], sm["am1"][:], lim[:], ALU.mult, [K_("am1"), "lim"], [K_("t2")])
    P.tt("dve", sm["t1"][:], sm["t1"][:], sm["t2"][:], ALU.subtract, [K_("t2")], [K_("t1")])
    P.tt("dve", sm["fim"][:], sm["t1"][:], sm["den"][:], ALU.mult, [K_("t1"), K_("den")], [K_("fim")])
    P.ts("dve", sm["nfim"][:], sm["fim"][:], -1.0, None, ALU.mult, None, [K_("fim")], [K_("nfim")])

    Bblk = [P.sb([128, 8, 128], F32) for _ in range(2)]
    P.memset("pool", Bblk[0][:], 0.0, [("Bblk", 0)])
    P.memset("pool", Bblk[1][:], 0.0, [("Bblk", 1)])
    tb = P.sb([128, 16], F32)
    for gp in range(8):
        i = gp % 4
        for gl in range(2):
            rows = slice(gl * 64, gl * 64 + 64)
            c0 = 32 * i + 16 * gl
            fr = sm["fre"][rows, gp:gp + 1]
            fi = sm["fim"][rows, gp:gp + 1]
            nfi = sm["nfim"][rows, gp:gp + 1]
            P.ts("dve", tb[rows, :], Bst_re[rows, gp, :], fr, None, ALU.mult, None, ["Bst_re", K_("fre")], [("tb", gl)])
            P.stt(Bblk[0][rows, gp, c0:c0 + 16], Bst_im[rows, gp, :], nfi, tb[rows, :], ALU.mult, ALU.add, ["Bst_im", K_("nfim"), ("tb", gl)], [("Bblk", 0)])
            P.ts("dve", tb[rows, :], Bst_im[rows, gp, :], fr, None, ALU.mult, None, ["Bst_im", K_("fre")], [("tb", gl)])
            P.stt(Bblk[1][rows, gp, c0:c0 + 16], Bst_re[rows, gp, :], fi, tb[rows, :], ALU.mult, ALU.add, ["Bst_re", K_("fim"), ("tb", gl)], [("Bblk", 1)])
    LB = [P.sb([128, 8, 128], BF16) for _ in range(2)]
    LC = [P.sb([128, 8, 128], BF16) for _ in range(3)]
    P.memset("pool", LC[0][:], 0.0, [("LC", 0)])
    P.memset("pool", LC[1][:], 0.0, [("LC", 1)])
    P.memset("pool", LC[2][:], 0.0, [("LC", 2)])
    banks = [P.ps([128, 512], F32) for _ in range(8)]
    nb = 0
    for ri in range(2):
        for gp in range(8):
            bk = nb % 8
            nb += 1
            P.tr(banks[bk][:, 0:128], Bblk[ri][:, gp, :], identf[:], [("Bblk", ri), "ident"], [("bank", bk)])
            P.copy("act", LB[ri][:, gp, :], banks[bk][:, 0:128], [("bank", bk)], [("LB", ri)])
        for cs in range(2):
            bk = nb % 8
            nb += 1
            P.tr(banks[bk][:, 0:128], Cin[ri][cs][:], identf[:], [("Cin", ri, cs), "ident"], [("bank", bk)])
            for i in range(4):
                gp = 4 * cs + i
                if ri == 0:
                    P.copy("act", LC[0][:, gp, 32 * i:32 * i + 32], banks[bk][:, 32 * i:32 * i + 32], [("bank", bk)], [("LC", 0)])
                    P.act(LC[2][:, gp, 32 * i:32 * i + 32], banks[bk][:, 32 * i:32 * i + 32], AF.Copy, [("bank", bk)], [("LC", 2)], scale=-1.0)
                else:
                    P.act(LC[ri][:, gp, 32 * i:32 * i + 32], banks[bk][:, 32 * i:32 * i + 32], AF.Copy, [("bank", bk)], [("LC", ri)], scale=-1.0)

    taui = P.sb([128, 513], I32)
    tauf = P.sb([128, 513], F32)
    P.op("pool", lambda e: e.iota(out=taui[:], pattern=[[1, 513]], base=0, channel_multiplier=0), [], ["taui"])
    P.copy("dve", tauf[:], taui[:], ["taui"], ["tauf"])
    ones5 = P.sb([128, 512], F32)
    P.memset("pool", ones5[:], 1.0, ["ones5"])
    St = P.sb([128, 8, 512], BF16)
    Ct = P.sb([128, 8, 512], BF16)
    SL = P.sb([128, 8], F32)
    CL = P.sb([128, 8], F32)
    magT = P.sb([128, 8, 512], F32)
    ph = P.sb([128, 513], F32)
    ti = P.sb([128, 513], I32)
    for gp in range(8):
        for off, tab, last, nm in ((4.0, St, SL, "St"), (4.25, Ct, CL, "Ct")):
            P.ts("dve", ph[:], tauf[:], sm["angt"][:, gp:gp + 1], off, ALU.mult, ALU.add, ["tauf", K_("angt")], ["ph"])
            P.copy("dve", ti[:], ph[:], ["ph"], ["ti"])
            P.tt("dve", ph[:], ph[:], ti[:], ALU.subtract, ["ti"], ["ph"])
            P.act(tab[:, gp, :], ph[:, 0:512], AF.Sin, ["ph"], [(nm, gp)], scale=TWO_PI)
            P.act(last[:, gp:gp + 1], ph[:, 512:513], AF.Sin, ["ph"], [(nm + "L", gp)], scale=TWO_PI)
        P.ts("dve", magT[:, gp, :], ones5[:], sm["mag"][:, gp:gp + 1], None, ALU.mult, None, ["ones5", K_("mag")], [("magT", gp)])
        P.ts("dve", sm["nsT"][:, gp:gp + 1], SL[:, gp:gp + 1], -1.0, None, ALU.mult, None, [("StL", gp)], [("nsT", gp)])

    carry = [P.sb([128, 8], F32) for _ in range(2)]
    P.memset("dve", carry[0][:], 0.0, [("carry", 0, gp) for gp in range(8)])
    P.memset("dve", carry[1][:], 0.0, [("carry", 1, gp) for gp in range(8)])
    ND = 4
    bsb = [[P.sb([128, 512], BF16) for _ in range(2)] for _ in range(ND)]
    ptr_ = [[P.sb([128, 512], BF16) for _ in range(4)] for _ in range(2)]
    zin = [[P.sb([128, 512], BF16) for _ in range(2)] for _ in range(ND)]
    zz = [[P.sb([128, 512], F32) for _ in range(2)] for _ in range(ND)]
    zb = [[P.sb([128, 512], BF16) for _ in range(2)] for _ in range(ND)]
    dtr_ = [[P.sb([128, 512], BF16) for _ in range(4)] for _ in range(2)]
    yf = P.sb([128, 512], F32)
    g2 = P.sb([128, 512], F32)
    zg = [P.sb([128, 512], BF16) for _ in range(2)]
    sg = P.sb([128, 512], F32)
    stage = P.sb([128, 2, S], BF16)
    ctmp = P.sb([128, 2], F32)

    def unit(c, cs, i, ui):
        tsl = slice(c * 512, (c + 1) * 512)
        Y = banks[4 + cs]
        gp = 4 * cs + i
        rb = ui % ND
        pb = ui % 2
        Cg = Ct[:, gp, :]
        Sg = St[:, gp, :]
        kC, kS = ("Ct", gp), ("St", gp)
        for ri in range(2):
            P.mm(banks[2 * pb + ri][:], LB[ri][:, gp, :], uS[:, cs, tsl], True, True, [("LB", ri), ("uS", cs)], [("bank", 2 * pb + ri)])
            P.copy("act", bsb[rb][ri][:], banks[2 * pb + ri][:], [("bank", 2 * pb + ri)], [("bsb", rb, ri)])
        yield
        bre, bim = bsb[rb]
        pt = ptr_[ui % 2]
        P.tt("dve", pt[0][:], bre[:], Cg, ALU.mult, [("bsb", rb, 0), kC], [("pt", ui % 2, 0)])
        P.tt("dve", pt[1][:], bim[:], Sg, ALU.mult, [("bsb", rb, 1), kS], [("pt", ui % 2, 1)])
        P.tt("dve", pt[2][:], bim[:], Cg, ALU.mult, [("bsb", rb, 1), kC], [("pt", ui % 2, 2)])
        P.tt("dve", pt[3][:], bre[:], Sg, ALU.mult, [("bsb", rb, 0), kS], [("pt", ui % 2, 3)])
        P.tt("pool", zin[rb][0][:], pt[0][:], pt[1][:], ALU.add, [("pt", ui % 2, 0), ("pt", ui % 2, 1)], [("zin", rb, 0)])
        P.tt("pool", zin[rb][1][:], pt[2][:], pt[3][:], ALU.subtract, [("pt", ui % 2, 2), ("pt", ui % 2, 3)], [("zin", rb, 1)])
        yield
        for ri in range(2):
            P.op("dve", lambda e, o=zz[rb][ri], d1=zin[rb][ri], ini=carry[ri][:, gp:gp + 1], d0=magT[:, gp, :]:
                 e.tensor_tensor_scan(out=o[:], data0=d0, data1=d1[:], initial=ini, op0=ALU.mult, op1=ALU.add),
                 [("zin", rb, ri), ("magT", gp), ("carry", ri, gp)], [("zz", rb, ri)])
            P.copy("act", zb[rb][ri][:], zz[rb][ri][:], [("zz", rb, ri)], [("zb", rb, ri)])
        zr_, zi_ = zz[rb]
        cT = CL[:, gp:gp + 1]
        sT = SL[:, gp:gp + 1]
        kCL, kSL = ("CtL", gp), ("StL", gp)
        nsT = sm["nsT"][:, gp:gp + 1]
        P.ts("dve", ctmp[:, 0:1], zr_[:, 511:512], cT, None, ALU.mult, None, [("zz", rb, 0), kCL], [("ctmp", 0)])
        P.stt(carry[0][:, gp:gp + 1], zi_[:, 511:512], nsT, ctmp[:, 0:1], ALU.mult, ALU.add, [("zz", rb, 1), ("nsT", gp), ("ctmp", 0)], [("carry", 0, gp)])
        P.ts("dve", ctmp[:, 1:2], zr_[:, 511:512], sT, None, ALU.mult, None, [("zz", rb, 0), kSL], [("ctmp", 1)])
        P.stt(carry[1][:, gp:gp + 1], zi_[:, 511:512], cT, ctmp[:, 1:2], ALU.mult, ALU.add, [("zz", rb, 1), kCL, ("ctmp", 1)], [("carry", 1, gp)])
        yield
        zbr, zbi = zb[rb]
        dt = dtr_[ui % 2]
        dk = [("dt", ui % 2, q) for q in range(4)]
        P.tt("dve", dt[0][:], zbr[:], Cg, ALU.mult, [("zb", rb, 0), kC], [dk[0]])
        P.tt("dve", dt[1][:], zbi[:], Sg, ALU.mult, [("zb", rb, 1), kS], [dk[1]])
        P.tt("dve", dt[2][:], zbr[:], Sg, ALU.mult, [("zb", rb, 0), kS], [dk[2]])
        P.tt("dve", dt[3][:], zbi[:], Cg, ALU.mult, [("zb", rb, 1), kC], [dk[3]])
        for q, wsel in enumerate((0, 2, 1, 1)):
            P.mm(Y[:], LC[wsel][:, gp, :], dt[q][:], i == 0 and q == 0, i == 3 and q == 3, [("LC", wsel), dk[q]], [("bank", 4 + cs)])
        if i == 3:
            P.stt(yf[:], uS[:, cs, tsl], dT[:, cs:cs + 1], Y[:], ALU.mult, ALU.add, [("uS", cs), "dT", ("bank", 4 + cs)], ["yf"])
            P.tt("dve", g2[:], yf[:], yf[:], ALU.mult, ["yf"], ["g2"])
            P.ts("dve", g2[:], g2[:], 0.044715, 1.0, ALU.mult, ALU.add, [], ["g2"])
            P.tt("dve", g2[:], g2[:], yf[:], ALU.mult, ["yf"], ["g2"])
            P.act(g2[:], g2[:], AF.Sigmoid, [], ["g2"], scale=GELU_C)
            P.tt("dve", zg[cs][:], g2[:], yf[:], ALU.mult, ["g2", "yf"], [("zg", cs)])
            if cs == 1:
                for oc in range(2):
                    for k in range(2):
                        P.mm(banks[6][:], wv[:, k, oc * 128:(oc + 1) * 128], zg[k][:], k == 0, k == 1, ["wv", ("zg", k)], [("bank", 6)])
                    for k in range(2):
                        P.mm(banks[7][:], wg[:, k, oc * 128:(oc + 1) * 128], zg[k][:], k == 0, k == 1, ["wg", ("zg", k)], [("bank", 7)])
                    P.act(sg[:], banks[7][:], AF.Sigmoid, [("bank", 7)], ["sg"])
                    P.tt("dve", stage[:, oc, tsl], banks[6][:], sg[:], ALU.mult, [("bank", 6), "sg"], [("stage", oc, c)])
        yield

    units = []
    for c in range(8):
        for cs in range(2):
            for i in range(4):
                units.append(unit(c, cs, i, len(units)))
    pipeline(units, ND)
    for oc in range(2):
        P.dma("sp", brT_d[6 + oc], stage[:, oc, :], reads=[("stage", oc, c) for c in range(8)], writes=[("brT_d", 6 + oc)])


C0 = 0.6065306597126334
GN_EPS = 64e-5
NCK = 64
RB = 8
GL = 4
RWKV_DEBUG = int(os.environ.get('RWKV_DEBUG', '0'))


def phase_rwkv(P, zr_d, prm, brT_d):
    identb = make_ident(P, BF16)

    def colparam(name, ap):
        t = P.sb([128, 3], F32)
        P.dma("sp", t[:], ap.rearrange("(c p) -> p c", p=128), writes=[name], allow_slow_non_contiguous=True)
        return t
    w0T = colparam("w0T", prm["w0"])
    a0T = colparam("a0T", prm["a0"])
    kkT = colparam("kkT", prm["k_k"])
    kaT = colparam("kaT", prm["k_a"])
    lnwT = colparam("lnwT", prm["ln_w"])
    lnbT = colparam("lnbT", prm["ln_b"])
    rkT = colparam("rkT", prm["r_k"].rearrange("h j -> (h j)"))
    omka = P.sb([128, 3], F32)
    P.ts("dve", omka[:], kaT[:], -1.0, 1.0, ALU.mult, ALU.add, ["kaT"], ["omka"])
    w2b = P.sb([128, 384], BF16)
    a2b = P.sb([128, 384], BF16)
    g2b = P.sb([128, 384], BF16)
    P.memset("pool", w2b[:], 0.0, ["w2b"])
    P.memset("pool", a2b[:], 0.0, ["a2b"])
    P.dma("pool", w2b[0:64, :], prm["w2"], writes=["w2b"])
    P.dma("pool", a2b[64:128, :], prm["a2"], writes=["a2b"])
    P.dma("pool", g2b[:], prm["g2"], writes=["g2b"])
    blk1 = P.sb([128, 128], BF16)
    P.memset("pool", blk1[:], 0.0, ["blk1"])
    P.memset("pool", blk1[0:64, 0:64], 1.0, ["blk1"])
    P.memset("pool", blk1[64:128, 64:128], 1.0, ["blk1"])
    rst = P.sb([128, 512], F32)
    P.memset("pool", rst[:], 1.0, ["rst"])
    P.memset("pool", rst[:, 0:512:64], 0.0, ["rst"])
    onesb = P.sb([128, 128], BF16)
    P.memset("pool", onesb[:], 1.0, ["onesb"])
    mask2 = P.sb([128, 256], BF16)
    maskL = P.sb([128, 128], BF16)
    P.memset("pool", mask2[:], 0.0, ["mask2"])
    P.memset("pool", maskL[:], 0.0, ["maskL"])
    for half in range(2):
        rows = slice(64 * half, 64 * half + 64)
        for qb in range(2):
            cols = slice(128 * qb + 64 * half, 128 * qb + 64 * half + 64)
            P.op("pool", lambda e, rows=rows, cols=cols, qb=qb: e.affine_select(
                out=mask2[rows, cols], in_=onesb[rows, 0:64], pattern=[[1, 64]],
                compare_op=(ALU.is_gt if qb == 0 else ALU.is_ge), fill=0.0, base=0, channel_multiplier=-1), ["onesb"], ["mask2"])
        P.op("pool", lambda e, rows=rows: e.affine_select(
            out=maskL[rows, rows], in_=onesb[rows, 0:64], pattern=[[-1, 64]],
            compare_op=ALU.is_gt, fill=0.0, base=0, channel_multiplier=1), ["onesb"], ["maskL"])
    eps = P.sb([128, 1], F32)
    P.memset("pool", eps[:], GN_EPS, ["eps"])

    tx = P.sb([128, S], BF16)
    sgx = P.sb([128, S], BF16)
    zl = [P.sb([128, 512], F32) for _ in range(2)]
    for q in range(8):
        b = q % 2
        qs = slice(q * 512, (q + 1) * 512)
        P.dma("sp", zl[b][:], zr_d[9][:, qs], writes=[("zl", b)])
        P.act(tx[0:64, qs], zl[b][0:64, :], AF.Tanh, [("zl", b)], [("tx", q)])
        P.copy("dve", tx[64:128, qs], zl[b][64:128, :], [("zl", b)], [("tx", q)])
    for q in range(8):
        b = q % 2
        qs = slice(q * 512, (q + 1) * 512)
        P.dma("sp", zl[b][:], zr_d[10][:, qs], writes=[("zl", b)])
        P.act(sgx[:, qs], zl[b][:], AF.Sigmoid, [("zl", b)], [("sgx", q)])

    ARbd = P.sb([128, NCK, 2, 128], BF16)
    BKbd = P.sb([128, NCK, 2, 128], BF16)
    vbd = P.sb([128, NCK, 128], BF16)
    for q in range(4):
        cq = slice(q * 16, (q + 1) * 16)
        P.memset("pool", ARbd[:, cq], 0.0, [("ARz", q)])
        P.memset("pool", BKbd[:, cq], 0.0, [("BKz", q)])
    P.memset("pool", vbd[:], 0.0, ["vbz"])
    PCt = P.sb([128, NCK], F32)
    bonv = P.sb([128, S], BF16)
    gT = P.sb([128, S], BF16)
    yT = P.sb([128, S], BF16)
    TW = 256
    zin_s = [[P.sb([128, TW], F32) for _ in range(3)] for _ in range(2)]
    T1s = [{n: P.sb([128, TW], F32) for n in ("sig", "eta", "cum", "ex", "eP", "eN", "eX", "kkk", "rn", "kk", "fac", "kp", "b", "rk")} for _ in range(2)]
    T1bs = [{n: P.sb([128, TW], BF16) for n in ("sq", "rkr")} for _ in range(2)]
    Ab = [P.sb([128, 256], BF16) for _ in range(RB)]
    Ak = [P.sb([128, 256], BF16) for _ in range(RB)]
    Tt2 = [P.sb([128, 128], BF16) for _ in range(RB)]
    Btok = [P.sb([128, 128], BF16) for _ in range(RB)]
    Ktok = [P.sb([128, 128], BF16) for _ in range(RB)]
    Vw = [P.sb([128, 128], BF16) for _ in range(RB)]
    Uw = P.sb([128, 128], BF16)
    Mk = [[P.sb([128, 128], BF16) for _ in range(2)] for _ in range(GL)]
    Lk = [[P.sb([128, 128], BF16) for _ in range(2)] for _ in range(GL)]
    X2b = P.sb([128, 128], BF16)
    ST32 = P.sb([128, 128], F32)
    STb = P.sb([128, 128], BF16)
    tmpS = P.sb([128, 128], F32)
    ynw = [P.sb([128, 128], BF16) for _ in range(2)]
    gsum = [P.sb([128, 8], F32) for _ in range(2)]
    junk = P.sb([128, 128], BF16)
    Xps = P.ps([128, 512], F32)
    Ups = P.ps([128, 512], F32)
    Yps = P.ps([128, 512], F32)
    Dps = P.ps([128, 512], F32)
    pre = [P.ps([128, 512], F32) for _ in range(GL)]

    for hp in range(3):
        NT1 = S // TW
        CPT = TW // 64

        def step1(tt):
            sb = tt % 2
            T1 = T1s[sb]
            T1b = T1bs[sb]
            zb = zin_s[sb]
            K1 = lambda n: (n, sb)
            tsl = slice(tt * TW, (tt + 1) * TW)
            t8 = (tt * TW) // 512
            for qi, ch in enumerate((hp, 3 + hp, 6 + hp)):
                P.dma("sp", zb[qi][:], zr_d[ch][:, tsl], writes=[("zin", sb, qi)])
            kr, kk_, kv = [("zin", sb, qi) for qi in range(3)]
            r_, k_, v_ = zb
            ck = slice(tt * CPT, (tt + 1) * CPT)
            zq = [("ARz", q) for q in range(4)] + [("BKz", q) for q in range(4)] + ["vbz"]

            def v3h(t, h):
                return t[64 * h:64 * h + 64, :].rearrange("p (c e) -> p c e", e=64)

            def bd(T_, slot, h):
                return T_[64 * h:64 * h + 64, ck, slot, 64 * h:64 * h + 64]
            P.mm(Xps[:, 0:TW], w2b[:, hp * 128:(hp + 1) * 128], tx[:, tsl], True, True, ["w2b", ("tx", t8)], ["Xps"])
            P.mm(Ups[:, 0:TW], a2b[:, hp * 128:(hp + 1) * 128], tx[:, tsl], True, True, ["a2b", ("tx", t8)], ["Ups"])
            P.mm(Yps[:, 0:TW], g2b[:, hp * 128:(hp + 1) * 128], sgx[:, tsl], True, True, ["g2b", ("sgx", t8)], ["Yps"])
            P.act(T1["sig"][:], Xps[:, 0:TW], AF.Sigmoid, ["Xps", "w0T"], [K1("sig")], bias=w0T[:, hp:hp + 1])
            P.act(T1["eta"][:], Ups[:, 0:TW], AF.Sigmoid, ["Ups", "a0T"], [K1("eta")], bias=a0T[:, hp:hp + 1])
            P.copy("act", gT[:, tsl], Yps[:, 0:TW], ["Yps"], [("gT", tt)])
            P.act(T1["kkk"][:], k_[:], AF.Copy, [kk_, "kkT"], [K1("kkk")], scale=kkT[:, hp:hp + 1])
            P.act(T1b["sq"][:], k_[:], AF.Square, [kk_, "kkT"], [K1("sq")], scale=kkT[:, hp:hp + 1])
            yield
            P.op("dve", lambda e: e.tensor_tensor_scan(out=T1["cum"][:], data0=rst[:, 0:TW], data1=T1["sig"][:], initial=0.0, op0=ALU.mult, op1=ALU.add),
                 ["rst", K1("sig")], [K1("cum")])
            P.mm(Dps[:, 0:TW], blk1[:], T1b["sq"][:], True, True, ["blk1", K1("sq")], ["Dps"])
            P.ts("dve", T1["fac"][:], T1["eta"][:], kaT[:, hp:hp + 1], omka[:, hp:hp + 1], ALU.mult, ALU.add, [K1("eta"), "kaT", "omka"], [K1("fac")])
            P.tt("dve", T1["ex"][:], T1["cum"][:], T1["sig"][:], ALU.subtract, [K1("cum"), K1("sig")], [K1("ex")])
            yield
            P.act(T1["rn"][:], Dps[:, 0:TW], AF.Sqrt, ["Dps"], [K1("rn")])
            P.act(T1["eP"][:], T1["cum"][:], AF.Exp, [K1("cum")], [K1("eP")], scale=-C0)
            P.act(T1["eN"][:], T1["cum"][:], AF.Exp, [K1("cum")], [K1("eN")], scale=C0)
            P.act(T1["eX"][:], T1["ex"][:], AF.Exp, [K1("ex")], [K1("eX")], scale=-C0)
            P.tt("pool", T1["kp"][:], k_[:], T1["fac"][:], ALU.mult, [kk_, K1("fac")], [K1("kp")])
            yield
            P.ts("dve", T1["rn"][:], T1["rn"][:], 1e-12, None, ALU.max, None, [], [K1("rn")])
            P.op("dve", lambda e: e.reciprocal(out=T1["rn"][:], in_=T1["rn"][:]), [], [K1("rn")])
            P.copy("dve", PCt[:, tt * CPT:(tt + 1) * CPT], T1["eP"][:, 63:TW:64], [K1("eP")], [("PCt", tt)])
            P.tt("dve", T1["kk"][:], T1["kkk"][:], T1["rn"][:], ALU.mult, [K1("kkk"), K1("rn")], [K1("kk")])
            P.tt("dve", T1["b"][:], T1["kk"][:], T1["eta"][:], ALU.mult, [K1("kk"), K1("eta")], [K1("b")])
            for h in range(2):
                P.tt("pool", bd(ARbd, 1, h), v3h(r_, h), v3h(T1["eP"], h), ALU.mult, [kr, K1("eP")] + zq, [("AR", tt, 1, h)])
            P.tt("pool", T1["rk"][:], r_[:], T1["kp"][:], ALU.mult, [kr, K1("kp")], [K1("rk")])
            yield
            for h in range(2):
                P.tt("pool", bd(BKbd, 1, h), v3h(T1["kp"], h), v3h(T1["eN"], h), ALU.mult, [K1("kp"), K1("eN")] + zq, [("BK", tt, 1, h)])
                P.tt("dve", bd(BKbd, 0, h), v3h(T1["b"], h), v3h(T1["eN"], h), ALU.mult, [K1("b"), K1("eN")] + zq, [("BK", tt, 0, h)])
                P.stt(bd(ARbd, 0, h), v3h(T1["kk"], h), -1.0, v3h(T1["eX"], h), ALU.mult, ALU.mult, [K1("kk"), K1("eX")] + zq, [("AR", tt, 0, h)])
                P.copy("act", vbd[64 * h:64 * h + 64, ck, 64 * h:64 * h + 64], v3h(v_, h), [kv] + zq, [("vbd", tt, h)])
            P.act(T1b["rkr"][:], T1["rk"][:], AF.Copy, [K1("rk"), "rkT"], [K1("rkr")], scale=rkT[:, hp:hp + 1])
            yield
            P.mm(Dps[:, 0:TW], blk1[:], T1b["rkr"][:], True, True, ["blk1", K1("rkr")], ["Dps"])
            P.tt("dve", bonv[:, tsl], Dps[:, 0:TW], v_[:], ALU.mult, ["Dps", kv], [("bonv", tt)])
            yield

        pipeline([step1(tt) for tt in range(NT1)], 2)
        NT8 = NT1
        ARk = [("AR", tt, q, h) for tt in range(NT1) for q in range(2) for h in range(2)]
        BKk = [("BK", tt, q, h) for tt in range(NT1) for q in range(2) for h in range(2)]
        vbk = [("vbd", tt, h) for tt in range(NT1) for h in range(2)]
        PCk = [("PCt", tt) for tt in range(NT1)]
        P.memset("dve", ST32[:], 0.0, ["ST32"])
        P.memset("dve", STb[:], 0.0, ["STb"])
        if RWKV_DEBUG == 1:
            break

        def precompute(n, sl_):
            rb = n % RB
            Bk = pre[sl_]
            kA = ("pre", sl_)
            R0 = Bk[:, 0:256]
            R1 = Bk[:, 256:512]
            ARn = ARbd[:, n, :, :].rearrange("p a e -> p (a e)")
            Abd = ARbd[:, n, 0, :]
            Bbd = BKbd[:, n, 0, :]
            Kbd = BKbd[:, n, 1, :]
            P.mm(R0, Bbd, ARn, True, True, ARk + BKk, [kA])
            P.mm(R1, Kbd, ARn, True, True, ARk + BKk, [kA])
            yield
            P.tt("dve", Ab[rb][:], R0, mask2[:], ALU.mult, ["mask2"], [kA, ("Ab", rb)])
            P.tt("dve", Ak[rb][:], R1, mask2[:], ALU.mult, ["mask2"], [kA, ("Ak", rb)])
            P.mm(R0[:, 0:128], Abd, Bbd, True, True, ARk + BKk, [kA])
            Bv = Bk.bitcast(BF16)
            P.tr(Bv[:, 512:640], Bbd, identb[:], BKk + ["ident"], [kA])
            P.tr(Bv[:, 640:768], Kbd, identb[:], BKk + ["ident"], [kA])
            P.tr(Bv[:, 768:896], vbd[:, n, :], identb[:], vbk + ["ident"], [kA])
            yield
            P.tt("dve", Lk[sl_][0][:], R0[:, 0:128], maskL[:], ALU.mult, ["maskL"], [kA, ("Lk", sl_, 0)])
            P.tt("pool", Tt2[rb][:], Ab[rb][:, 0:128], identb[:], ALU.add, [("Ab", rb), "ident"], [("Tt2", rb)])
            P.copy("act", Btok[rb][:], Bv[:, 512:640], [], [kA, ("Btok", rb)])
            P.copy("act", Ktok[rb][:], Bv[:, 640:768], [], [kA, ("Ktok", rb)])
            P.copy("act", Vw[rb][:], Bv[:, 768:896], [], [kA, ("Vw", rb)])
            yield
            pp = 0
            for lev in range(1, 6):
                if lev == 1:
                    Mo, kMo = Ab[rb][:, 0:128], ("Ab", rb)
                else:
                    Mo, kMo = Mk[sl_][pp][:], ("Mk", sl_, pp)
                Lo = Lk[sl_][pp]
                Mn, Ln = Mk[sl_][1 - pp], Lk[sl_][1 - pp]
                if lev < 5:
                    P.mm(R0[:, 0:128], Lo[:], Mo, True, True, [("Lk", sl_, pp), kMo], [kA])
                P.mm(R1[:, 0:128], Mo, Lo[:], True, True, [("Lk", sl_, pp), kMo], [kA])
                yield
                if lev < 5:
                    P.copy("act", Mn[:], R0[:, 0:128], [], [kA, ("Mk", sl_, 1 - pp)])
                P.copy("act", Ln[:], R1[:, 0:128], [], [kA, ("Lk", sl_, 1 - pp)])
                P.mm(R0[:, 128:256], Ln[:], Tt2[rb][:], True, True, [("Lk", sl_, 1 - pp), ("Tt2", rb)], [kA])
                yield
                P.tt("dve", Tt2[rb][:], R0[:, 128:256], Tt2[rb][:], ALU.add, [], [kA, ("Tt2", rb)])
                pp = 1 - pp
            yield

        def chain(n):
            rb = n % RB
            t0 = n * 64
            X = Xps[:, 0:128]
            U = Ups[:, 0:128]
            Y = Yps[:, 0:128]
            Dd = Dps[:, 0:128]
            Abd = ARbd[:, n, 0, :]
            Rbd = ARbd[:, n, 1, :]
            P.mm(X, Abd, STb[:], True, False, ARk + ["STb"], ["Xps"])
            P.mm(X, Ak[rb][:, 0:128], Vw[rb][:], False, True, [("Ak", rb), ("Vw", rb)], ["Xps"])
            yield
            P.copy("dve", X2b[:], X, ["Xps"], ["X2b"])
            P.mm(U, Tt2[rb][:], X2b[:], True, True, [("Tt2", rb), "X2b"], ["Ups"])
            yield
            P.copy("dve", Uw[:], U, ["Ups"], ["Uw"])
            P.mm(Dd, Btok[rb][:], Uw[:], True, False, [("Btok", rb), "Uw"], ["Dps"])
            P.mm(Dd, Ktok[rb][:], Vw[rb][:], False, True, [("Ktok", rb), ("Vw", rb)], ["Dps"])
            P.mm(Y, Rbd, STb[:], True, False, ARk + ["STb"], ["Yps"])
            P.mm(Y, Ab[rb][:, 128:256], Uw[:], False, False, [("Ab", rb), "Uw"], ["Yps"])
            P.mm(Y, Ak[rb][:, 128:256], Vw[rb][:], False, True, [("Ak", rb), ("Vw", rb)], ["Yps"])
            yield
            P.tt("dve", tmpS[:], Dd, ST32[:], ALU.add, ["Dps", "ST32"], ["tmpS"])
            P.ts("dve", STb[:], tmpS[:], PCt[:, n:n + 1], None, ALU.mult, None, ["tmpS"] + PCk, ["STb"])
            P.act(ST32[:], tmpS[:], AF.Copy, ["tmpS"] + PCk, ["ST32"], scale=PCt[:, n:n + 1])
            yb = n % 2
            gs = gsum[yb]
            gk = ("gs", yb)
            P.op("dve", lambda e: e.tensor_reduce(out=gs[:, 0:1], in_=Y, axis=AX.X, op=ALU.add), [], ["Yps", gk])
            P.ts("dve", gs[:, 1:2], gs[:, 0:1], -1.0 / 64, None, ALU.mult, None, [], [gk])
            P.act(junk[:], Y, AF.Square, [], ["Yps", "junk", ("gq", yb)], accum_out=gs[:, 2:3])
            yield
            P.tt("dve", gs[:, 3:4], gs[:, 1:2], gs[:, 1:2], ALU.mult, [], [gk])
            P.ts("dve", gs[:, 4:5], gs[:, 3:4], -1.0, GN_EPS, ALU.mult, ALU.add, [], [gk])
            P.act(gs[:, 5:6], gs[:, 2:3], AF.Sqrt, [gk, ("gq", yb)], [("gr", yb)], scale=1.0 / 64, bias=gs[:, 4:5])
            P.op("dve", lambda e: e.reciprocal(out=gs[:, 6:7], in_=gs[:, 5:6]), [("gr", yb)], [("gi", yb)])
            P.ts("dve", ynw[yb][:], Y, gs[:, 1:2], gs[:, 6:7], ALU.add, ALU.mult, [gk, ("gi", yb)], ["Yps", ("ynw", yb)])
            yt_ps = Yps.bitcast(BF16)[:, 512 + 128 * yb:512 + 128 * (yb + 1)]
            P.tr(yt_ps, ynw[yb][:], identb[:], [("ynw", yb), "ident"], ["Yps"])
            P.copy("act", yT[0:64, t0:t0 + 64], yt_ps[0:64, 0:64], [], ["Yps", ("yT", n, 0)])
            P.copy("act", yT[64:128, t0:t0 + 64], yt_ps[64:128, 64:128], [], ["Yps", ("yT", n, 1)])
            yield

        def run(gens):
            for g_ in gens:
                for _ in g_:
                    pass

        def interleave(ga, gb):
            alive = [ga, gb]
            while alive:
                for g_ in list(alive):
                    try:
                        next(g_)
                    except StopIteration:
                        alive.remove(g_)

        def lockstep(gs_):
            alive = list(gs_)
            while alive:
                for g_ in list(alive):
                    try:
                        next(g_)
                    except StopIteration:
                        alive.remove(g_)
                yield

        def seq(gs_):
            for g_ in gs_:
                for _ in g_:
                    yield

        ngroups = NCK // GL
        run([lockstep([precompute(q, q) for q in range(GL)])])
        if RWKV_DEBUG == 2:
            break
        for gi in range(ngroups):
            ch = seq([chain(GL * gi + q) for q in range(GL)])
            if gi + 1 < ngroups:
                prg = lockstep([precompute(GL * (gi + 1) + q, q) for q in range(GL)])
                interleave(ch, prg)
            else:
                run([ch])
        yk = [("yT", n, s) for n in range(NCK) for s in range(2)]
        for q in range(4):
            qs = slice(q * 1024, (q + 1) * 1024)
            tq = list(range(q * (1024 // TW), (q + 1) * (1024 // TW)))
            P.ts("dve", yT[:, qs], yT[:, qs], lnwT[:, hp:hp + 1], lnbT[:, hp:hp + 1], ALU.mult, ALU.add, yk + ["lnwT", "lnbT"], [("yTf", q)])
            P.tt("pool", bonv[:, qs], yT[:, qs], bonv[:, qs], ALU.add, [("yTf", q)], [("bonv", t_) for t_ in tq])
            P.tt("pool", bonv[:, qs], bonv[:, qs], gT[:, qs], ALU.mult, [("gT", t_) for t_ in tq], [("bonv", t_) for t_ in tq])
        P.dma("sp", brT_d[3 + hp], bonv[:], reads=[("bonv", t_) for t_ in range(NT1)], writes=[("brT_d", 3 + hp)])


def phase_merge(P, brT_d, pT_d, wba, wbr, wbs, mT_d):
    brT = P.sb([128, 8, S], BF16)
    for t_ in range(8):
        P.dma("sp" if t_ % 2 == 0 else "act", brT[:, :, t_ * 512:(t_ + 1) * 512],
              brT_d[:, :, t_ * 512:(t_ + 1) * 512].rearrange("k p t -> p k t"), writes=[("brT", t_)])
    wall = P.sb([128, 8, 1024], BF16)
    P.dma("pool", wall[:, 0:3, :], wba.rearrange("(k p) c -> p k c", p=128), writes=[("wall", 0)])
    P.dma("pool", wall[:, 3:6, :], wbr.rearrange("(k p) c -> p k c", p=128), writes=[("wall", 1)])
    P.dma("pool", wall[:, 6:8, :], wbs.rearrange("(k p) c -> p k c", p=128), writes=[("wall", 2)])
    gt = [P.sb([128, 3, S], BF16) for _ in range(2)]
    stage = [P.sb([128, S], BF16) for _ in range(2)]
    NM = 4
    m = [P.sb([128, 3, 512], BF16) for _ in range(NM)]
    banks = [P.ps([128, 512], F32) for _ in range(8)]
    kr = ((0, 3), (3, 6), (6, 8))

    def unit(dc, tt, ui):
        gb = dc % 2
        mb = ui % NM
        tsl = slice(tt * 512, (tt + 1) * 512)
        if tt == 0:
            if dc == 0:
                for b3 in range(3):
                    P.dma("act", gt[0][:, b3, :], pT_d[22 + b3 * 8], writes=[("gt", 0, b3)])
            if dc + 1 < 8:
                ngb = (dc + 1) % 2
                for b3 in range(3):
                    P.dma("act", gt[ngb][:, b3, :], pT_d[22 + b3 * 8 + dc + 1], writes=[("gt", ngb, b3)])
        bks = [(3 * ui + b3) % 8 for b3 in range(3)]
        for b3 in range(3):
            k0, k1 = kr[b3]
            for k in range(k0, k1):
                P.mm(banks[bks[b3]][:], wall[:, k, dc * 128:(dc + 1) * 128], brT[:, k, tsl], k == k0, k == k1 - 1,
                     [("wall", b3), ("brT", tt)], [("bank", bks[b3])])
        yield
        for b3 in range(3):
            P.tt("dve", m[mb][:, b3, :], banks[bks[b3]][:], gt[gb][:, b3, tsl], ALU.mult, [("bank", bks[b3]), ("gt", gb, b3)], [("m", mb, b3)])
        yield
        P.tt("pool", m[mb][:, 0, :], m[mb][:, 0, :], m[mb][:, 1, :], ALU.add, [("m", mb, 1)], [("m", mb, 0)])
        yield
        P.tt("dve", stage[gb][:, tsl], m[mb][:, 0, :], m[mb][:, 2, :], ALU.add, [("m", mb, 0), ("m", mb, 2)], [("stage", gb, tt)])
        if tt == 7:
            P.dma("sp", mT_d[:, :, dc, :].rearrange("tt p t -> p tt t"), stage[gb][:].rearrange("p (tt t) -> p tt t", tt=8), reads=[("stage", gb, t_) for t_ in range(8)], writes=[("mT_d", dc)])
        yield

    units = []
    for dc in range(8):
        for tt in range(8):
            units.append(unit(dc, tt, len(units)))
    pipeline(units, 4)


def load_down_w(P, nk, w_ap):
    wsb = P.sb([128, nk, 1024], BF16)
    for k0 in range(0, nk, 4):
        k1 = min(nk, k0 + 4)
        P.dma("pool", wsb[:, k0:k1, :], w_ap[k0 * 128:k1 * 128, :].rearrange("(k p) c -> p k c", p=128), writes=[("wsb", k0)])
    return wsb


def with_prefetch(P, nk, w_ap, first, second):
    wsb = load_down_w(P, nk, w_ap)
    outer = P.stack
    with ExitStack() as st2:
        P.stack = st2
        first(P)
        P.emit()
    P.stack = outer
    second(P, wsb)


def phase_down(P, aT_d, nk, w_ap, hsrc, hdst, norm=None, final=None, wsb=None):
    if wsb is None:
        wsb = load_down_w(P, nk, w_ap)
        wk = [("wsb", k0) for k0 in range(0, nk, 4)]
    else:
        wk = []
    at = [P.sb([128, nk, 512], BF16) for _ in range(2)]
    hb = [P.sb([128, 1024], F32) for _ in range(3)]
    banks = [P.ps([128, 1024], F32) for _ in range(3)]
    eps = P.sb([128, 1], F32)
    P.memset("pool", eps[:], 1e-6, ["eps"])
    ssq = [P.sb([128, 2], F32) for _ in range(3)]
    junk = P.sb([128, 1024], BF16)
    if norm is not None:
        gain_ap, uT_d = norm
        ident = make_ident(P)
        gT = P.sb([128, 8], F32)
        P.dma("sp", gT[:], gain_ap.rearrange("(k p) -> p k", p=128), writes=["gT"], allow_slow_non_contiguous=True)
        ub = [P.sb([128, 1024], BF16) for _ in range(3)]
        pst2_ = [P.ps([128, 1024], BF16) for _ in range(2)]
        pst2 = [t[:].rearrange("p (k t) -> p k t", k=8) for t in pst2_]
        uTt = [P.sb([128, 8, 512], BF16) for _ in range(2)]
    if final is not None:
        gain_ap, y_d = final
        g1 = P.sb([1, 1024], F32)
        P.dma("sp", g1[:], gain_ap.rearrange("(o d) -> o d", o=1), writes=["g1"])
        ones = P.sb([1, 128], F32)
        P.memset("pool", ones[:], 1.0, ["ones"])
        gbt = P.sb([128, 1024], F32)
        for half in range(2):
            P.mm(banks[0][:, half * 512:(half + 1) * 512], ones[:], g1[:, half * 512:(half + 1) * 512], True, True, ["ones", "g1"], [("bank", 0)])
        P.copy("dve", gbt[:], banks[0][:], [("bank", 0)], ["gbt"])
        ob = [P.sb([128, 1024], F32) for _ in range(2)]
    def unit(tt, s, ui):
        ab = tt % 2
        t0 = tt * 512 + s * 128
        hbk = ui % 3
        bk = ui % 3
        ub_i = ui % 3
        if s == 0:
            if tt == 0:
                P.dma("sp", at[0][:], aT_d[0], writes=[("at", 0)])
            if tt + 1 < 8:
                nab = (tt + 1) % 2
                P.dma("sp", at[nab][:], aT_d[tt + 1], writes=[("at", nab)])
        P.dma("sp", hb[hbk][:], hsrc[t0:t0 + 128, :], writes=[("hb", hbk)])
        for half in range(2):
            for k in range(nk):
                P.mm(banks[bk][:, half * 512:(half + 1) * 512], at[ab][:, k, s * 128:(s + 1) * 128], wsb[:, k, half * 512:(half + 1) * 512],
                     k == 0, k == nk - 1, [("at", ab)] + wk, [("bank", bk)])
        yield
        P.tt("dve", hb[hbk][:], banks[bk][:], hb[hbk][:], ALU.add, [("bank", bk)], [("hb", hbk)])
        if final is None:
            P.dma("pool", hdst[t0:t0 + 128, :], hb[hbk][:], reads=[("hb", hbk)], writes=[("hdst", t0)])
        if norm is not None or final is not None:
            sq = ssq[hbk]
            P.act(junk[:], hb[hbk][:], AF.Square, [("hb", hbk)], ["junk", ("ssq", hbk)], accum_out=sq[:, 0:1])
            P.act(sq[:, 1:2], sq[:, 0:1], AF.Sqrt, ["eps"], [("ssq", hbk)], scale=1.0 / D, bias=eps[:])
            P.op("dve", lambda e, o=sq: e.reciprocal(out=o[:, 1:2], in_=o[:, 1:2]), [], [("ssq", hbk)])
        yield
        if norm is not None:
            P.ts("dve", ub[ub_i][:], hb[hbk][:], sq[:, 1:2], None, ALU.mult, None, [("hb", hbk), ("ssq", hbk)], [("ub", ub_i)])
        if final is not None:
            ob_i = ui % 2
            P.stt(ob[ob_i][:], hb[hbk][:], sq[:, 1:2], gbt[:], ALU.mult, ALU.mult, [("hb", hbk), ("ssq", hbk), "gbt"], [("ob", ob_i)])
            P.dma("pool", y_d[t0:t0 + 128, :], ob[ob_i][:], reads=[("ob", ob_i)], writes=[("y", t0)])
        yield
        if norm is not None:
            pst = pst2[ui % 2]
            pk = ("pst", ui % 2)
            for k in range(8):
                P.tr(pst[:, k, :], ub[ub_i][:, k * 128:(k + 1) * 128], ident[:], [("ub", ub_i), "ident"], [pk])
            yield
            for k in range(8):
                if ui % 2 == 0:
                    P.act(uTt[ab][:, k, s * 128:(s + 1) * 128], pst[:, k, :], AF.Copy, ["gT"], [pk, ("uTt", ab, s)], scale=gT[:, k:k + 1])
                else:
                    P.ts("dve", uTt[ab][:, k, s * 128:(s + 1) * 128], pst[:, k, :], gT[:, k:k + 1], None, ALU.mult, None, ["gT"], [pk, ("uTt", ab, s)])
            if s == 3:
                P.dma("pool", uT_d[tt], uTt[ab][:],
                      reads=[("uTt", ab, q) for q in range(4)], writes=[("uT_d", tt)])
        yield

    units = []
    for tt in range(8):
        for s in range(4):
            units.append(unit(tt, s, len(units)))
    pipeline(units, 3)


def phase_ffn_up(P, uT_d, wgu, hidT_d):
    uT = P.sb([128, 8, S], BF16)
    for t_ in range(8):
        P.dma("sp" if t_ % 2 == 0 else "act", uT[:, :, t_ * 512:(t_ + 1) * 512],
              uT_d[t_], writes=[("uT", t_)])
    wa = [P.sb([128, 8, 512], BF16) for _ in range(2)]
    wg = [P.sb([128, 8, 512], BF16) for _ in range(2)]
    stage = [P.sb([128, S], BF16) for _ in range(2)]
    sa = [P.sb([128, 512], F32) for _ in range(2)]
    banks = [P.ps([128, 512], F32) for _ in range(6)]
    nb = 0
    it = 0
    for jg in range(6):
        j0 = jg * 4
        nj = min(4, 22 - j0)
        wb_ = jg % 2
        P.dma("pool", wa[wb_][:, :, 0:nj * 128], wgu[:, j0 * 128:(j0 + nj) * 128].rearrange("(k p) c -> p k c", p=128), writes=[("wa", wb_)])
        P.dma("pool", wg[wb_][:, :, 0:nj * 128], wgu[:, FH + j0 * 128:FH + (j0 + nj) * 128].rearrange("(k p) c -> p k c", p=128), writes=[("wg", wb_)])
        for jj in range(nj):
            j = j0 + jj
            st = stage[j % 2]
            for tt in range(8):
                tsl = slice(tt * 512, (tt + 1) * 512)
                ba = nb % 6
                bb = (nb + 1) % 6
                nb += 2
                sb_ = it % 2
                it += 1
                for k in range(8):
                    P.mm(banks[ba][:], wa[wb_][:, k, jj * 128:(jj + 1) * 128], uT[:, k, tsl], k == 0, k == 7, [("wa", wb_), ("uT", tt)], [("bank", ba)])
                for k in range(8):
                    P.mm(banks[bb][:], wg[wb_][:, k, jj * 128:(jj + 1) * 128], uT[:, k, tsl], k == 0, k == 7, [("wg", wb_), ("uT", tt)], [("bank", bb)])
                P.act(sa[sb_][:], banks[ba][:], AF.Silu, [("bank", ba)], [("sa", sb_)])
                P.tt("dve", st[:, tsl], banks[bb][:], sa[sb_][:], ALU.mult, [("bank", bb), ("sa", sb_)], [("stage", j % 2, tt)])
            P.dma("sp", hidT_d[:, :, j, :].rearrange("tt p t -> p tt t"), st[:].rearrange("p (tt t) -> p tt t", tt=8), reads=[("stage", j % 2, tt) for tt in range(8)], writes=[("hidT_d", j)])


def phase_final(P, hsrc, gain_ap, y_d):
    g1 = P.sb([1, 1024], F32)
    P.dma("sp", g1[:], gain_ap.rearrange("(o d) -> o d", o=1), writes=["g1"])
    ones = P.sb([1, 128], F32)
    P.memset("dve", ones[:], 1.0, ["ones"])
    gps = P.ps([128, 1024], F32)
    gb = P.sb([128, 1024], F32)
    for half in range(2):
        P.mm(gps[:, half * 512:(half + 1) * 512], ones[:], g1[:, half * 512:(half + 1) * 512], True, True, ["ones", "g1"], [("gps", half)])
    P.copy("dve", gb[:], gps[:], [("gps", 0), ("gps", 1)], ["gb"])
    eps = P.sb([128, 1], F32)
    P.memset("dve", eps[:], 1e-6, ["eps"])
    hb = [P.sb([128, 1024], F32) for _ in range(2)]
    ob = [P.sb([128, 1024], F32) for _ in range(2)]
    junk = P.sb([128, 1024], BF16)
    ss = [P.sb([128, 1], F32) for _ in range(2)]
    for t in range(32):
        b = t % 2
        P.dma("sp", hb[b][:], hsrc[t * 128:(t + 1) * 128, :], writes=[("hb", b)])
        P.act(junk[:], hb[b][:], AF.Square, [("hb", b)], ["junk", ("ss", b)], accum_out=ss[b][:])
        P.act(ss[b][:], ss[b][:], AF.Sqrt, ["eps"], [("ss", b)], scale=1.0 / D, bias=eps[:])
        P.op("dve", lambda e, o=ss[b]: e.reciprocal(out=o[:], in_=o[:]), [], [("ss", b)])
        P.stt(ob[b][:], hb[b][:], ss[b][:, 0:1], gb[:], ALU.mult, ALU.mult, [("hb", b), ("ss", b), "gb"], [("ob", b)])
        P.dma("act", y_d[t * 128:(t + 1) * 128, :], ob[b][:], reads=[("ob", b)], writes=[("y", t)])


WEIGHT_SPECS = [
    ("norm_mix", [NL, D]), ("w_in", [NL, D, N_IN]), ("rwkv_shift_mix", [NL, 1408]), ("rwkv_w0", [NL, 384]),
    ("rwkv_w2", [NL, 64, 384]), ("rwkv_a0", [NL, 384]), ("rwkv_a2", [NL, 64, 384]), ("rwkv_g2", [NL, 128, 384]),
    ("rwkv_k_k", [NL, 384]), ("rwkv_k_a", [NL, 384]), ("rwkv_r_k", [NL, 6, 64]), ("rwkv_ln_w", [NL, 384]),
    ("rwkv_ln_b", [NL, 384]), ("ssm_a_re", [NL, 16, 64]), ("ssm_a_im", [NL, 16, 64]), ("ssm_log_step", [NL, 16]),
    ("ssm_b_re", [NL, 16, 64, 16]), ("ssm_b_im", [NL, 16, 64, 16]), ("ssm_c_re", [NL, 16, 16, 64]),
    ("ssm_c_im", [NL, 16, 16, 64]), ("ssm_d", [NL, 256]), ("ssm_glu_val", [NL, 256, 256]), ("ssm_glu_gate", [NL, 256, 256]),
    ("w_branch_attn", [NL, 384, D]), ("w_branch_rwkv", [NL, 384, D]), ("w_branch_ssm", [NL, 256, D]), ("w_out", [NL, D, D]),
    ("norm_ffn", [NL, D]), ("ffn_w_gate_up", [NL, D, 2 * FH]), ("ffn_w_down", [NL, FH, D]), ("norm_final", [D]),
]

SCRATCH_SPECS = [
    ("hb", [S, D], F32), ("uT", [8, 128, 8, 512], BF16), ("pT", [NCH, 128, S], BF16), ("zr", [11, 128, S], F32),
    ("ao", [3, S, 130], F32), ("brT", [8, 128, S], BF16), ("mT", [8, 128, 8, 512], BF16), ("hidT", [8, 128, 22, 512], BF16),
]


def build_program(phases=None, debug_out=()):
    nc = bass.Bass("TRN2", target_bir_lowering=False)
    T = {}
    T["x"] = nc.dram_tensor("x", [S, D], F32, kind="ExternalInput").ap()
    for name, shp in WEIGHT_SPECS:
        T[name] = nc.dram_tensor(name, shp, F32, kind="ExternalInput").ap()
    for name, shp, dt in SCRATCH_SPECS:
        kind = "ExternalOutput" if name in debug_out else "Internal"
        T[name] = nc.dram_tensor(name, shp, dt, kind=kind).ap()
    T["y"] = nc.dram_tensor("y", [S, D], F32, kind="ExternalOutput").ap()

    plist = []
    for l in range(NL):
        hsrc = T["x"] if l == 0 else T["hb"]
        if l == 0:
            plist.append(("norm%d" % l, lambda P, l=l, hsrc=hsrc: phase_norm(P, hsrc, T["norm_mix"][l], T["uT"])))
        plist.append(("proj%d" % l, lambda P, l=l: phase_proj(P, T["uT"], T["w_in"][l], T["rwkv_shift_mix"][l], T["pT"], T["zr"])))
        plist.append(("attn%d" % l, lambda P, l=l: phase_attn(P, T["pT"], T["ao"])))
        plist.append(("attnc%d" % l, lambda P, l=l: phase_attn_combine(P, T["ao"], T["brT"])))
        plist.append(("s5_%d" % l, lambda P, l=l: phase_s5(P, T["pT"], {
            "a_re": T["ssm_a_re"][l], "a_im": T["ssm_a_im"][l], "log_step": T["ssm_log_step"][l], "b_re": T["ssm_b_re"][l],
            "b_im": T["ssm_b_im"][l], "c_re": T["ssm_c_re"][l], "c_im": T["ssm_c_im"][l], "d": T["ssm_d"][l],
            "glu_val": T["ssm_glu_val"][l], "glu_gate": T["ssm_glu_gate"][l]}, T["brT"])))
        plist.append(("rwkv%d" % l, lambda P, l=l: phase_rwkv(P, T["zr"], {
            "w0": T["rwkv_w0"][l], "w2": T["rwkv_w2"][l], "a0": T["rwkv_a0"][l], "a2": T["rwkv_a2"][l], "g2": T["rwkv_g2"][l],
            "k_k": T["rwkv_k_k"][l], "k_a": T["rwkv_k_a"][l], "r_k": T["rwkv_r_k"][l], "ln_w": T["rwkv_ln_w"][l],
            "ln_b": T["rwkv_ln_b"][l]}, T["brT"])))
        def f_merge(P, l=l):
            phase_merge(P, T["brT"], T["pT"], T["w_branch_attn"][l], T["w_branch_rwkv"][l], T["w_branch_ssm"][l], T["mT"])

        def f_out(P, wsb, l=l, hsrc=hsrc):
            phase_down(P, T["mT"], 8, T["w_out"][l], hsrc, T["hb"], norm=(T["norm_ffn"][l], T["uT"]), wsb=wsb)

        def f_up(P, l=l):
            phase_ffn_up(P, T["uT"], T["ffn_w_gate_up"][l], T["hidT"])

        def f_dn(P, wsb, l=l):
            if l + 1 < NL:
                phase_down(P, T["hidT"], 22, T["ffn_w_down"][l], T["hb"], T["hb"], norm=(T["norm_mix"][l + 1], T["uT"]), wsb=wsb)
            else:
                phase_down(P, T["hidT"], 22, T["ffn_w_down"][l], T["hb"], T["hb"], final=(T["norm_final"], T["y"]), wsb=wsb)
        plist.append(("mergeout%d" % l, lambda P, l=l, a=f_merge, b=f_out: with_prefetch(P, 8, T["w_out"][l], a, b)))
        plist.append(("ffn%d" % l, lambda P, l=l, a=f_up, b=f_dn: with_prefetch(P, 22, T["ffn_w_down"][l], a, b)))

    with ExitStack() as st:
        P = Prog(nc, st)
        for name, fn in plist:
            if phases is not None and name not in phases:
                continue
            with ExitStack() as pst:
                P.stack = pst
                fn(P)
                P.emit()
        nins = P.nins
    return nc, nins


def kernel(**inputs):
    nc, _ = build_program()
    x = np.ascontiguousarray(inputs["x"], dtype=np.float32)
    wmap = {name: np.ascontiguousarray(inputs[name], dtype=np.float32) for name, _ in WEIGHT_SPECS}
    in_maps = []
    for c in range(8):
        m = dict(wmap)
        m["x"] = x[c]
        in_maps.append(m)
    res = run_bass_kernel_spmd(nc, in_maps, core_ids=list(range(8)))
    return np.stack([np.asarray(r["y"], dtype=np.float32) for r in res.results], axis=0)
```

```python
import os
import numpy as np
from contextlib import ExitStack
import concourse.bass as bass
import concourse.mybir as mybir
from concourse.bass_utils import run_bass_kernel_spmd

F32 = mybir.dt.float32
BF16 = mybir.dt.bfloat16
I32 = mybir.dt.int32
AF = mybir.ActivationFunctionType
ALU = mybir.AluOpType
AX = mybir.AxisListType

SES_ENG = set(os.environ.get("SES", "dve").split(","))
import os
OP_LIMIT = int(os.environ.get('OP_LIMIT', '0'))
NDS = 8

S = 4096
D = 1024
NL = 2
N_IN = 5888
FH = 2816
NCH = 46


class Prog:
    ENG = ("pe", "dve", "act", "pool", "sp")

    def __init__(self, nc, stack):
        self.nc = nc
        self.stack = stack
        self.sem = {e: stack.enter_context(nc.semaphore("s_" + e)) for e in self.ENG}
        self.cnt = {e: 0 for e in self.ENG}
        self.dq = ("sp", "act", "pool")
        self.dsem = {q: [stack.enter_context(nc.semaphore("d_%s%d" % (q, i))) for i in range(NDS)] for q in self.dq}
        self.dcnt = {q: [0] * NDS for q in self.dq}
        self.drr = {q: 0 for q in self.dq}
        self.semobj = {}
        for e in self.ENG:
            self.semobj["s_" + e] = self.sem[e]
        for q in self.dq:
            for i in range(NDS):
                self.semobj["d_%s%d" % (q, i)] = self.dsem[q][i]
        self.waited = {e: {} for e in self.ENG}
        self.ops = {e: [] for e in self.ENG}
        self.buf = {}
        self.nalloc = 0
        self.nins = 0

    def sb(self, shape, dt, name=None):
        self.nalloc += 1
        return self.stack.enter_context(self.nc.sbuf_tensor(name or ("t%d" % self.nalloc), list(shape), dt))

    def ps(self, shape, dt, name=None):
        self.nalloc += 1
        return self.stack.enter_context(self.nc.psum_tensor(name or ("p%d" % self.nalloc), list(shape), dt))

    def _deps(self, reads, writes):
        deps = {}

        def add(tok):
            if tok is None:
                return
            s, v = tok
            if deps.get(s, 0) < v:
                deps[s] = v

        for k in reads:
            b = self.buf.get(k)
            if b:
                add(b["w"])
        for k in writes:
            b = self.buf.get(k)
            if b:
                add(b["w"])
                for s, v in b["r"].items():
                    add((s, v))
        return deps

    def _commit(self, tok, reads, writes):
        for k in writes:
            self.buf[k] = {"w": tok, "r": {}}
        for k in reads:
            if k in writes:
                continue
            b = self.buf.setdefault(k, {"w": None, "r": {}})
            if b["r"].get(tok[0], 0) < tok[1]:
                b["r"][tok[0]] = tok[1]

    def _waits(self, e, deps):
        waits = []
        own = "s_" + e
        for s, v in deps.items():
            if s == own and (e not in SES_ENG):
                continue
            if self.waited[e].get(s, 0) >= v:
                continue
            self.waited[e][s] = v
            waits.append((s, v))
        return waits

    def op(self, e, fn, reads=(), writes=()):
        self.total = getattr(self, "total", 0) + 1
        if OP_LIMIT and self.total > OP_LIMIT:
            return None
        deps = self._deps(reads, writes)
        waits = self._waits(e, deps)
        self.cnt[e] += 1
        tok = ("s_" + e, self.cnt[e])
        self.ops[e].append((waits, fn, tok, 1))
        self._commit(tok, reads, writes)
        return tok

    def dma(self, q, out, in_, reads=(), writes=(), **kw):
        self.total = getattr(self, "total", 0) + 1
        if OP_LIMIT and self.total > OP_LIMIT:
            return None
        deps = self._deps(reads, writes)
        i = self.drr[q]
        self.drr[q] = (i + 1) % NDS
        sname = "d_%s%d" % (q, i)
        prev = self.dcnt[q][i]
        if prev > 0 and deps.get(sname, 0) < prev:
            deps[sname] = prev
        waits = self._waits(q, deps)
        self.dcnt[q][i] = prev + 16
        tok = (sname, prev + 16)

        def fn(eng, out=out, in_=in_, kw=kw):
            return eng.dma_start(out=out, in_=in_, **kw)

        self.ops[q].append((waits, fn, tok, 16))
        self._commit(tok, reads, writes)
        return tok

    def wait_all(self, e):
        deps = {}
        for en in self.ENG:
            if self.cnt[en] > 0:
                deps["s_" + en] = self.cnt[en]
        for q in self.dq:
            for i in range(NDS):
                if self.dcnt[q][i] > 0:
                    deps["d_%s%d" % (q, i)] = self.dcnt[q][i]
        own = "s_" + e
        waits = [(s, v) for s, v in deps.items() if s != own and self.waited[e].get(s, 0) < v]
        for s, v in waits:
            self.waited[e][s] = v
        self.ops[e].append((waits, None, None, 0))

    def emit(self):
        for e in self.ENG:
            self.wait_all(e)
        nc = self.nc
        with nc.Block() as block:
            def mk(e):
                def body(eng):
                    for waits, fn, tok, inc in self.ops[e]:
                        for s, v in waits:
                            eng.wait_ge(self.semobj[s], v)
                        if fn is None:
                            continue
                        ins = fn(eng)
                        ins.then_inc(self.semobj[tok[0]], inc)
                        self.nins += 1
                return body
            block.tensor(mk("pe"))
            block.vector(mk("dve"))
            block.scalar(mk("act"))
            block.gpsimd(mk("pool"))
            block.sync(mk("sp"))
        self.ops = {e: [] for e in self.ENG}
        self.buf = {}

    def mm(self, out, lhsT, rhs, start, stop, reads, writes):
        return self.op("pe", lambda e: e.matmul(out, lhsT, rhs, start=start, stop=stop), reads, writes)

    def tr(self, out, in_, ident, reads, writes):
        return self.op("pe", lambda e: e.transpose(out, in_, ident), reads, writes)

    def act(self, out, in_, func, reads, writes, eng="act", **kw):
        return self.op(eng, lambda e: e.activation(out=out, in_=in_, func=func, **kw), reads, writes)

    def tt(self, eng, out, in0, in1, op, reads, writes):
        return self.op(eng, lambda e: e.tensor_tensor(out=out, in0=in0, in1=in1, op=op), reads, writes)

    def ts(self, eng, out, in0, s1, s2, op0, op1, reads, writes):
        if op1 is None:
            return self.op(eng, lambda e: e.tensor_scalar(out=out, in0=in0, scalar1=s1, scalar2=None, op0=op0), reads, writes)
        return self.op(eng, lambda e: e.tensor_scalar(out=out, in0=in0, scalar1=s1, scalar2=s2, op0=op0, op1=op1), reads, writes)

    def stt(self, out, in0, scalar, in1, op0, op1, reads, writes):
        return self.op("dve", lambda e: e.scalar_tensor_tensor(out=out, in0=in0, scalar=scalar, in1=in1, op0=op0, op1=op1), reads, writes)

    def copy(self, eng, out, in_, reads, writes):
        if eng == "act":
            return self.op(eng, lambda e: e.copy(out=out, in_=in_), reads, writes)
        return self.op(eng, lambda e: e.tensor_copy(out=out, in_=in_), reads, writes)

    def memset(self, eng, ap, val, writes):
        return self.op(eng, lambda e: e.memset(ap, val), (), writes)


def pipeline(gens, depth):
    gens = list(gens)
    live = []
    nxt = 0
    while nxt < len(gens) or live:
        if nxt < len(gens) and len(live) < depth:
            live.append(gens[nxt])
            nxt += 1
        for g_ in list(live):
            try:
                next(g_)
            except StopIteration:
                live.remove(g_)


def make_ident(P, dt=BF16):
    ones = P.sb([128, 128], dt)
    ident = P.sb([128, 128], dt)
    P.memset("pool", ones[:], 1.0, ["ident_ones"])
    P.op("pool", lambda e: e.affine_select(out=ident[:], in_=ones[:], pattern=[[-1, 128]], compare_op=ALU.is_equal,
                                           fill=0.0, base=0, channel_multiplier=1), ["ident_ones"], ["ident"])
    return ident


def phase_norm(P, src, gain_ap, uT_d):
    ident = make_ident(P)
    gT = P.sb([128, 8], F32)
    P.dma("sp", gT[:], gain_ap.rearrange("(k p) -> p k", p=128), writes=["gT"], allow_slow_non_contiguous=True)
    eps = P.sb([128, 1], F32)
    P.memset("dve", eps[:], 1e-6, ["eps"])
    hb = [P.sb([128, 4, 1024], F32) for _ in range(2)]
    ub = [P.sb([128, 4, 1024], BF16) for _ in range(2)]
    junk = P.sb([128, 1024], BF16)
    ss = [P.sb([128, 4], F32) for _ in range(2)]
    rs = [P.sb([128, 4], F32) for _ in range(2)]
    pst = P.ps([128, 8, 512], BF16)
    uTt = [P.sb([128, 8, 512], BF16) for _ in range(2)]
    for tt in range(8):
        b = tt % 2
        P.dma("sp", hb[b][:], src[tt * 512:(tt + 1) * 512, :].rearrange("(s p) d -> p s d", p=128), writes=[("h", b)])
        for s in range(4):
            P.act(junk[:], hb[b][:, s, :], AF.Square, [("h", b)], ["junk", ("ss", b, s)], accum_out=ss[b][:, s:s + 1])
        P.act(rs[b][:], ss[b][:], AF.Sqrt, [("ss", b, s) for s in range(4)] + ["eps"], [("rs", b)], scale=1.0 / D, bias=eps[:])
        P.op("dve", lambda e, o=rs[b]: e.reciprocal(out=o[:], in_=o[:]), [], [("rs", b)])
        for s in range(4):
            P.ts("dve", ub[b][:, s, :], hb[b][:, s, :], rs[b][:, s:s + 1], None, ALU.mult, None, [("h", b), ("rs", b)], [("ub", b, s)])
        for s in range(4):
            for k in range(8):
                P.tr(pst[:, k, s * 128:(s + 1) * 128], ub[b][:, s, k * 128:(k + 1) * 128], ident[:], [("ub", b, s), "ident"], [("pst", k // 2)])
        for k in range(8):
            P.act(uTt[b][:, k, :], pst[:, k, :], AF.Copy, ["gT"], [("pst", k // 2), ("uTt", b)], scale=gT[:, k:k + 1])
        P.dma("sp", uT_d[tt], uTt[b][:], reads=[("uTt", b)], writes=[("uT_d", tt)])


def phase_proj(P, uT_d, w_in_l, mix_ap, pT_d, zr_d):
    uT = P.sb([128, 8, S], BF16)
    for t_ in range(8):
        P.dma("sp" if t_ % 2 == 0 else "act", uT[:, :, t_ * 512:(t_ + 1) * 512],
              uT_d[t_], writes=[("uT", t_)])
    mixT = P.sb([128, 11], F32)
    ommT = P.sb([128, 11], F32)
    P.dma("sp", mixT[:], mix_ap.rearrange("(c p) -> p c", p=128), writes=["mixT"], allow_slow_non_contiguous=True)
    P.ts("dve", ommT[:], mixT[:], -1.0, 1.0, ALU.mult, ALU.add, ["mixT"], ["ommT"])
    wb = [P.sb([128, 8, 512], BF16) for _ in range(2)]
    stage = [P.sb([128, S], BF16) for _ in range(2)]
    zf = P.sb([128, S + 1], F32)
    zm = P.sb([128, S], F32)
    P.memset("dve", zf[:, 0:1], 0.0, [("zf", 0)])
    banks = [P.ps([128, 512], F32) for _ in range(6)]
    nb = 0
    ngroups = (N_IN + 511) // 512
    for cg in range(ngroups):
        c0 = cg * 512
        ncol = min(512, N_IN - c0)
        w = wb[cg % 2]
        P.dma("pool", w[:, :, 0:ncol], w_in_l[:, c0:c0 + ncol].rearrange("(k p) c -> p k c", p=128), writes=[("wb", cg % 2)])
        for cc in range(ncol // 128):
            c = cg * 4 + cc
            is_r = 9 <= c < 20
            st = stage[c % 2]
            for tt in range(8):
                bk = nb % 6
                nb += 1
                for k in range(8):
                    P.mm(banks[bk][:], w[:, k, cc * 128:(cc + 1) * 128], uT[:, k, tt * 512:(tt + 1) * 512], k == 0, k == 7,
                         [("wb", cg % 2), ("uT", tt)], [("bank", bk)])
                if is_r:
                    P.copy("act", zf[:, 1 + tt * 512:1 + (tt + 1) * 512], banks[bk][:], [("bank", bk)], [("zf", 1 + tt)])
                elif c < 3:
                    P.act(st[:, tt * 512:(tt + 1) * 512], banks[bk][:], AF.Copy, [("bank", bk)], [("stage", c % 2, tt)], scale=0.125)
                elif c >= 22:
                    P.act(st[:, tt * 512:(tt + 1) * 512], banks[bk][:], AF.Sigmoid, [("bank", bk)], [("stage", c % 2, tt)])
                else:
                    P.copy("act", st[:, tt * 512:(tt + 1) * 512], banks[bk][:], [("bank", bk)], [("stage", c % 2, tt)])
            if is_r:
                j = c - 9
                zfk = [("zf", i) for i in range(9)]
                P.ts("dve", zm[:], zf[:, 1:S + 1], ommT[:, j:j + 1], None, ALU.mult, None, zfk + ["ommT"], ["zm"])
                P.stt(zm[:], zf[:, 0:S], mixT[:, j:j + 1], zm[:], ALU.mult, ALU.add, zfk + ["mixT"], ["zm"])
                P.dma("sp", zr_d[j], zm[:], reads=["zm"], writes=[("zr_d", j)])
            else:
                P.dma("sp", pT_d[c], st[:], reads=[("stage", c % 2, tt) for tt in range(8)], writes=[("pT_d", c)])


GROUPS = ((128, 1), (512, 4), (2048, 16))


def sl(st, n, d):
    return slice(st, st + (n - 1) * d + 1, d)


def phase_attn(P, pT_d, ao_d):
    ident = make_ident(P)
    qT = P.sb([128, 3, S], BF16)
    kT = P.sb([128, 3, S], BF16)
    vT = P.sb([128, 3, S], BF16)
    for g in range(3):
        P.dma("sp", qT[:, g, :], pT_d[g], writes=[("qT", g)])
        P.dma("act", kT[:, g, :], pT_d[3 + g], writes=[("kT", g)])
        P.dma("sp", vT[:, g, :], pT_d[6 + g], writes=[("vT", g)])
    ones = P.sb([128, 256], BF16)
    mask = P.sb([128, 256], BF16)
    P.memset("pool", ones[:], 1.0, ["mones"])
    P.op("pool", lambda e: e.affine_select(out=mask[:, 0:128], in_=ones[:, 0:128], pattern=[[1, 128]], compare_op=ALU.is_ge,
                                           fill=0.0, base=0, channel_multiplier=-1), ["mones"], ["mask0"])
    P.op("pool", lambda e: e.affine_select(out=mask[:, 128:256], in_=ones[:, 128:256], pattern=[[-1, 128]], compare_op=ALU.is_ge,
                                           fill=0.0, base=0, channel_multiplier=1), ["mones"], ["mask1"])
    vdil = [P.sb([128, 32, 2, 65], BF16) for _ in range(3)]
    pvt_ = [P.ps([128, 1024], BF16) for _ in range(2)]
    pvt = [t[:, 0:128] for t in pvt_]
    sc = [P.ps([128, 512], F32) for _ in range(4)]
    ob_ = [P.ps([128, 512], F32) for _ in range(2)]
    ob = [t[:, 0:130].rearrange("p (j e) -> p j e", j=2) for t in ob_]
    ntr = 0
    for g, (win, d) in enumerate(GROUPS):
        P.memset("dve", vdil[g][:], 1.0, [("vdil", g, b) for b in range(32)])
        nblk = 32 // d
        for r in range(d):
            for n in range(nblk):
                b = r * nblk + n
                st = r + 128 * d * n
                pb = ntr % 2
                ntr += 1
                P.tr(pvt[pb], vT[:, g, sl(st, 128, d)], ident[:], [("vT", g), "ident"], [("pvt", pb)])
                P.copy("dve", vdil[g][:, b, :, 0:64], pvt[pb].rearrange("p (j e) -> p j e", j=2), [("pvt", pb)], [("vdil", g, b)])
    NPT = 6
    pt = [[P.sb([128, 256], BF16) for _ in range(NPT)] for _ in range(2)]
    osb = [P.sb([128, 130], F32) for _ in range(2)]

    def unit(g, d, r, n, nblk, ui):
        b = r * nblk + n
        st = r + 128 * d * n
        nq = 256 if n < nblk - 1 else 128
        o = ui % 2
        pi = ui % NPT
        pp = (ui - 1) % NPT
        for j in range(2):
            s_ = (2 * ui + j) % 4
            P.mm(sc[s_][:, 0:nq], kT[64 * j:64 * j + 64, g, sl(st, 128, d)], qT[64 * j:64 * j + 64, g, sl(st, nq, d)],
                 True, True, [("kT", g), ("qT", g)], [("sc", s_)])
        yield
        for j in range(2):
            s_ = (2 * ui + j) % 4
            P.act(pt[j][pi][:, 0:nq], sc[s_][:, 0:nq], AF.Exp, [("sc", s_)], [("pt", j, pi)])
        yield
        for j in range(2):
            P.tt("pool", pt[j][pi][:, 0:nq], pt[j][pi][:, 0:nq], mask[:, 0:nq], ALU.mult, ["mask0", "mask1"], [("pt", j, pi)])
        yield
        for j in range(2):
            P.mm(ob[o][:, j, :], pt[j][pi][:, 0:128], vdil[g][:, b, j, :], True, n == 0,
                 [("pt", j, pi), ("vdil", g, b)], [("ob", o)])
            if n > 0:
                P.mm(ob[o][:, j, :], pt[j][pp][:, 128:256], vdil[g][:, b - 1, j, :], False, True,
                     [("pt", j, pp), ("vdil", g, b - 1)], [("ob", o)])
        yield
        P.copy("dve", osb[o][:], ob_[o][:, 0:130], [("ob", o)], [("osb", o)])
        P.dma("sp", ao_d[g, sl(st, 128, d), :], osb[o][:], reads=[("osb", o)], writes=[("ao_d", g, b)])
        yield

    units = []
    for g, (win, d) in enumerate(GROUPS):
        nblk = 32 // d
        for r in range(d):
            for n in range(nblk):
                units.append(unit(g, d, r, n, nblk, len(units)))
    pipeline(units, 5)


def phase_attn_combine(P, ao_d, brT_d):
    ident = make_ident(P)
    stage = P.sb([128, 3, S], BF16)
    at = [P.sb([128, 3, 130], F32) for _ in range(2)]
    den = [P.sb([128, 2], F32) for _ in range(2)]
    on = [P.sb([128, 3, 2, 64], BF16) for _ in range(2)]
    pst_ = [P.ps([128, 1024], BF16) for _ in range(2)]
    pst = [t[:, 0:384].rearrange("p (g e) -> p g e", g=3) for t in pst_]
    for t in range(32):
        b = t % 2
        P.dma("sp", at[b][:], ao_d[:, t * 128:(t + 1) * 128, :].rearrange("g p e -> p g e"), writes=[("at", b)])
        a4 = at[b][:].rearrange("p g (j e) -> p g j e", j=2)
        P.op("dve", lambda e, o=den[b], i=a4: e.tensor_reduce(out=o[:], in_=i[:, :, :, 64].rearrange("p g j -> p j g"), axis=AX.X, op=ALU.add),
             [("at", b)], [("den", b)])
        P.op("dve", lambda e, o=den[b]: e.reciprocal(out=o[:], in_=o[:]), [], [("den", b)])
        for j in range(2):
            P.ts("dve", on[b][:, :, j, :], a4[:, :, j, 0:64], den[b][:, j:j + 1], None, ALU.mult, None, [("at", b), ("den", b)], [("on", b, j)])
        onf = on[b][:].rearrange("p g j e -> p (g j e)")
        for g in range(3):
            P.tr(pst[b][:, g, :], onf[:, g * 128:(g + 1) * 128], ident[:], [("on", b, 0), ("on", b, 1), "ident"], [("pst", b)])
        P.copy("act", stage[:, :, t * 128:(t + 1) * 128], pst[b], [("pst", b)], [("stage", t)])
    for g in range(3):
        P.dma("sp", brT_d[g], stage[:, g, :], reads=[("stage", t) for t in range(32)], writes=[("brT_d", g)])


TWO_PI = 6.283185307179586
GELU_C = 2.0 * 0.7978845608028654


def phase_s5(P, pT_d, prm, brT_d):
    identf = make_ident(P, F32)
    uS = P.sb([128, 2, S], BF16)
    P.dma("sp", uS[:, 0, :], pT_d[20], writes=[("uS", 0)])
    P.dma("act", uS[:, 1, :], pT_d[21], writes=[("uS", 1)])
    lre = P.sb([128, 8], F32)
    lim = P.sb([128, 8], F32)
    lst = P.sb([128, 8], F32)
    P.dma("sp", lre[:], prm["a_re"].rearrange("(gp gl) n -> (gl n) gp", gl=2), writes=["lre"], allow_slow_non_contiguous=True)
    P.dma("act", lim[:], prm["a_im"].rearrange("(gp gl) n -> (gl n) gp", gl=2), writes=["lim"], allow_slow_non_contiguous=True)
    for gl in range(2):
        P.dma("sp", lst[gl * 64:(gl + 1) * 64, :], prm["log_step"][gl::2].partition_broadcast(64), writes=[("lst", gl)], allow_slow_non_contiguous=True)
    dT = P.sb([128, 2], F32)
    P.dma("sp", dT[:], prm["d"].rearrange("(cs p) -> p cs", p=128), writes=["dT"], allow_slow_non_contiguous=True)
    Bst_re = P.sb([128, 8, 16], F32)
    Bst_im = P.sb([128, 8, 16], F32)
    P.dma("sp", Bst_re[:], prm["b_re"].rearrange("(gp gl) n c -> (gl n) gp c", gl=2), writes=["Bst_re"])
    P.dma("act", Bst_im[:], prm["b_im"].rearrange("(gp gl) n c -> (gl n) gp c", gl=2), writes=["Bst_im"])
    wv = P.sb([128, 2, 256], BF16)
    wg = P.sb([128, 2, 256], BF16)
    P.dma("pool", wv[:], prm["glu_val"].rearrange("(k p) c -> p k c", p=128), writes=["wv"])
    P.dma("pool", wg[:], prm["glu_gate"].rearrange("(k p) c -> p k c", p=128), writes=["wg"])
    Cin = [[P.sb([128, 128], F32) for _ in range(2)] for _ in range(2)]
    for ri, nm in enumerate(("c_re", "c_im")):
        for cs in range(2):
            P.memset("pool", Cin[ri][cs][:], 0.0, [("Cin", ri, cs)])
            for i in range(4):
                for gl in range(2):
                    g = 2 * (4 * cs + i) + gl
                    P.dma("sp" if gl == 0 else "act", Cin[ri][cs][32 * i + 16 * gl:32 * i + 16 * gl + 16, 64 * gl:64 * gl + 64], prm[nm][g],
                          writes=[("Cin", ri, cs)])

    sm = {}

    def small(name):
        sm[name] = P.sb([128, 8], F32)
        return sm[name]

    for nm in ("stp", "lr", "mag", "angt", "phs", "phc", "s1", "c1", "are", "aim", "den", "am1", "fre", "fim", "nfim", "t1", "t2", "nsT"):
        small(nm)
    ti8 = P.sb([128, 8], I32)
    K_ = lambda n: "sm_" + n
    P.act(sm["stp"][:], lst[:], AF.Exp, [("lst", 0), ("lst", 1)], [K_("stp")])
    P.tt("dve", sm["lr"][:], lre[:], sm["stp"][:], ALU.mult, ["lre", K_("stp")], [K_("lr")])
    P.act(sm["mag"][:], sm["lr"][:], AF.Exp, [K_("lr")], [K_("mag")])
    P.tt("dve", sm["angt"][:], lim[:], sm["stp"][:], ALU.mult, ["lim", K_("stp")], [K_("angt")])
    P.ts("dve", sm["angt"][:], sm["angt"][:], 1.0 / TWO_PI, None, ALU.mult, None, [], [K_("angt")])
    for ph, off, outn in (("phs", 4.0, "s1"), ("phc", 4.25, "c1")):
        P.ts("dve", sm[ph][:], sm["angt"][:], off, None, ALU.add, None, [K_("angt")], [K_(ph)])
        P.copy("dve", ti8[:], sm[ph][:], [K_(ph)], ["ti8"])
        P.tt("dve", sm[ph][:], sm[ph][:], ti8[:], ALU.subtract, ["ti8"], [K_(ph)])
        P.act(sm[outn][:], sm[ph][:], AF.Sin, [K_(ph)], [K_(outn)], scale=TWO_PI)
    P.tt("dve", sm["are"][:], sm["mag"][:], sm["c1"][:], ALU.mult, [K_("mag"), K_("c1")], [K_("are")])
    P.tt("dve", sm["aim"][:], sm["mag"][:], sm["s1"][:], ALU.mult, [K_("mag"), K_("s1")], [K_("aim")])
    P.tt("dve", sm["den"][:], lre[:], lre[:], ALU.mult, ["lre"], [K_("den")])
    P.tt("dve", sm["t1"][:], lim[:], lim[:], ALU.mult, ["lim"], [K_("t1")])
    P.tt("dve", sm["den"][:], sm["den"][:], sm["t1"][:], ALU.add, [K_("t1")], [K_("den")])
    P.op("dve", lambda e: e.reciprocal(out=sm["den"][:], in_=sm["den"][:]), [], [K_("den")])
    P.ts("dve", sm["am1"][:], sm["are"][:], -1.0, None, ALU.add, None, [K_("are")], [K_("am1")])
    P.tt("dve", sm["t1"][:], sm["am1"][:], lre[:], ALU.mult, [K_("am1"), "lre"], [K_("t1")])
    P.tt("dve", sm["t2"][:], sm["aim"][:], lim[:], ALU.mult, [K_("aim"), "lim"], [K_("t2")])
    P.tt("dve", sm["t1"][:], sm["t1"][:], sm["t2"][:], ALU.add, [K_("t2")], [K_("t1")])
    P.tt("dve", sm["fre"][:], sm["t1"][:], sm["den"][:], ALU.mult, [K_("t1"), K_("den")], [K_("fre")])
    P.tt("dve", sm["t1"][:], sm["aim"][:], lre[:], ALU.mult, [K_("aim"), "lre"], [K_("t1")])
    P.tt("dve", sm["t2"][:], sm["am1"][:], lim[:], ALU.mult, [K_("am1"), "lim"], [K_("t2")])
    P.tt("dve", sm["t1"][:], sm["t1"][:], sm["t2"][:], ALU.subtract, [K_("t2")], [K_("t1")])
    P.tt("dve", sm["fim"][:], sm["t1"][:], sm["den"][:], ALU.mult, [K_("t1"), K_("den")], [K_("fim")])
    P.ts("dve", sm["nfim"][:], sm["fim"][:], -1.0, None, ALU.mult, None, [K_("fim")], [K_("nfim")])

    Bblk = [P.sb([128, 8, 128], F32) for _ in range(2)]
    P.memset("pool", Bblk[0][:], 0.0, [("Bblk", 0)])
    P.memset("pool", Bblk[1][:], 0.0, [("Bblk", 1)])
    tb = P.sb([128, 16], F32)
    for gp in range(8):
        i = gp % 4
        for gl in range(2):
            rows = slice(gl * 64, gl * 64 + 64)
            c0 = 32 * i + 16 * gl
            fr = sm["fre"][rows, gp:gp + 1]
            fi = sm["fim"][rows, gp:gp + 1]
            nfi = sm["nfim"][rows, gp:gp + 1]
            P.ts("dve", tb[rows, :], Bst_re[rows, gp, :], fr, None, ALU.mult, None, ["Bst_re", K_("fre")], [("tb", gl)])
            P.stt(Bblk[0][rows, gp, c0:c0 + 16], Bst_im[rows, gp, :], nfi, tb[rows, :], ALU.mult, ALU.add, ["Bst_im", K_("nfim"), ("tb", gl)], [("Bblk", 0)])
            P.ts("dve", tb[rows, :], Bst_im[rows, gp, :], fr, None, ALU.mult, None, ["Bst_im", K_("fre")], [("tb", gl)])
            P.stt(Bblk[1][rows, gp, c0:c0 + 16], Bst_re[rows, gp, :], fi, tb[rows, :], ALU.mult, ALU.add, ["Bst_re", K_("fim"), ("tb", gl)], [("Bblk", 1)])
    LB = [P.sb([128, 8, 128], BF16) for _ in range(2)]
    LC = [P.sb([128, 8, 128], BF16) for _ in range(3)]
    P.memset("pool", LC[0][:], 0.0, [("LC", 0)])
    P.memset("pool", LC[1][:], 0.0, [("LC", 1)])
    P.memset("pool", LC[2][:], 0.0, [("LC", 2)])
    banks = [P.ps([128, 512], F32) for _ in range(8)]
    nb = 0
    for ri in range(2):
        for gp in range(8):
            bk = nb % 8
            nb += 1
            P.tr(banks[bk][:, 0:128], Bblk[ri][:, gp, :], identf[:], [("Bblk", ri), "ident"], [("bank", bk)])
            P.copy("act", LB[ri][:, gp, :], banks[bk][:, 0:128], [("bank", bk)], [("LB", ri)])
        for cs in range(2):
            bk = nb % 8
            nb += 1
            P.tr(banks[bk][:, 0:128], Cin[ri][cs][:], identf[:], [("Cin", ri, cs), "ident"], [("bank", bk)])
            for i in range(4):
                gp = 4 * cs + i
                if ri == 0:
                    P.copy("act", LC[0][:, gp, 32 * i:32 * i + 32], banks[bk][:, 32 * i:32 * i + 32], [("bank", bk)], [("LC", 0)])
                    P.act(LC[2][:, gp, 32 * i:32 * i + 32], banks[bk][:, 32 * i:32 * i + 32], AF.Copy, [("bank", bk)], [("LC", 2)], scale=-1.0)
                else:
                    P.act(LC[ri][:, gp, 32 * i:32 * i + 32], banks[bk][:, 32 * i:32 * i + 32], AF.Copy, [("bank", bk)], [("LC", ri)], scale=-1.0)

    taui = P.sb([128, 513], I32)
    tauf = P.sb([128, 513], F32)
    P.op("pool", lambda e: e.iota(out=taui[:], pattern=[[1, 513]], base=0, channel_multiplier=0), [], ["taui"])
    P.copy("dve", tauf[:], taui[:], ["taui"], ["tauf"])
    ones5 = P.sb([128, 512], F32)
    P.memset("pool", ones5[:], 1.0, ["ones5"])
    St = P.sb([128, 8, 512], BF16)
    Ct = P.sb([128, 8, 512], BF16)
    SL = P.sb([128, 8], F32)
    CL = P.sb([128, 8], F32)
    magT = P.sb([128, 8, 512], F32)
    ph = P.sb([128, 513], F32)
    ti = P.sb([128, 513], I32)
    for gp in range(8):
        for off, tab, last, nm in ((4.0, St, SL, "St"), (4.25, Ct, CL, "Ct")):
            P.ts("dve", ph[:], tauf[:], sm["angt"][:, gp:gp + 1], off, ALU.mult, ALU.add, ["tauf", K_("angt")], ["ph"])
            P.copy("dve", ti[:], ph[:], ["ph"], ["ti"])
            P.tt("dve", ph[:], ph[:], ti[:], ALU.subtract, ["ti"], ["ph"])
            P.act(tab[:, gp, :], ph[:, 0:512], AF.Sin, ["ph"], [(nm, gp)], scale=TWO_PI)
            P.act(last[:, gp:gp + 1], ph[:, 512:513], AF.Sin, ["ph"], [(nm + "L", gp)], scale=TWO_PI)
        P.ts("dve", magT[:, gp, :], ones5[:], sm["mag"][:, gp:gp + 1], None, ALU.mult, None, ["ones5", K_("mag")], [("magT", gp)])
        P.ts("dve", sm["nsT"][:, gp:gp + 1], SL[:, gp:gp + 1], -1.0, None, ALU.mult, None, [("StL", gp)], [("nsT", gp)])

    carry = [P.sb([128, 8], F32) for _ in range(2)]
    P.memset("dve", carry[0][:], 0.0, [("carry", 0, gp) for gp in range(8)])
    P.memset("dve", carry[1][:], 0.0, [("carry", 1, gp) for gp in range(8)])
    ND = 4
    bsb = [[P.sb([128, 512], BF16) for _ in range(2)] for _ in range(ND)]
    ptr_ = [[P.sb([128, 512], BF16) for _ in range(4)] for _ in range(2)]
    zin = [[P.sb([128, 512], BF16) for _ in range(2)] for _ in range(ND)]
    zz = [[P.sb([128, 512], F32) for _ in range(2)] for _ in range(ND)]
    zb = [[P.sb([128, 512], BF16) for _ in range(2)] for _ in range(ND)]
    dtr_ = [[P.sb([128, 512], BF16) for _ in range(4)] for _ in range(2)]
    yf = P.sb([128, 512], F32)
    g2 = P.sb([128, 512], F32)
    zg = [P.sb([128, 512], BF16) for _ in range(2)]
    sg = P.sb([128, 512], F32)
    stage = P.sb([128, 2, S], BF16)
    ctmp = P.sb([128, 2], F32)

    def unit(c, cs, i, ui):
        tsl = slice(c * 512, (c + 1) * 512)
        Y = banks[4 + cs]
        gp = 4 * cs + i
        rb = ui % ND
        pb = ui % 2
        Cg = Ct[:, gp, :]
        Sg = St[:, gp, :]
        kC, kS = ("Ct", gp), ("St", gp)
        for ri in range(2):
            P.mm(banks[2 * pb + ri][:], LB[ri][:, gp, :], uS[:, cs, tsl], True, True, [("LB", ri), ("uS", cs)], [("bank", 2 * pb + ri)])
            P.copy("act", bsb[rb][ri][:], banks[2 * pb + ri][:], [("bank", 2 * pb + ri)], [("bsb", rb, ri)])
        yield
        bre, bim = bsb[rb]
        pt = ptr_[ui % 2]
        P.tt("dve", pt[0][:], bre[:], Cg, ALU.mult, [("bsb", rb, 0), kC], [("pt", ui % 2, 0)])
        P.tt("dve", pt[1][:], bim[:], Sg, ALU.mult, [("bsb", rb, 1), kS], [("pt", ui % 2, 1)])
        P.tt("dve", pt[2][:], bim[:], Cg, ALU.mult, [("bsb", rb, 1), kC], [("pt", ui % 2, 2)])
        P.tt("dve", pt[3][:], bre[:], Sg, ALU.mult, [("bsb", rb, 0), kS], [("pt", ui % 2, 3)])
        P.tt("pool", zin[rb][0][:], pt[0][:], pt[1][:], ALU.add, [("pt", ui % 2, 0), ("pt", ui % 2, 1)], [("zin", rb, 0)])
        P.tt("pool", zin[rb][1][:], pt[2][:], pt[3][:], ALU.subtract, [("pt", ui % 2, 2), ("pt", ui % 2, 3)], [("zin", rb, 1)])
        yield
        for ri in range(2):
            P.op("dve", lambda e, o=zz[rb][ri], d1=zin[rb][ri], ini=carry[ri][:, gp:gp + 1], d0=magT[:, gp, :]:
                 e.tensor_tensor_scan(out=o[:], data0=d0, data1=d1[:], initial=ini, op0=ALU.mult, op1=ALU.add),
                 [("zin", rb, ri), ("magT", gp), ("carry", ri, gp)], [("zz", rb, ri)])
            P.copy("act", zb[rb][ri][:], zz[rb][ri][:], [("zz", rb, ri)], [("zb", rb, ri)])
        zr_, zi_ = zz[rb]
        cT = CL[:, gp:gp + 1]
        sT = SL[:, gp:gp + 1]
        kCL, kSL = ("CtL", gp), ("StL", gp)
        nsT = sm["nsT"][:, gp:gp + 1]
        P.ts("dve", ctmp[:, 0:1], zr_[:, 511:512], cT, None, ALU.mult, None, [("zz", rb, 0), kCL], [("ctmp", 0)])
        P.stt(carry[0][:, gp:gp + 1], zi_[:, 511:512], nsT, ctmp[:, 0:1], ALU.mult, ALU.add, [("zz", rb, 1), ("nsT", gp), ("ctmp", 0)], [("carry", 0, gp)])
        P.ts("dve", ctmp[:, 1:2], zr_[:, 511:512], sT, None, ALU.mult, None, [("zz", rb, 0), kSL], [("ctmp", 1)])
        P.stt(carry[1][:, gp:gp + 1], zi_[:, 511:512], cT, ctmp[:, 1:2], ALU.mult, ALU.add, [("zz", rb, 1), kCL, ("ctmp", 1)], [("carry", 1, gp)])
        yield
        zbr, zbi = zb[rb]
        dt = dtr_[ui % 2]
        dk = [("dt", ui % 2, q) for q in range(4)]
        P.tt("dve", dt[0][:], zbr[:], Cg, ALU.mult, [("zb", rb, 0), kC], [dk[0]])
        P.tt("dve", dt[1][:], zbi[:], Sg, ALU.mult, [("zb", rb, 1), kS], [dk[1]])
        P.tt("dve", dt[2][:], zbr[:], Sg, ALU.mult, [("zb", rb, 0), kS], [dk[2]])
        P.tt("dve", dt[3][:], zbi[:], Cg, ALU.mult, [("zb", rb, 1), kC], [dk[3]])
        for q, wsel in enumerate((0, 2, 1, 1)):
            P.mm(Y[:], LC[wsel][:, gp, :], dt[q][:], i == 0 and q == 0, i == 3 and q == 3, [("LC", wsel), dk[q]], [("bank", 4 + cs)])
        if i == 3:
            P.stt(yf[:], uS[:, cs, tsl], dT[:, cs:cs + 1], Y[:], ALU.mult, ALU.add, [("uS", cs), "dT", ("bank", 4 + cs)], ["yf"])
            P.tt("dve", g2[:], yf[:], yf[:], ALU.mult, ["yf"], ["g2"])
            P.ts("dve", g2[:], g2[:], 0.044715, 1.0, ALU.mult, ALU.add, [], ["g2"])
            P.tt("dve", g2[:], g2[:], yf[:], ALU.mult, ["yf"], ["g2"])
            P.act(g2[:], g2[:], AF.Sigmoid, [], ["g2"], scale=GELU_C)
            P.tt("dve", zg[cs][:], g2[:], yf[:], ALU.mult, ["g2", "yf"], [("zg", cs)])
            if cs == 1:
                for oc in range(2):
                    for k in range(2):
                        P.mm(banks[6][:], wv[:, k, oc * 128:(oc + 1) * 128], zg[k][:], k == 0, k == 1, ["wv", ("zg", k)], [("bank", 6)])
                    for k in range(2):
                        P.mm(banks[7][:], wg[:, k, oc * 128:(oc + 1) * 128], zg[k][:], k == 0, k == 1, ["wg", ("zg", k)], [("bank", 7)])
                    P.act(sg[:], banks[7][:], AF.Sigmoid, [("bank", 7)], ["sg"])
                    P.tt("dve", stage[:, oc, tsl], banks[6][:], sg[:], ALU.mult, [("bank", 6), "sg"], [("stage", oc, c)])
        yield

    units = []
    for c in range(8):
        for cs in range(2):
            for i in range(4):
                units.append(unit(c, cs, i, len(units)))
    pipeline(units, ND)
    for oc in range(2):
        P.dma("sp", brT_d[6 + oc], stage[:, oc, :], reads=[("stage", oc, c) for c in range(8)], writes=[("brT_d", 6 + oc)])


C0 = 0.6065306597126334
GN_EPS = 64e-5
NCK = 64
RB = 8
GL = 4
RWKV_DEBUG = int(os.environ.get('RWKV_DEBUG', '0'))


def phase_rwkv(P, zr_d, prm, brT_d):
    identb = make_ident(P, BF16)

    def colparam(name, ap):
        t = P.sb([128, 3], F32)
        P.dma("sp", t[:], ap.rearrange("(c p) -> p c", p=128), writes=[name], allow_slow_non_contiguous=True)
        return t
    w0T = colparam("w0T", prm["w0"])
    a0T = colparam("a0T", prm["a0"])
    kkT = colparam("kkT", prm["k_k"])
    kaT = colparam("kaT", prm["k_a"])
    lnwT = colparam("lnwT", prm["ln_w"])
    lnbT = colparam("lnbT", prm["ln_b"])
    rkT = colparam("rkT", prm["r_k"].rearrange("h j -> (h j)"))
    omka = P.sb([128, 3], F32)
    P.ts("dve", omka[:], kaT[:], -1.0, 1.0, ALU.mult, ALU.add, ["kaT"], ["omka"])
    w2b = P.sb([128, 384], BF16)
    a2b = P.sb([128, 384], BF16)
    g2b = P.sb([128, 384], BF16)
    P.memset("pool", w2b[:], 0.0, ["w2b"])
    P.memset("pool", a2b[:], 0.0, ["a2b"])
    P.dma("pool", w2b[0:64, :], prm["w2"], writes=["w2b"])
    P.dma("pool", a2b[64:128, :], prm["a2"], writes=["a2b"])
    P.dma("pool", g2b[:], prm["g2"], writes=["g2b"])
    blk1 = P.sb([128, 128], BF16)
    P.memset("pool", blk1[:], 0.0, ["blk1"])
    P.memset("pool", blk1[0:64, 0:64], 1.0, ["blk1"])
    P.memset("pool", blk1[64:128, 64:128], 1.0, ["blk1"])
    rst = P.sb([128, 512], F32)
    P.memset("pool", rst[:], 1.0, ["rst"])
    P.memset("pool", rst[:, 0:512:64], 0.0, ["rst"])
    onesb = P.sb([128, 128], BF16)
    P.memset("pool", onesb[:], 1.0, ["onesb"])
    mask2 = P.sb([128, 256], BF16)
    maskL = P.sb([128, 128], BF16)
    P.memset("pool", mask2[:], 0.0, ["mask2"])
    P.memset("pool", maskL[:], 0.0, ["maskL"])
    for half in range(2):
        rows = slice(64 * half, 64 * half + 64)
        for qb in range(2):
            cols = slice(128 * qb + 64 * half, 128 * qb + 64 * half + 64)
            P.op("pool", lambda e, rows=rows, cols=cols, qb=qb: e.affine_select(
                out=mask2[rows, cols], in_=onesb[rows, 0:64], pattern=[[1, 64]],
                compare_op=(ALU.is_gt if qb == 0 else ALU.is_ge), fill=0.0, base=0, channel_multiplier=-1), ["onesb"], ["mask2"])
        P.op("pool", lambda e, rows=rows: e.affine_select(
            out=maskL[rows, rows], in_=onesb[rows, 0:64], pattern=[[-1, 64]],
            compare_op=ALU.is_gt, fill=0.0, base=0, channel_multiplier=1), ["onesb"], ["maskL"])
    eps = P.sb([128, 1], F32)
    P.memset("pool", eps[:], GN_EPS, ["eps"])

    tx = P.sb([128, S], BF16)
    sgx = P.sb([128, S], BF16)
    zl = [P.sb([128, 512], F32) for _ in range(2)]
    for q in range(8):
        b = q % 2
        qs = slice(q * 512, (q + 1) * 512)
        P.dma("sp", zl[b][:], zr_d[9][:, qs], writes=[("zl", b)])
        P.act(tx[0:64, qs], zl[b][0:64, :], AF.Tanh, [("zl", b)], [("tx", q)])
        P.copy("dve", tx[64:128, qs], zl[b][64:128, :], [("zl", b)], [("tx", q)])
    for q in range(8):
        b = q % 2
        qs = slice(q * 512, (q + 1) * 512)
        P.dma("sp", zl[b][:], zr_d[10][:, qs], writes=[("zl", b)])
        P.act(sgx[:, qs], zl[b][:], AF.Sigmoid, [("zl", b)], [("sgx", q)])

    ARbd = P.sb([128, NCK, 2, 128], BF16)
    BKbd = P.sb([128, NCK, 2, 128], BF16)
    vbd = P.sb([128, NCK, 128], BF16)
    for q in range(4):
        cq = slice(q * 16, (q + 1) * 16)
        P.memset("pool", ARbd[:, cq], 0.0, [("ARz", q)])
        P.memset("pool", BKbd[:, cq], 0.0, [("BKz", q)])
    P.memset("pool", vbd[:], 0.0, ["vbz"])
    PCt = P.sb([128, NCK], F32)
    bonv = P.sb([128, S], BF16)
    gT = P.sb([128, S], BF16)
    yT = P.sb([128, S], BF16)
    TW = 256
    zin_s = [[P.sb([128, TW], F32) for _ in range(3)] for _ in range(2)]
    T1s = [{n: P.sb([128, TW], F32) for n in ("sig", "eta", "cum", "ex", "eP", "eN", "eX", "kkk", "rn", "kk", "fac", "kp", "b", "rk")} for _ in range(2)]
    T1bs = [{n: P.sb([128, TW], BF16) for n in ("sq", "rkr")} for _ in range(2)]
    Ab = [P.sb([128, 256], BF16) for _ in range(RB)]
    Ak = [P.sb([128, 256], BF16) for _ in range(RB)]
    Tt2 = [P.sb([128, 128], BF16) for _ in range(RB)]
    Btok = [P.sb([128, 128], BF16) for _ in range(RB)]
    Ktok = [P.sb([128, 128], BF16) for _ in range(RB)]
    Vw = [P.sb([128, 128], BF16) for _ in range(RB)]
    Uw = P.sb([128, 128], BF16)
    Mk = [[P.sb([128, 128], BF16) for _ in range(2)] for _ in range(GL)]
    Lk = [[P.sb([128, 128], BF16) for _ in range(2)] for _ in range(GL)]
    MPt = [[P.sb([128, 256], BF16) for _ in range(2)] for _ in range(GL)]
    X2b = P.sb([128, 128], BF16)
    ST32 = P.sb([128, 128], F32)
    STb = P.sb([128, 128], BF16)
    tmpS = P.sb([128, 128], F32)
    ynw = [P.sb([128, 128], BF16) for _ in range(2)]
    gsum = [P.sb([128, 8], F32) for _ in range(2)]
    junk = P.sb([128, 128], BF16)
    Xps = P.ps([128, 512], F32)
    Ups = P.ps([128, 512], F32)
    Yps = P.ps([128, 512], F32)
    Dps = P.ps([128, 512], F32)
    pre = [P.ps([128, 512], F32) for _ in range(GL)]

    for hp in range(3):
        NT1 = S // TW
        CPT = TW // 64

        def step1(tt):
            sb = tt % 2
            T1 = T1s[sb]
            T1b = T1bs[sb]
            zb = zin_s[sb]
            K1 = lambda n: (n, sb)
            tsl = slice(tt * TW, (tt + 1) * TW)
            t8 = (tt * TW) // 512
            for qi, ch in enumerate((hp, 3 + hp, 6 + hp)):
                P.dma("sp", zb[qi][:], zr_d[ch][:, tsl], writes=[("zin", sb, qi)])
            kr, kk_, kv = [("zin", sb, qi) for qi in range(3)]
            r_, k_, v_ = zb
            ck = slice(tt * CPT, (tt + 1) * CPT)
            zq = [("ARz", q) for q in range(4)] + [("BKz", q) for q in range(4)] + ["vbz"]

            def v3h(t, h):
                return t[64 * h:64 * h + 64, :].rearrange("p (c e) -> p c e", e=64)

            def bd(T_, slot, h):
                return T_[64 * h:64 * h + 64, ck, slot, 64 * h:64 * h + 64]
            P.mm(Xps[:, 0:TW], w2b[:, hp * 128:(hp + 1) * 128], tx[:, tsl], True, True, ["w2b", ("tx", t8)], ["Xps"])
            P.mm(Ups[:, 0:TW], a2b[:, hp * 128:(hp + 1) * 128], tx[:, tsl], True, True, ["a2b", ("tx", t8)], ["Ups"])
            P.mm(Yps[:, 0:TW], g2b[:, hp * 128:(hp + 1) * 128], sgx[:, tsl], True, True, ["g2b", ("sgx", t8)], ["Yps"])
            P.act(T1["sig"][:], Xps[:, 0:TW], AF.Sigmoid, ["Xps", "w0T"], [K1("sig")], bias=w0T[:, hp:hp + 1])
            P.act(T1["eta"][:], Ups[:, 0:TW], AF.Sigmoid, ["Ups", "a0T"], [K1("eta")], bias=a0T[:, hp:hp + 1])
            P.copy("act", gT[:, tsl], Yps[:, 0:TW], ["Yps"], [("gT", tt)])
            P.act(T1["kkk"][:], k_[:], AF.Copy, [kk_, "kkT"], [K1("kkk")], scale=kkT[:, hp:hp + 1])
            P.act(T1b["sq"][:], k_[:], AF.Square, [kk_, "kkT"], [K1("sq")], scale=kkT[:, hp:hp + 1])
            yield
            P.op("dve", lambda e: e.tensor_tensor_scan(out=T1["cum"][:], data0=rst[:, 0:TW], data1=T1["sig"][:], initial=0.0, op0=ALU.mult, op1=ALU.add),
                 ["rst", K1("sig")], [K1("cum")])
            P.mm(Dps[:, 0:TW], blk1[:], T1b["sq"][:], True, True, ["blk1", K1("sq")], ["Dps"])
            P.ts("dve", T1["fac"][:], T1["eta"][:], kaT[:, hp:hp + 1], omka[:, hp:hp + 1], ALU.mult, ALU.add, [K1("eta"), "kaT", "omka"], [K1("fac")])
            P.tt("dve", T1["ex"][:], T1["cum"][:], T1["sig"][:], ALU.subtract, [K1("cum"), K1("sig")], [K1("ex")])
            yield
            P.act(T1["rn"][:], Dps[:, 0:TW], AF.Sqrt, ["Dps"], [K1("rn")])
            P.act(T1["eP"][:], T1["cum"][:], AF.Exp, [K1("cum")], [K1("eP")], scale=-C0)
            P.act(T1["eN"][:], T1["cum"][:], AF.Exp, [K1("cum")], [K1("eN")], scale=C0)
            P.act(T1["eX"][:], T1["ex"][:], AF.Exp, [K1("ex")], [K1("eX")], scale=-C0)
            P.tt("pool", T1["kp"][:], k_[:], T1["fac"][:], ALU.mult, [kk_, K1("fac")], [K1("kp")])
            yield
            P.ts("dve", T1["rn"][:], T1["rn"][:], 1e-12, None, ALU.max, None, [], [K1("rn")])
            P.op("dve", lambda e: e.reciprocal(out=T1["rn"][:], in_=T1["rn"][:]), [], [K1("rn")])
            P.copy("dve", PCt[:, tt * CPT:(tt + 1) * CPT], T1["eP"][:, 63:TW:64], [K1("eP")], [("PCt", tt)])
            P.tt("dve", T1["kk"][:], T1["kkk"][:], T1["rn"][:], ALU.mult, [K1("kkk"), K1("rn")], [K1("kk")])
            P.tt("dve", T1["b"][:], T1["kk"][:], T1["eta"][:], ALU.mult, [K1("kk"), K1("eta")], [K1("b")])
            for h in range(2):
                P.tt("pool", bd(ARbd, 1, h), v3h(r_, h), v3h(T1["eP"], h), ALU.mult, [kr, K1("eP")] + zq, [("AR", tt, 1, h)])
            P.tt("pool", T1["rk"][:], r_[:], T1["kp"][:], ALU.mult, [kr, K1("kp")], [K1("rk")])
            yield
            for h in range(2):
                P.tt("pool", bd(BKbd, 1, h), v3h(T1["kp"], h), v3h(T1["eN"], h), ALU.mult, [K1("kp"), K1("eN")] + zq, [("BK", tt, 1, h)])
                P.tt("dve", bd(BKbd, 0, h), v3h(T1["b"], h), v3h(T1["eN"], h), ALU.mult, [K1("b"), K1("eN")] + zq, [("BK", tt, 0, h)])
                P.stt(bd(ARbd, 0, h), v3h(T1["kk"], h), -1.0, v3h(T1["eX"], h), ALU.mult, ALU.mult, [K1("kk"), K1("eX")] + zq, [("AR", tt, 0, h)])
                P.copy("act", vbd[64 * h:64 * h + 64, ck, 64 * h:64 * h + 64], v3h(v_, h), [kv] + zq, [("vbd", tt, h)])
            P.act(T1b["rkr"][:], T1["rk"][:], AF.Copy, [K1("rk"), "rkT"], [K1("rkr")], scale=rkT[:, hp:hp + 1])
            yield
            P.mm(Dps[:, 0:TW], blk1[:], T1b["rkr"][:], True, True, ["blk1", K1("rkr")], ["Dps"])
            P.tt("dve", bonv[:, tsl], Dps[:, 0:TW], v_[:], ALU.mult, ["Dps", kv], [("bonv", tt)])
            yield

        pipeline([step1(tt) for tt in range(NT1)], 2)
        NT8 = NT1
        ARk = [("AR", tt, q, h) for tt in range(NT1) for q in range(2) for h in range(2)]
        BKk = [("BK", tt, q, h) for tt in range(NT1) for q in range(2) for h in range(2)]
        vbk = [("vbd", tt, h) for tt in range(NT1) for h in range(2)]
        PCk = [("PCt", tt) for tt in range(NT1)]
        P.memset("dve", ST32[:], 0.0, ["ST32"])
        P.memset("dve", STb[:], 0.0, ["STb"])
        if RWKV_DEBUG == 1:
            break

        def precompute(n, sl_):
            rb = n % RB
            Bk = pre[sl_]
            kA = ("pre", sl_)
            R0 = Bk[:, 0:256]
            R1 = Bk[:, 256:512]
            ARn = ARbd[:, n, :, :].rearrange("p a e -> p (a e)")
            Abd = ARbd[:, n, 0, :]
            Bbd = BKbd[:, n, 0, :]
            Kbd = BKbd[:, n, 1, :]
            P.mm(R0, Bbd, ARn, True, True, ARk + BKk, [kA])
            P.mm(R1, Kbd, ARn, True, True, ARk + BKk, [kA])
            yield
            P.tt("dve", Ab[rb][:], R0, mask2[:], ALU.mult, ["mask2"], [kA, ("Ab", rb)])
            P.tt("dve", Ak[rb][:], R1, mask2[:], ALU.mult, ["mask2"], [kA, ("Ak", rb)])
            P.mm(R0[:, 0:128], Abd, Bbd, True, True, ARk + BKk, [kA])
            Bv = Bk.bitcast(BF16)
            P.tr(Bv[:, 512:640], Bbd, identb[:], BKk + ["ident"], [kA])
            P.tr(Bv[:, 640:768], Kbd, identb[:], BKk + ["ident"], [kA])
            P.tr(Bv[:, 768:896], vbd[:, n, :], identb[:], vbk + ["ident"], [kA])
            yield
            P.tt("dve", Lk[sl_][0][:], R0[:, 0:128], maskL[:], ALU.mult, ["maskL"], [kA, ("Lk", sl_, 0)])
            P.copy("act", Btok[rb][:], Bv[:, 512:640], [], [kA, ("Btok", rb)])
            P.copy("act", Ktok[rb][:], Bv[:, 640:768], [], [kA, ("Ktok", rb)])
            P.copy("act", Vw[rb][:], Bv[:, 768:896], [], [kA, ("Vw", rb)])
            yield
            M0, kM0 = Ab[rb][:, 0:128], ("Ab", rb)
            L0 = Lk[sl_][0]
            P.mm(R0[:, 0:128], L0[:], M0, True, True, [("Lk", sl_, 0), kM0], [kA])
            P.mm(R1[:, 0:128], M0, L0[:], True, True, [("Lk", sl_, 0), kM0], [kA])
            yield
            P.copy("act", MPt[sl_][0][:, 0:128], R0[:, 0:128], [], [kA, ("MPm", sl_, 0)])
            P.copy("act", Lk[sl_][1][:], R1[:, 0:128], [], [kA, ("Lk", sl_, 1)])
            P.tt("pool", MPt[sl_][0][:, 128:256], M0, identb[:], ALU.add, [kM0, "ident"], [("MPp", sl_, 0)])
            pp = 0
            lp = 1
            for k in range(1, 6):
                MPc = MPt[sl_][pp]
                MPn = MPt[sl_][1 - pp]
                Lc = Lk[sl_][lp]
                Ln = Lk[sl_][1 - lp]
                rd = [("Lk", sl_, lp), ("MPm", sl_, pp), ("MPp", sl_, pp)]
                if k <= 3:
                    P.mm(R0, Lc[:], MPc[:], True, True, rd, [kA])
                else:
                    P.mm(R0[:, 128:256], Lc[:], MPc[:, 128:256], True, True, rd, [kA])
                if k <= 4:
                    P.mm(R1[:, 0:128], MPc[:, 0:128], Lc[:], True, True, rd, [kA])
                yield
                if k <= 3:
                    P.copy("act", MPn[:, 0:128], R0[:, 0:128], [], [kA, ("MPm", sl_, 1 - pp)])
                if k <= 4:
                    P.copy("act", Ln[:], R1[:, 0:128], [], [kA, ("Lk", sl_, 1 - lp)])
                    P.tt("dve", MPn[:, 128:256], R0[:, 128:256], MPc[:, 128:256], ALU.add, [("MPp", sl_, pp)], [kA, ("MPp", sl_, 1 - pp)])
                else:
                    P.tt("dve", Tt2[rb][:], R0[:, 128:256], MPc[:, 128:256], ALU.add, [("MPp", sl_, pp)], [kA, ("Tt2", rb)])
                pp = 1 - pp
                lp = 1 - lp
            yield

        def chain(n):
            rb = n % RB
            t0 = n * 64
            X = Xps[:, 0:128]
            U = Ups[:, 0:128]
            Y = Yps[:, 0:128]
            Dd = Dps[:, 0:128]
            Abd = ARbd[:, n, 0, :]
            Rbd = ARbd[:, n, 1, :]
            P.mm(X, Abd, STb[:], True, False, ARk + ["STb"], ["Xps"])
            P.mm(X, Ak[rb][:, 0:128], Vw[rb][:], False, True, [("Ak", rb), ("Vw", rb)], ["Xps"])
            yield
            P.copy("dve", X2b[:], X, ["Xps"], ["X2b"])
            P.mm(U, Tt2[rb][:], X2b[:], True, True, [("Tt2", rb), "X2b"], ["Ups"])
            yield
            P.copy("dve", Uw[:], U, ["Ups"], ["Uw"])
            P.mm(Dd, Btok[rb][:], Uw[:], True, False, [("Btok", rb), "Uw"], ["Dps"])
            P.mm(Dd, Ktok[rb][:], Vw[rb][:], False, True, [("Ktok", rb), ("Vw", rb)], ["Dps"])
            P.mm(Y, Rbd, STb[:], True, False, ARk + ["STb"], ["Yps"])
            P.mm(Y, Ab[rb][:, 128:256], Uw[:], False, False, [("Ab", rb), "Uw"], ["Yps"])
            P.mm(Y, Ak[rb][:, 128:256], Vw[rb][:], False, True, [("Ak", rb), ("Vw", rb)], ["Yps"])
            yield
            P.tt("dve", tmpS[:], Dd, ST32[:], ALU.add, ["Dps", "ST32"], ["tmpS"])
            P.ts("dve", STb[:], tmpS[:], PCt[:, n:n + 1], None, ALU.mult, None, ["tmpS"] + PCk, ["STb"])
            P.act(ST32[:], tmpS[:], AF.Copy, ["tmpS"] + PCk, ["ST32"], scale=PCt[:, n:n + 1])
            yb = n % 2
            gs = gsum[yb]
            gk = ("gs", yb)
            P.op("dve", lambda e: e.tensor_reduce(out=gs[:, 0:1], in_=Y, axis=AX.X, op=ALU.add), [], ["Yps", gk])
            P.ts("dve", gs[:, 1:2], gs[:, 0:1], -1.0 / 64, None, ALU.mult, None, [], [gk])
            P.act(junk[:], Y, AF.Square, [], ["Yps", "junk", ("gq", yb)], accum_out=gs[:, 2:3])
            yield
            P.tt("dve", gs[:, 3:4], gs[:, 1:2], gs[:, 1:2], ALU.mult, [], [gk])
            P.ts("dve", gs[:, 4:5], gs[:, 3:4], -1.0, GN_EPS, ALU.mult, ALU.add, [], [gk])
            P.act(gs[:, 5:6], gs[:, 2:3], AF.Sqrt, [gk, ("gq", yb)], [("gr", yb)], scale=1.0 / 64, bias=gs[:, 4:5])
            P.op("dve", lambda e: e.reciprocal(out=gs[:, 6:7], in_=gs[:, 5:6]), [("gr", yb)], [("gi", yb)])
            P.ts("dve", ynw[yb][:], Y, gs[:, 1:2], gs[:, 6:7], ALU.add, ALU.mult, [gk, ("gi", yb)], ["Yps", ("ynw", yb)])
            yt_ps = Yps.bitcast(BF16)[:, 512 + 128 * yb:512 + 128 * (yb + 1)]
            P.tr(yt_ps, ynw[yb][:], identb[:], [("ynw", yb), "ident"], ["Yps"])
            P.copy("act", yT[0:64, t0:t0 + 64], yt_ps[0:64, 0:64], [], ["Yps", ("yT", n, 0)])
            P.copy("act", yT[64:128, t0:t0 + 64], yt_ps[64:128, 64:128], [], ["Yps", ("yT", n, 1)])
            yield

        def run(gens):
            for g_ in gens:
                for _ in g_:
                    pass

        def interleave(ga, gb):
            alive = [ga, gb]
            while alive:
                for g_ in list(alive):
                    try:
                        next(g_)
                    except StopIteration:
                        alive.remove(g_)

        def lockstep(gs_):
            alive = list(gs_)
            while alive:
                for g_ in list(alive):
                    try:
                        next(g_)
                    except StopIteration:
                        alive.remove(g_)
                yield

        def seq(gs_):
            for g_ in gs_:
                for _ in g_:
                    yield

        ngroups = NCK // GL
        run([lockstep([precompute(q, q) for q in range(GL)])])
        if RWKV_DEBUG == 2:
            break
        for gi in range(ngroups):
            ch = seq([chain(GL * gi + q) for q in range(GL)])
            if gi + 1 < ngroups:
                prg = lockstep([precompute(GL * (gi + 1) + q, q) for q in range(GL)])
                interleave(ch, prg)
            else:
                run([ch])
        yk = [("yT", n, s) for n in range(NCK) for s in range(2)]
        for q in range(4):
            qs = slice(q * 1024, (q + 1) * 1024)
            tq = list(range(q * (1024 // TW), (q + 1) * (1024 // TW)))
            P.ts("dve", yT[:, qs], yT[:, qs], lnwT[:, hp:hp + 1], lnbT[:, hp:hp + 1], ALU.mult, ALU.add, yk + ["lnwT", "lnbT"], [("yTf", q)])
            P.tt("pool", bonv[:, qs], yT[:, qs], bonv[:, qs], ALU.add, [("yTf", q)], [("bonv", t_) for t_ in tq])
            P.tt("pool", bonv[:, qs], bonv[:, qs], gT[:, qs], ALU.mult, [("gT", t_) for t_ in tq], [("bonv", t_) for t_ in tq])
        P.dma("sp", brT_d[3 + hp], bonv[:], reads=[("bonv", t_) for t_ in range(NT1)], writes=[("brT_d", 3 + hp)])


def phase_merge(P, brT_d, pT_d, wba, wbr, wbs, mT_d):
    brT = P.sb([128, 8, S], BF16)
    for t_ in range(8):
        P.dma("sp" if t_ % 2 == 0 else "act", brT[:, :, t_ * 512:(t_ + 1) * 512],
              brT_d[:, :, t_ * 512:(t_ + 1) * 512].rearrange("k p t -> p k t"), writes=[("brT", t_)])
    wall = P.sb([128, 8, 1024], BF16)
    P.dma("pool", wall[:, 0:3, :], wba.rearrange("(k p) c -> p k c", p=128), writes=[("wall", 0)])
    P.dma("pool", wall[:, 3:6, :], wbr.rearrange("(k p) c -> p k c", p=128), writes=[("wall", 1)])
    P.dma("pool", wall[:, 6:8, :], wbs.rearrange("(k p) c -> p k c", p=128), writes=[("wall", 2)])
    gt = [P.sb([128, 3, S], BF16) for _ in range(2)]
    stage = [P.sb([128, S], BF16) for _ in range(2)]
    NM = 4
    m = [P.sb([128, 3, 512], BF16) for _ in range(NM)]
    banks = [P.ps([128, 512], F32) for _ in range(8)]
    kr = ((0, 3), (3, 6), (6, 8))

    def unit(dc, tt, ui):
        gb = dc % 2
        mb = ui % NM
        tsl = slice(tt * 512, (tt + 1) * 512)
        if tt == 0:
            if dc == 0:
                for b3 in range(3):
                    P.dma("act", gt[0][:, b3, :], pT_d[22 + b3 * 8], writes=[("gt", 0, b3)])
            if dc + 1 < 8:
                ngb = (dc + 1) % 2
                for b3 in range(3):
                    P.dma("act", gt[ngb][:, b3, :], pT_d[22 + b3 * 8 + dc + 1], writes=[("gt", ngb, b3)])
        bks = [(3 * ui + b3) % 8 for b3 in range(3)]
        for b3 in range(3):
            k0, k1 = kr[b3]
            for k in range(k0, k1):
                P.mm(banks[bks[b3]][:], wall[:, k, dc * 128:(dc + 1) * 128], brT[:, k, tsl], k == k0, k == k1 - 1,
                     [("wall", b3), ("brT", tt)], [("bank", bks[b3])])
        yield
        for b3 in range(3):
            P.tt("dve", m[mb][:, b3, :], banks[bks[b3]][:], gt[gb][:, b3, tsl], ALU.mult, [("bank", bks[b3]), ("gt", gb, b3)], [("m", mb, b3)])
        yield
        P.tt("pool", m[mb][:, 0, :], m[mb][:, 0, :], m[mb][:, 1, :], ALU.add, [("m", mb, 1)], [("m", mb, 0)])
        yield
        P.tt("dve", stage[gb][:, tsl], m[mb][:, 0, :], m[mb][:, 2, :], ALU.add, [("m", mb, 0), ("m", mb, 2)], [("stage", gb, tt)])
        if tt == 7:
            P.dma("sp", mT_d[:, :, dc, :].rearrange("tt p t -> p tt t"), stage[gb][:].rearrange("p (tt t) -> p tt t", tt=8), reads=[("stage", gb, t_) for t_ in range(8)], writes=[("mT_d", dc)])
        yield

    units = []
    for dc in range(8):
        for tt in range(8):
            units.append(unit(dc, tt, len(units)))
    pipeline(units, 4)


def load_down_w(P, nk, w_ap):
    wsb = P.sb([128, nk, 1024], BF16)
    for k0 in range(0, nk, 4):
        k1 = min(nk, k0 + 4)
        P.dma("pool", wsb[:, k0:k1, :], w_ap[k0 * 128:k1 * 128, :].rearrange("(k p) c -> p k c", p=128), writes=[("wsb", k0)])
    return wsb


def with_prefetch(P, nk, w_ap, first, second):
    wsb = load_down_w(P, nk, w_ap)
    outer = P.stack
    with ExitStack() as st2:
        P.stack = st2
        first(P)
        P.emit()
    P.stack = outer
    second(P, wsb)


def phase_down(P, aT_d, nk, w_ap, hsrc, hdst, norm=None, final=None, wsb=None):
    if wsb is None:
        wsb = load_down_w(P, nk, w_ap)
        wk = [("wsb", k0) for k0 in range(0, nk, 4)]
    else:
        wk = []
    at = [P.sb([128, nk, 512], BF16) for _ in range(2)]
    hb = [P.sb([128, 1024], F32) for _ in range(3)]
    banks = [P.ps([128, 1024], F32) for _ in range(3)]
    eps = P.sb([128, 1], F32)
    P.memset("pool", eps[:], 1e-6, ["eps"])
    ssq = [P.sb([128, 2], F32) for _ in range(3)]
    junk = P.sb([128, 1024], BF16)
    if norm is not None:
        gain_ap, uT_d = norm
        ident = make_ident(P)
        gT = P.sb([128, 8], F32)
        P.dma("sp", gT[:], gain_ap.rearrange("(k p) -> p k", p=128), writes=["gT"], allow_slow_non_contiguous=True)
        ub = [P.sb([128, 1024], BF16) for _ in range(3)]
        pst2_ = [P.ps([128, 1024], BF16) for _ in range(2)]
        pst2 = [t[:].rearrange("p (k t) -> p k t", k=8) for t in pst2_]
        uTt = [P.sb([128, 8, 512], BF16) for _ in range(2)]
    if final is not None:
        gain_ap, y_d = final
        g1 = P.sb([1, 1024], F32)
        P.dma("sp", g1[:], gain_ap.rearrange("(o d) -> o d", o=1), writes=["g1"])
        ones = P.sb([1, 128], F32)
        P.memset("pool", ones[:], 1.0, ["ones"])
        gbt = P.sb([128, 1024], F32)
        for half in range(2):
            P.mm(banks[0][:, half * 512:(half + 1) * 512], ones[:], g1[:, half * 512:(half + 1) * 512], True, True, ["ones", "g1"], [("bank", 0)])
        P.copy("dve", gbt[:], banks[0][:], [("bank", 0)], ["gbt"])
        ob = [P.sb([128, 1024], F32) for _ in range(2)]
    def unit(tt, s, ui):
        ab = tt % 2
        t0 = tt * 512 + s * 128
        hbk = ui % 3
        bk = ui % 3
        ub_i = ui % 3
        if s == 0:
            if tt == 0:
                P.dma("sp", at[0][:], aT_d[0], writes=[("at", 0)])
            if tt + 1 < 8:
                nab = (tt + 1) % 2
                P.dma("sp", at[nab][:], aT_d[tt + 1], writes=[("at", nab)])
        P.dma("sp", hb[hbk][:], hsrc[t0:t0 + 128, :], writes=[("hb", hbk)])
        for half in range(2):
            for k in range(nk):
                P.mm(banks[bk][:, half * 512:(half + 1) * 512], at[ab][:, k, s * 128:(s + 1) * 128], wsb[:, k, half * 512:(half + 1) * 512],
                     k == 0, k == nk - 1, [("at", ab)] + wk, [("bank", bk)])
        yield
        P.tt("dve", hb[hbk][:], banks[bk][:], hb[hbk][:], ALU.add, [("bank", bk)], [("hb", hbk)])
        if final is None:
            P.dma("pool", hdst[t0:t0 + 128, :], hb[hbk][:], reads=[("hb", hbk)], writes=[("hdst", t0)])
        if norm is not None or final is not None:
            sq = ssq[hbk]
            P.act(junk[:], hb[hbk][:], AF.Square, [("hb", hbk)], ["junk", ("ssq", hbk)], accum_out=sq[:, 0:1])
            P.act(sq[:, 1:2], sq[:, 0:1], AF.Sqrt, ["eps"], [("ssq", hbk)], scale=1.0 / D, bias=eps[:])
            P.op("dve", lambda e, o=sq: e.reciprocal(out=o[:, 1:2], in_=o[:, 1:2]), [], [("ssq", hbk)])
        yield
        if norm is not None:
            P.ts("dve", ub[ub_i][:], hb[hbk][:], sq[:, 1:2], None, ALU.mult, None, [("hb", hbk), ("ssq", hbk)], [("ub", ub_i)])
        if final is not None:
            ob_i = ui % 2
            P.stt(ob[ob_i][:], hb[hbk][:], sq[:, 1:2], gbt[:], ALU.mult, ALU.mult, [("hb", hbk), ("ssq", hbk), "gbt"], [("ob", ob_i)])
            P.dma("pool", y_d[t0:t0 + 128, :], ob[ob_i][:], reads=[("ob", ob_i)], writes=[("y", t0)])
        yield
        if norm is not None:
            pst = pst2[ui % 2]
            pk = ("pst", ui % 2)
            for k in range(8):
                P.tr(pst[:, k, :], ub[ub_i][:, k * 128:(k + 1) * 128], ident[:], [("ub", ub_i), "ident"], [pk])
            yield
            for k in range(8):
                if ui % 2 == 0:
                    P.act(uTt[ab][:, k, s * 128:(s + 1) * 128], pst[:, k, :], AF.Copy, ["gT"], [pk, ("uTt", ab, s)], scale=gT[:, k:k + 1])
                else:
                    P.ts("dve", uTt[ab][:, k, s * 128:(s + 1) * 128], pst[:, k, :], gT[:, k:k + 1], None, ALU.mult, None, ["gT"], [pk, ("uTt", ab, s)])
            if s == 3:
                P.dma("pool", uT_d[tt], uTt[ab][:],
                      reads=[("uTt", ab, q) for q in range(4)], writes=[("uT_d", tt)])
        yield

    units = []
    for tt in range(8):
        for s in range(4):
            units.append(unit(tt, s, len(units)))
    pipeline(units, 3)


def phase_ffn_up(P, uT_d, wgu, hidT_d):
    uT = P.sb([128, 8, S], BF16)
    for t_ in range(8):
        P.dma("sp" if t_ % 2 == 0 else "act", uT[:, :, t_ * 512:(t_ + 1) * 512],
              uT_d[t_], writes=[("uT", t_)])
    wa = [P.sb([128, 8, 512], BF16) for _ in range(2)]
    wg = [P.sb([128, 8, 512], BF16) for _ in range(2)]
    stage = [P.sb([128, S], BF16) for _ in range(2)]
    sa = [P.sb([128, 512], F32) for _ in range(2)]
    banks = [P.ps([128, 512], F32) for _ in range(6)]
    nb = 0
    it = 0
    for jg in range(6):
        j0 = jg * 4
        nj = min(4, 22 - j0)
        wb_ = jg % 2
        P.dma("pool", wa[wb_][:, :, 0:nj * 128], wgu[:, j0 * 128:(j0 + nj) * 128].rearrange("(k p) c -> p k c", p=128), writes=[("wa", wb_)])
        P.dma("pool", wg[wb_][:, :, 0:nj * 128], wgu[:, FH + j0 * 128:FH + (j0 + nj) * 128].rearrange("(k p) c -> p k c", p=128), writes=[("wg", wb_)])
        for jj in range(nj):
            j = j0 + jj
            st = stage[j % 2]
            for tt in range(8):
                tsl = slice(tt * 512, (tt + 1) * 512)
                ba = nb % 6
                bb = (nb + 1) % 6
                nb += 2
                sb_ = it % 2
                it += 1
                for k in range(8):
                    P.mm(banks[ba][:], wa[wb_][:, k, jj * 128:(jj + 1) * 128], uT[:, k, tsl], k == 0, k == 7, [("wa", wb_), ("uT", tt)], [("bank", ba)])
                for k in range(8):
                    P.mm(banks[bb][:], wg[wb_][:, k, jj * 128:(jj + 1) * 128], uT[:, k, tsl], k == 0, k == 7, [("wg", wb_), ("uT", tt)], [("bank", bb)])
                P.act(sa[sb_][:], banks[ba][:], AF.Silu, [("bank", ba)], [("sa", sb_)])
                P.tt("dve", st[:, tsl], banks[bb][:], sa[sb_][:], ALU.mult, [("bank", bb), ("sa", sb_)], [("stage", j % 2, tt)])
            P.dma("sp", hidT_d[:, :, j, :].rearrange("tt p t -> p tt t"), st[:].rearrange("p (tt t) -> p tt t", tt=8), reads=[("stage", j % 2, tt) for tt in range(8)], writes=[("hidT_d", j)])


def phase_final(P, hsrc, gain_ap, y_d):
    g1 = P.sb([1, 1024], F32)
    P.dma("sp", g1[:], gain_ap.rearrange("(o d) -> o d", o=1), writes=["g1"])
    ones = P.sb([1, 128], F32)
    P.memset("dve", ones[:], 1.0, ["ones"])
    gps = P.ps([128, 1024], F32)
    gb = P.sb([128, 1024], F32)
    for half in range(2):
        P.mm(gps[:, half * 512:(half + 1) * 512], ones[:], g1[:, half * 512:(half + 1) * 512], True, True, ["ones", "g1"], [("gps", half)])
    P.copy("dve", gb[:], gps[:], [("gps", 0), ("gps", 1)], ["gb"])
    eps = P.sb([128, 1], F32)
    P.memset("dve", eps[:], 1e-6, ["eps"])
    hb = [P.sb([128, 1024], F32) for _ in range(2)]
    ob = [P.sb([128, 1024], F32) for _ in range(2)]
    junk = P.sb([128, 1024], BF16)
    ss = [P.sb([128, 1], F32) for _ in range(2)]
    for t in range(32):
        b = t % 2
        P.dma("sp", hb[b][:], hsrc[t * 128:(t + 1) * 128, :], writes=[("hb", b)])
        P.act(junk[:], hb[b][:], AF.Square, [("hb", b)], ["junk", ("ss", b)], accum_out=ss[b][:])
        P.act(ss[b][:], ss[b][:], AF.Sqrt, ["eps"], [("ss", b)], scale=1.0 / D, bias=eps[:])
        P.op("dve", lambda e, o=ss[b]: e.reciprocal(out=o[:], in_=o[:]), [], [("ss", b)])
        P.stt(ob[b][:], hb[b][:], ss[b][:, 0:1], gb[:], ALU.mult, ALU.mult, [("hb", b), ("ss", b), "gb"], [("ob", b)])
        P.dma("act", y_d[t * 128:(t + 1) * 128, :], ob[b][:], reads=[("ob", b)], writes=[("y", t)])


WEIGHT_SPECS = [
    ("norm_mix", [NL, D]), ("w_in", [NL, D, N_IN]), ("rwkv_shift_mix", [NL, 1408]), ("rwkv_w0", [NL, 384]),
    ("rwkv_w2", [NL, 64, 384]), ("rwkv_a0", [NL, 384]), ("rwkv_a2", [NL, 64, 384]), ("rwkv_g2", [NL, 128, 384]),
    ("rwkv_k_k", [NL, 384]), ("rwkv_k_a", [NL, 384]), ("rwkv_r_k", [NL, 6, 64]), ("rwkv_ln_w", [NL, 384]),
    ("rwkv_ln_b", [NL, 384]), ("ssm_a_re", [NL, 16, 64]), ("ssm_a_im", [NL, 16, 64]), ("ssm_log_step", [NL, 16]),
    ("ssm_b_re", [NL, 16, 64, 16]), ("ssm_b_im", [NL, 16, 64, 16]), ("ssm_c_re", [NL, 16, 16, 64]),
    ("ssm_c_im", [NL, 16, 16, 64]), ("ssm_d", [NL, 256]), ("ssm_glu_val", [NL, 256, 256]), ("ssm_glu_gate", [NL, 256, 256]),
    ("w_branch_attn", [NL, 384, D]), ("w_branch_rwkv", [NL, 384, D]), ("w_branch_ssm", [NL, 256, D]), ("w_out", [NL, D, D]),
    ("norm_ffn", [NL, D]), ("ffn_w_gate_up", [NL, D, 2 * FH]), ("ffn_w_down", [NL, FH, D]), ("norm_final", [D]),
]

SCRATCH_SPECS = [
    ("hb", [S, D], F32), ("uT", [8, 128, 8, 512], BF16), ("pT", [NCH, 128, S], BF16), ("zr", [11, 128, S], F32),
    ("ao", [3, S, 130], F32), ("brT", [8, 128, S], BF16), ("mT", [8, 128, 8, 512], BF16), ("hidT", [8, 128, 22, 512], BF16),
]


def build_program(phases=None, debug_out=()):
    nc = bass.Bass("TRN2", target_bir_lowering=False)
    T = {}
    T["x"] = nc.dram_tensor("x", [S, D], F32, kind="ExternalInput").ap()
    for name, shp in WEIGHT_SPECS:
        T[name] = nc.dram_tensor(name, shp, F32, kind="ExternalInput").ap()
    for name, shp, dt in SCRATCH_SPECS:
        kind = "ExternalOutput" if name in debug_out else "Internal"
        T[name] = nc.dram_tensor(name, shp, dt, kind=kind).ap()
    T["y"] = nc.dram_tensor("y", [S, D], F32, kind="ExternalOutput").ap()

    plist = []
    for l in range(NL):
        hsrc = T["x"] if l == 0 else T["hb"]
        if l == 0:
            plist.append(("norm%d" % l, lambda P, l=l, hsrc=hsrc: phase_norm(P, hsrc, T["norm_mix"][l], T["uT"])))
        plist.append(("proj%d" % l, lambda P, l=l: phase_proj(P, T["uT"], T["w_in"][l], T["rwkv_shift_mix"][l], T["pT"], T["zr"])))
        plist.append(("attn%d" % l, lambda P, l=l: phase_attn(P, T["pT"], T["ao"])))
        plist.append(("attnc%d" % l, lambda P, l=l: phase_attn_combine(P, T["ao"], T["brT"])))
        plist.append(("s5_%d" % l, lambda P, l=l: phase_s5(P, T["pT"], {
            "a_re": T["ssm_a_re"][l], "a_im": T["ssm_a_im"][l], "log_step": T["ssm_log_step"][l], "b_re": T["ssm_b_re"][l],
            "b_im": T["ssm_b_im"][l], "c_re": T["ssm_c_re"][l], "c_im": T["ssm_c_im"][l], "d": T["ssm_d"][l],
            "glu_val": T["ssm_glu_val"][l], "glu_gate": T["ssm_glu_gate"][l]}, T["brT"])))
        plist.append(("rwkv%d" % l, lambda P, l=l: phase_rwkv(P, T["zr"], {
            "w0": T["rwkv_w0"][l], "w2": T["rwkv_w2"][l], "a0": T["rwkv_a0"][l], "a2": T["rwkv_a2"][l], "g2": T["rwkv_g2"][l],
            "k_k": T["rwkv_k_k"][l], "k_a": T["rwkv_k_a"][l], "r_k": T["rwkv_r_k"][l], "ln_w": T["rwkv_ln_w"][l],
            "ln_b": T["rwkv_ln_b"][l]}, T["brT"])))
        def f_merge(P, l=l):
            phase_merge(P, T["brT"], T["pT"], T["w_branch_attn"][l], T["w_branch_rwkv"][l], T["w_branch_ssm"][l], T["mT"])

        def f_out(P, wsb, l=l, hsrc=hsrc):
            phase_down(P, T["mT"], 8, T["w_out"][l], hsrc, T["hb"], norm=(T["norm_ffn"][l], T["uT"]), wsb=wsb)

        def f_up(P, l=l):
            phase_ffn_up(P, T["uT"], T["ffn_w_gate_up"][l], T["hidT"])

        def f_dn(P, wsb, l=l):
            if l + 1 < NL:
                phase_down(P, T["hidT"], 22, T["ffn_w_down"][l], T["hb"], T["hb"], norm=(T["norm_mix"][l + 1], T["uT"]), wsb=wsb)
            else:
                phase_down(P, T["hidT"], 22, T["ffn_w_down"][l], T["hb"], T["hb"], final=(T["norm_final"], T["y"]), wsb=wsb)
        plist.append(("mergeout%d" % l, lambda P, l=l, a=f_merge, b=f_out: with_prefetch(P, 8, T["w_out"][l], a, b)))
        plist.append(("ffn%d" % l, lambda P, l=l, a=f_up, b=f_dn: with_prefetch(P, 22, T["ffn_w_down"][l], a, b)))

    with ExitStack() as st:
        P = Prog(nc, st)
        for name, fn in plist:
            if phases is not None and name not in phases:
                continue
            with ExitStack() as pst:
                P.stack = pst
                fn(P)
                P.emit()
        nins = P.nins
    return nc, nins


def kernel(**inputs):
    nc, _ = build_program()
    x = np.ascontiguousarray(inputs["x"], dtype=np.float32)
    wmap = {name: np.ascontiguousarray(inputs[name], dtype=np.float32) for name, _ in WEIGHT_SPECS}
    in_maps = []
    for c in range(8):
        m = dict(wmap)
        m["x"] = x[c]
        in_maps.append(m)
    res = run_bass_kernel_spmd(nc, in_maps, core_ids=list(range(8)))
    return np.stack([np.asarray(r["y"], dtype=np.float32) for r in res.results], axis=0)
```

```python
import os
import numpy as np
from contextlib import ExitStack
import concourse.bass as bass
import concourse.mybir as mybir
from concourse.bass_utils import run_bass_kernel_spmd

F32 = mybir.dt.float32
BF16 = mybir.dt.bfloat16
I32 = mybir.dt.int32
AF = mybir.ActivationFunctionType
ALU = mybir.AluOpType
AX = mybir.AxisListType

SES_ENG = set(os.environ.get("SES", "dve").split(","))
import os
OP_LIMIT = int(os.environ.get('OP_LIMIT', '0'))
NDS = 8

S = 4096
D = 1024
NL = 2
N_IN = 5888
FH = 2816
NCH = 46


class Prog:
    ENG = ("pe", "dve", "act", "pool", "sp")

    def __init__(self, nc, stack):
        self.nc = nc
        self.stack = stack
        self.sem = {e: stack.enter_context(nc.semaphore("s_" + e)) for e in self.ENG}
        self.cnt = {e: 0 for e in self.ENG}
        self.dq = ("sp", "act", "pool")
        self.dsem = {q: [stack.enter_context(nc.semaphore("d_%s%d" % (q, i))) for i in range(NDS)] for q in self.dq}
        self.dcnt = {q: [0] * NDS for q in self.dq}
        self.drr = {q: 0 for q in self.dq}
        self.semobj = {}
        for e in self.ENG:
            self.semobj["s_" + e] = self.sem[e]
        for q in self.dq:
            for i in range(NDS):
                self.semobj["d_%s%d" % (q, i)] = self.dsem[q][i]
        self.waited = {e: {} for e in self.ENG}
        self.ops = {e: [] for e in self.ENG}
        self.buf = {}
        self.nalloc = 0
        self.nins = 0

    def sb(self, shape, dt, name=None):
        self.nalloc += 1
        return self.stack.enter_context(self.nc.sbuf_tensor(name or ("t%d" % self.nalloc), list(shape), dt))

    def ps(self, shape, dt, name=None):
        self.nalloc += 1
        return self.stack.enter_context(self.nc.psum_tensor(name or ("p%d" % self.nalloc), list(shape), dt))

    def _deps(self, reads, writes):
        deps = {}

        def add(tok):
            if tok is None:
                return
            s, v = tok
            if deps.get(s, 0) < v:
                deps[s] = v

        for k in reads:
            b = self.buf.get(k)
            if b:
                add(b["w"])
        for k in writes:
            b = self.buf.get(k)
            if b:
                add(b["w"])
                for s, v in b["r"].items():
                    add((s, v))
        return deps

    def _commit(self, tok, reads, writes):
        for k in writes:
            self.buf[k] = {"w": tok, "r": {}}
        for k in reads:
            if k in writes:
                continue
            b = self.buf.setdefault(k, {"w": None, "r": {}})
            if b["r"].get(tok[0], 0) < tok[1]:
                b["r"][tok[0]] = tok[1]

    def _waits(self, e, deps):
        waits = []
        own = "s_" + e
        for s, v in deps.items():
            if s == own and (e not in SES_ENG):
                continue
            if self.waited[e].get(s, 0) >= v:
                continue
            self.waited[e][s] = v
            waits.append((s, v))
        return waits

    def op(self, e, fn, reads=(), writes=()):
        self.total = getattr(self, "total", 0) + 1
        if OP_LIMIT and self.total > OP_LIMIT:
            return None
        deps = self._deps(reads, writes)
        waits = self._waits(e, deps)
        self.cnt[e] += 1
        tok = ("s_" + e, self.cnt[e])
        self.ops[e].append((waits, fn, tok, 1))
        self._commit(tok, reads, writes)
        return tok

    def dma(self, q, out, in_, reads=(), writes=(), **kw):
        self.total = getattr(self, "total", 0) + 1
        if OP_LIMIT and self.total > OP_LIMIT:
            return None
        deps = self._deps(reads, writes)
        i = self.drr[q]
        self.drr[q] = (i + 1) % NDS
        sname = "d_%s%d" % (q, i)
        prev = self.dcnt[q][i]
        if prev > 0 and deps.get(sname, 0) < prev:
            deps[sname] = prev
        waits = self._waits(q, deps)
        self.dcnt[q][i] = prev + 16
        tok = (sname, prev + 16)

        def fn(eng, out=out, in_=in_, kw=kw):
            return eng.dma_start(out=out, in_=in_, **kw)

        self.ops[q].append((waits, fn, tok, 16))
        self._commit(tok, reads, writes)
        return tok

    def wait_all(self, e):
        deps = {}
        for en in self.ENG:
            if self.cnt[en] > 0:
                deps["s_" + en] = self.cnt[en]
        for q in self.dq:
            for i in range(NDS):
                if self.dcnt[q][i] > 0:
                    deps["d_%s%d" % (q, i)] = self.dcnt[q][i]
        own = "s_" + e
        waits = [(s, v) for s, v in deps.items() if s != own and self.waited[e].get(s, 0) < v]
        for s, v in waits:
            self.waited[e][s] = v
        self.ops[e].append((waits, None, None, 0))

    def emit(self):
        for e in self.ENG:
            self.wait_all(e)
        nc = self.nc
        with nc.Block() as block:
            def mk(e):
                def body(eng):
                    for waits, fn, tok, inc in self.ops[e]:
                        for s, v in waits:
                            eng.wait_ge(self.semobj[s], v)
                        if fn is None:
                            continue
                        ins = fn(eng)
                        ins.then_inc(self.semobj[tok[0]], inc)
                        self.nins += 1
                return body
            block.tensor(mk("pe"))
            block.vector(mk("dve"))
            block.scalar(mk("act"))
            block.gpsimd(mk("pool"))
            block.sync(mk("sp"))
        self.ops = {e: [] for e in self.ENG}
        self.buf = {}

    def mm(self, out, lhsT, rhs, start, stop, reads, writes):
        return self.op("pe", lambda e: e.matmul(out, lhsT, rhs, start=start, stop=stop), reads, writes)

    def tr(self, out, in_, ident, reads, writes):
        return self.op("pe", lambda e: e.transpose(out, in_, ident), reads, writes)

    def act(self, out, in_, func, reads, writes, eng="act", **kw):
        return self.op(eng, lambda e: e.activation(out=out, in_=in_, func=func, **kw), reads, writes)

    def tt(self, eng, out, in0, in1, op, reads, writes):
        return self.op(eng, lambda e: e.tensor_tensor(out=out, in0=in0, in1=in1, op=op), reads, writes)

    def ts(self, eng, out, in0, s1, s2, op0, op1, reads, writes):
        if op1 is None:
            return self.op(eng, lambda e: e.tensor_scalar(out=out, in0=in0, scalar1=s1, scalar2=None, op0=op0), reads, writes)
        return self.op(eng, lambda e: e.tensor_scalar(out=out, in0=in0, scalar1=s1, scalar2=s2, op0=op0, op1=op1), reads, writes)

    def stt(self, out, in0, scalar, in1, op0, op1, reads, writes):
        return self.op("dve", lambda e: e.scalar_tensor_tensor(out=out, in0=in0, scalar=scalar, in1=in1, op0=op0, op1=op1), reads, writes)

    def copy(self, eng, out, in_, reads, writes):
        if eng == "act":
            return self.op(eng, lambda e: e.copy(out=out, in_=in_), reads, writes)
        return self.op(eng, lambda e: e.tensor_copy(out=out, in_=in_), reads, writes)

    def memset(self, eng, ap, val, writes):
        return self.op(eng, lambda e: e.memset(ap, val), (), writes)


def pipeline(gens, depth):
    gens = list(gens)
    live = []
    nxt = 0
    while nxt < len(gens) or live:
        if nxt < len(gens) and len(live) < depth:
            live.append(gens[nxt])
            nxt += 1
        for g_ in list(live):
            try:
                next(g_)
            except StopIteration:
                live.remove(g_)


def make_ident(P, dt=BF16):
    ones = P.sb([128, 128], dt)
    ident = P.sb([128, 128], dt)
    P.memset("pool", ones[:], 1.0, ["ident_ones"])
    P.op("pool", lambda e: e.affine_select(out=ident[:], in_=ones[:], pattern=[[-1, 128]], compare_op=ALU.is_equal,
                                           fill=0.0, base=0, channel_multiplier=1), ["ident_ones"], ["ident"])
    return ident


def phase_norm(P, src, gain_ap, uT_d):
    ident = make_ident(P)
    gT = P.sb([128, 8], F32)
    P.dma("sp", gT[:], gain_ap.rearrange("(k p) -> p k", p=128), writes=["gT"], allow_slow_non_contiguous=True)
    eps = P.sb([128, 1], F32)
    P.memset("dve", eps[:], 1e-6, ["eps"])
    hb = [P.sb([128, 4, 1024], F32) for _ in range(2)]
    ub = [P.sb([128, 4, 1024], BF16) for _ in range(2)]
    junk = P.sb([128, 1024], BF16)
    ss = [P.sb([128, 4], F32) for _ in range(2)]
    rs = [P.sb([128, 4], F32) for _ in range(2)]
    pstl = [P.ps([128, 8, 512], BF16) for _ in range(2)]
    uTt = [P.sb([128, 8, 512], BF16) for _ in range(2)]
    for tt in range(8):
        b = tt % 2
        P.dma("sp", hb[b][:], src[tt * 512:(tt + 1) * 512, :].rearrange("(s p) d -> p s d", p=128), writes=[("h", b)])
        for s in range(4):
            P.act(junk[:], hb[b][:, s, :], AF.Square, [("h", b)], ["junk", ("ss", b, s)], accum_out=ss[b][:, s:s + 1])
        P.act(rs[b][:], ss[b][:], AF.Sqrt, [("ss", b, s) for s in range(4)] + ["eps"], [("rs", b)], scale=1.0 / D, bias=eps[:])
        P.op("dve", lambda e, o=rs[b]: e.reciprocal(out=o[:], in_=o[:]), [], [("rs", b)])
        for s in range(4):
            P.ts("dve", ub[b][:, s, :], hb[b][:, s, :], rs[b][:, s:s + 1], None, ALU.mult, None, [("h", b), ("rs", b)], [("ub", b, s)])
        for s in range(4):
            for k in range(8):
                P.tr(pstl[b][:, k, s * 128:(s + 1) * 128], ub[b][:, s, k * 128:(k + 1) * 128], ident[:], [("ub", b, s), "ident"], [("pst", b, k // 2)])
        for k in range(8):
            P.act(uTt[b][:, k, :], pstl[b][:, k, :], AF.Copy, ["gT"], [("pst", b, k // 2), ("uTt", b)], scale=gT[:, k:k + 1])
        P.dma("pool", uT_d[tt], uTt[b][:], reads=[("uTt", b)], writes=[("uT_d", tt)])


def phase_proj(P, uT_d, w_in_l, mix_ap, pT_d, zr_d):
    uT = P.sb([128, 8, S], BF16)
    for t_ in range(8):
        P.dma("sp" if t_ % 2 == 0 else "act", uT[:, :, t_ * 512:(t_ + 1) * 512],
              uT_d[t_], writes=[("uT", t_)])
    mixT = P.sb([128, 11], F32)
    ommT = P.sb([128, 11], F32)
    P.dma("sp", mixT[:], mix_ap.rearrange("(c p) -> p c", p=128), writes=["mixT"], allow_slow_non_contiguous=True)
    P.ts("dve", ommT[:], mixT[:], -1.0, 1.0, ALU.mult, ALU.add, ["mixT"], ["ommT"])
    wb = [P.sb([128, 8, 512], BF16) for _ in range(2)]
    stage = [P.sb([128, S], BF16) for _ in range(2)]
    zf = P.sb([128, S + 1], F32)
    zm = P.sb([128, S], F32)
    P.memset("dve", zf[:, 0:1], 0.0, [("zf", 0)])
    banks = [P.ps([128, 512], F32) for _ in range(6)]
    nb = 0
    ngroups = (N_IN + 511) // 512
    for cg in range(ngroups):
        c0 = cg * 512
        ncol = min(512, N_IN - c0)
        w = wb[cg % 2]
        P.dma("pool", w[:, :, 0:ncol], w_in_l[:, c0:c0 + ncol].rearrange("(k p) c -> p k c", p=128), writes=[("wb", cg % 2)])
        for cc in range(ncol // 128):
            c = cg * 4 + cc
            is_r = 9 <= c < 20
            st = stage[c % 2]
            for tt in range(8):
                bk = nb % 6
                nb += 1
                for k in range(8):
                    P.mm(banks[bk][:], w[:, k, cc * 128:(cc + 1) * 128], uT[:, k, tt * 512:(tt + 1) * 512], k == 0, k == 7,
                         [("wb", cg % 2), ("uT", tt)], [("bank", bk)])
                if is_r:
                    P.copy("act", zf[:, 1 + tt * 512:1 + (tt + 1) * 512], banks[bk][:], [("bank", bk)], [("zf", 1 + tt)])
                elif c < 3:
                    P.act(st[:, tt * 512:(tt + 1) * 512], banks[bk][:], AF.Copy, [("bank", bk)], [("stage", c % 2, tt)], scale=0.125)
                elif c >= 22:
                    P.act(st[:, tt * 512:(tt + 1) * 512], banks[bk][:], AF.Sigmoid, [("bank", bk)], [("stage", c % 2, tt)])
                else:
                    P.copy("act", st[:, tt * 512:(tt + 1) * 512], banks[bk][:], [("bank", bk)], [("stage", c % 2, tt)])
            if is_r:
                j = c - 9
                zfk = [("zf", i) for i in range(9)]
                P.ts("dve", zm[:], zf[:, 1:S + 1], ommT[:, j:j + 1], None, ALU.mult, None, zfk + ["ommT"], ["zm"])
                P.stt(zm[:], zf[:, 0:S], mixT[:, j:j + 1], zm[:], ALU.mult, ALU.add, zfk + ["mixT"], ["zm"])
                P.dma("sp", zr_d[j], zm[:], reads=["zm"], writes=[("zr_d", j)])
            else:
                P.dma("sp", pT_d[c], st[:], reads=[("stage", c % 2, tt) for tt in range(8)], writes=[("pT_d", c)])


GROUPS = ((128, 1), (512, 4), (2048, 16))


def sl(st, n, d):
    return slice(st, st + (n - 1) * d + 1, d)


def phase_attn(P, pT_d, ao_d):
    ident = make_ident(P)
    qT = P.sb([128, 3, S], BF16)
    kT = P.sb([128, 3, S], BF16)
    vT = P.sb([128, 3, S], BF16)
    for g in range(3):
        P.dma("sp", qT[:, g, :], pT_d[g], writes=[("qT", g)])
        P.dma("act", kT[:, g, :], pT_d[3 + g], writes=[("kT", g)])
        P.dma("sp", vT[:, g, :], pT_d[6 + g], writes=[("vT", g)])
    ones = P.sb([128, 256], BF16)
    mask = P.sb([128, 256], BF16)
    P.memset("pool", ones[:], 1.0, ["mones"])
    P.op("pool", lambda e: e.affine_select(out=mask[:, 0:128], in_=ones[:, 0:128], pattern=[[1, 128]], compare_op=ALU.is_ge,
                                           fill=0.0, base=0, channel_multiplier=-1), ["mones"], ["mask0"])
    P.op("pool", lambda e: e.affine_select(out=mask[:, 128:256], in_=ones[:, 128:256], pattern=[[-1, 128]], compare_op=ALU.is_ge,
                                           fill=0.0, base=0, channel_multiplier=1), ["mones"], ["mask1"])
    vdil = [P.sb([128, 32, 2, 65], BF16) for _ in range(3)]
    pvt_ = [P.ps([128, 1024], BF16) for _ in range(2)]
    pvt = [t[:, 0:128] for t in pvt_]
    sc = [P.ps([128, 512], F32) for _ in range(4)]
    ob_ = [P.ps([128, 512], F32) for _ in range(2)]
    ob = [t[:, 0:130].rearrange("p (j e) -> p j e", j=2) for t in ob_]
    ntr = 0
    for g, (win, d) in enumerate(GROUPS):
        P.memset("dve", vdil[g][:], 1.0, [("vdil", g, b) for b in range(32)])
        nblk = 32 // d
        for r in range(d):
            for n in range(nblk):
                b = r * nblk + n
                st = r + 128 * d * n
                pb = ntr % 2
                ntr += 1
                P.tr(pvt[pb], vT[:, g, sl(st, 128, d)], ident[:], [("vT", g), "ident"], [("pvt", pb)])
                P.copy("dve", vdil[g][:, b, :, 0:64], pvt[pb].rearrange("p (j e) -> p j e", j=2), [("pvt", pb)], [("vdil", g, b)])
    NPT = 6
    pt = [[P.sb([128, 256], BF16) for _ in range(NPT)] for _ in range(2)]
    osb = [P.sb([128, 130], F32) for _ in range(2)]

    def unit(g, d, r, n, nblk, ui):
        b = r * nblk + n
        st = r + 128 * d * n
        nq = 256 if n < nblk - 1 else 128
        o = ui % 2
        pi = ui % NPT
        pp = (ui - 1) % NPT
        for j in range(2):
            s_ = (2 * ui + j) % 4
            P.mm(sc[s_][:, 0:nq], kT[64 * j:64 * j + 64, g, sl(st, 128, d)], qT[64 * j:64 * j + 64, g, sl(st, nq, d)],
                 True, True, [("kT", g), ("qT", g)], [("sc", s_)])
        yield
        for j in range(2):
            s_ = (2 * ui + j) % 4
            P.act(pt[j][pi][:, 0:nq], sc[s_][:, 0:nq], AF.Exp, [("sc", s_)], [("pt", j, pi)])
        yield
        for j in range(2):
            P.tt("pool", pt[j][pi][:, 0:nq], pt[j][pi][:, 0:nq], mask[:, 0:nq], ALU.mult, ["mask0", "mask1"], [("pt", j, pi)])
        yield
        for j in range(2):
            P.mm(ob[o][:, j, :], pt[j][pi][:, 0:128], vdil[g][:, b, j, :], True, n == 0,
                 [("pt", j, pi), ("vdil", g, b)], [("ob", o)])
            if n > 0:
                P.mm(ob[o][:, j, :], pt[j][pp][:, 128:256], vdil[g][:, b - 1, j, :], False, True,
                     [("pt", j, pp), ("vdil", g, b - 1)], [("ob", o)])
        yield
        P.copy("dve", osb[o][:], ob_[o][:, 0:130], [("ob", o)], [("osb", o)])
        P.dma("sp", ao_d[g, sl(st, 128, d), :], osb[o][:], reads=[("osb", o)], writes=[("ao_d", g, b)])
        yield

    units = []
    for g, (win, d) in enumerate(GROUPS):
        nblk = 32 // d
        for r in range(d):
            for n in range(nblk):
                units.append(unit(g, d, r, n, nblk, len(units)))
    pipeline(units, 5)


def phase_attn_combine(P, ao_d, brT_d):
    ident = make_ident(P)
    stage = P.sb([128, 3, S], BF16)
    at = [P.sb([128, 3, 130], F32) for _ in range(2)]
    den = [P.sb([128, 2], F32) for _ in range(2)]
    on = [P.sb([128, 3, 2, 64], BF16) for _ in range(2)]
    pst_ = [P.ps([128, 1024], BF16) for _ in range(2)]
    pst = [t[:, 0:384].rearrange("p (g e) -> p g e", g=3) for t in pst_]
    for t in range(32):
        b = t % 2
        P.dma("sp", at[b][:], ao_d[:, t * 128:(t + 1) * 128, :].rearrange("g p e -> p g e"), writes=[("at", b)])
        a4 = at[b][:].rearrange("p g (j e) -> p g j e", j=2)
        P.op("dve", lambda e, o=den[b], i=a4: e.tensor_reduce(out=o[:], in_=i[:, :, :, 64].rearrange("p g j -> p j g"), axis=AX.X, op=ALU.add),
             [("at", b)], [("den", b)])
        P.op("dve", lambda e, o=den[b]: e.reciprocal(out=o[:], in_=o[:]), [], [("den", b)])
        for j in range(2):
            P.ts("dve", on[b][:, :, j, :], a4[:, :, j, 0:64], den[b][:, j:j + 1], None, ALU.mult, None, [("at", b), ("den", b)], [("on", b, j)])
        onf = on[b][:].rearrange("p g j e -> p (g j e)")
        for g in range(3):
            P.tr(pst[b][:, g, :], onf[:, g * 128:(g + 1) * 128], ident[:], [("on", b, 0), ("on", b, 1), "ident"], [("pst", b)])
        P.copy("act", stage[:, :, t * 128:(t + 1) * 128], pst[b], [("pst", b)], [("stage", t)])
    for g in range(3):
        P.dma("sp", brT_d[g], stage[:, g, :], reads=[("stage", t) for t in range(32)], writes=[("brT_d", g)])


TWO_PI = 6.283185307179586
GELU_C = 2.0 * 0.7978845608028654


def phase_s5(P, pT_d, prm, brT_d):
    identf = make_ident(P, F32)
    uS = P.sb([128, 2, S], BF16)
    P.dma("sp", uS[:, 0, :], pT_d[20], writes=[("uS", 0)])
    P.dma("act", uS[:, 1, :], pT_d[21], writes=[("uS", 1)])
    lre = P.sb([128, 8], F32)
    lim = P.sb([128, 8], F32)
    lst = P.sb([128, 8], F32)
    P.dma("sp", lre[:], prm["a_re"].rearrange("(gp gl) n -> (gl n) gp", gl=2), writes=["lre"], allow_slow_non_contiguous=True)
    P.dma("act", lim[:], prm["a_im"].rearrange("(gp gl) n -> (gl n) gp", gl=2), writes=["lim"], allow_slow_non_contiguous=True)
    for gl in range(2):
        P.dma("sp", lst[gl * 64:(gl + 1) * 64, :], prm["log_step"][gl::2].partition_broadcast(64), writes=[("lst", gl)], allow_slow_non_contiguous=True)
    dT = P.sb([128, 2], F32)
    P.dma("sp", dT[:], prm["d"].rearrange("(cs p) -> p cs", p=128), writes=["dT"], allow_slow_non_contiguous=True)
    Bst_re = P.sb([128, 8, 16], F32)
    Bst_im = P.sb([128, 8, 16], F32)
    P.dma("sp", Bst_re[:], prm["b_re"].rearrange("(gp gl) n c -> (gl n) gp c", gl=2), writes=["Bst_re"])
    P.dma("act", Bst_im[:], prm["b_im"].rearrange("(gp gl) n c -> (gl n) gp c", gl=2), writes=["Bst_im"])
    wv = P.sb([128, 2, 256], BF16)
    wg = P.sb([128, 2, 256], BF16)
    P.dma("pool", wv[:], prm["glu_val"].rearrange("(k p) c -> p k c", p=128), writes=["wv"])
    P.dma("pool", wg[:], prm["glu_gate"].rearrange("(k p) c -> p k c", p=128), writes=["wg"])
    Cin = [[P.sb([128, 128], F32) for _ in range(2)] for _ in range(2)]
    for ri, nm in enumerate(("c_re", "c_im")):
        for cs in range(2):
            P.memset("pool", Cin[ri][cs][:], 0.0, [("Cin", ri, cs)])
            for i in range(4):
                for gl in range(2):
                    g = 2 * (4 * cs + i) + gl
                    P.dma("sp" if gl == 0 else "act", Cin[ri][cs][32 * i + 16 * gl:32 * i + 16 * gl + 16, 64 * gl:64 * gl + 64], prm[nm][g],
                          writes=[("Cin", ri, cs)])

    sm = {}

    def small(name):
        sm[name] = P.sb([128, 8], F32)
        return sm[name]

    for nm in ("stp", "lr", "mag", "angt", "phs", "phc", "s1", "c1", "are", "aim", "den", "am1", "fre", "fim", "nfim", "t1", "t2", "nsT"):
        small(nm)
    ti8 = P.sb([128, 8], I32)
    K_ = lambda n: "sm_" + n
    P.act(sm["stp"][:], lst[:], AF.Exp, [("lst", 0), ("lst", 1)], [K_("stp")])
    P.tt("dve", sm["lr"][:], lre[:], sm["stp"][:], ALU.mult, ["lre", K_("stp")], [K_("lr")])
    P.act(sm["mag"][:], sm["lr"][:], AF.Exp, [K_("lr")], [K_("mag")])
    P.tt("dve", sm["angt"][:], lim[:], sm["stp"][:], ALU.mult, ["lim", K_("stp")], [K_("angt")])
    P.ts("dve", sm["angt"][:], sm["angt"][:], 1.0 / TWO_PI, None, ALU.mult, None, [], [K_("angt")])
    for ph, off, outn in (("phs", 4.0, "s1"), ("phc", 4.25, "c1")):
        P.ts("dve", sm[ph][:], sm["angt"][:], off, None, ALU.add, None, [K_("angt")], [K_(ph)])
        P.copy("dve", ti8[:], sm[ph][:], [K_(ph)], ["ti8"])
        P.tt("dve", sm[ph][:], sm[ph][:], ti8[:], ALU.subtract, ["ti8"], [K_(ph)])
        P.act(sm[outn][:], sm[ph][:], AF.Sin, [K_(ph)], [K_(outn)], scale=TWO_PI)
    P.tt("dve", sm["are"][:], sm["mag"][:], sm["c1"][:], ALU.mult, [K_("mag"), K_("c1")], [K_("are")])
    P.tt("dve", sm["aim"][:], sm["mag"][:], sm["s1"][:], ALU.mult, [K_("mag"), K_("s1")], [K_("aim")])
    P.tt("dve", sm["den"][:], lre[:], lre[:], ALU.mult, ["lre"], [K_("den")])
    P.tt("dve", sm["t1"][:], lim[:], lim[:], ALU.mult, ["lim"], [K_("t1")])
    P.tt("dve", sm["den"][:], sm["den"][:], sm["t1"][:], ALU.add, [K_("t1")], [K_("den")])
    P.op("dve", lambda e: e.reciprocal(out=sm["den"][:], in_=sm["den"][:]), [], [K_("den")])
    P.ts("dve", sm["am1"][:], sm["are"][:], -1.0, None, ALU.add, None, [K_("are")], [K_("am1")])
    P.tt("dve", sm["t1"][:], sm["am1"][:], lre[:], ALU.mult, [K_("am1"), "lre"], [K_("t1")])
    P.tt("dve", sm["t2"][:], sm["aim"][:], lim[:], ALU.mult, [K_("aim"), "lim"], [K_("t2")])
    P.tt("dve", sm["t1"][:], sm["t1"][:], sm["t2"][:], ALU.add, [K_("t2")], [K_("t1")])
    P.tt("dve", sm["fre"][:], sm["t1"][:], sm["den"][:], ALU.mult, [K_("t1"), K_("den")], [K_("fre")])
    P.tt("dve", sm["t1"][:], sm["aim"][:], lre[:], ALU.mult, [K_("aim"), "lre"], [K_("t1")])
    P.tt("dve", sm["t2"][:], sm["am1"][:], lim[:], ALU.mult, [K_("am1"), "lim"], [K_("t2")])
    P.tt("dve", sm["t1"][:], sm["t1"][:], sm["t2"][:], ALU.subtract, [K_("t2")], [K_("t1")])
    P.tt("dve", sm["fim"][:], sm["t1"][:], sm["den"][:], ALU.mult, [K_("t1"), K_("den")], [K_("fim")])
    P.ts("dve", sm["nfim"][:], sm["fim"][:], -1.0, None, ALU.mult, None, [K_("fim")], [K_("nfim")])

    Bblk = [P.sb([128, 8, 128], F32) for _ in range(2)]
    P.memset("pool", Bblk[0][:], 0.0, [("Bblk", 0)])
    P.memset("pool", Bblk[1][:], 0.0, [("Bblk", 1)])
    tb = P.sb([128, 16], F32)
    for gp in range(8):
        i = gp % 4
        for gl in range(2):
            rows = slice(gl * 64, gl * 64 + 64)
            c0 = 32 * i + 16 * gl
            fr = sm["fre"][rows, gp:gp + 1]
            fi = sm["fim"][rows, gp:gp + 1]
            nfi = sm["nfim"][rows, gp:gp + 1]
            P.ts("dve", tb[rows, :], Bst_re[rows, gp, :], fr, None, ALU.mult, None, ["Bst_re", K_("fre")], [("tb", gl)])
            P.stt(Bblk[0][rows, gp, c0:c0 + 16], Bst_im[rows, gp, :], nfi, tb[rows, :], ALU.mult, ALU.add, ["Bst_im", K_("nfim"), ("tb", gl)], [("Bblk", 0)])
            P.ts("dve", tb[rows, :], Bst_im[rows, gp, :], fr, None, ALU.mult, None, ["Bst_im", K_("fre")], [("tb", gl)])
            P.stt(Bblk[1][rows, gp, c0:c0 + 16], Bst_re[rows, gp, :], fi, tb[rows, :], ALU.mult, ALU.add, ["Bst_re", K_("fim"), ("tb", gl)], [("Bblk", 1)])
    LB = [P.sb([128, 8, 128], BF16) for _ in range(2)]
    LC = [P.sb([128, 8, 128], BF16) for _ in range(3)]
    P.memset("pool", LC[0][:], 0.0, [("LC", 0)])
    P.memset("pool", LC[1][:], 0.0, [("LC", 1)])
    P.memset("pool", LC[2][:], 0.0, [("LC", 2)])
    banks = [P.ps([128, 512], F32) for _ in range(8)]
    nb = 0
    for ri in range(2):
        for gp in range(8):
            bk = nb % 8
            nb += 1
            P.tr(banks[bk][:, 0:128], Bblk[ri][:, gp, :], identf[:], [("Bblk", ri), "ident"], [("bank", bk)])
            P.copy("act", LB[ri][:, gp, :], banks[bk][:, 0:128], [("bank", bk)], [("LB", ri)])
        for cs in range(2):
            bk = nb % 8
            nb += 1
            P.tr(banks[bk][:, 0:128], Cin[ri][cs][:], identf[:], [("Cin", ri, cs), "ident"], [("bank", bk)])
            for i in range(4):
                gp = 4 * cs + i
                if ri == 0:
                    P.copy("act", LC[0][:, gp, 32 * i:32 * i + 32], banks[bk][:, 32 * i:32 * i + 32], [("bank", bk)], [("LC", 0)])
                    P.act(LC[2][:, gp, 32 * i:32 * i + 32], banks[bk][:, 32 * i:32 * i + 32], AF.Copy, [("bank", bk)], [("LC", 2)], scale=-1.0)
                else:
                    P.act(LC[ri][:, gp, 32 * i:32 * i + 32], banks[bk][:, 32 * i:32 * i + 32], AF.Copy, [("bank", bk)], [("LC", ri)], scale=-1.0)

    taui = P.sb([128, 513], I32)
    tauf = P.sb([128, 513], F32)
    P.op("pool", lambda e: e.iota(out=taui[:], pattern=[[1, 513]], base=0, channel_multiplier=0), [], ["taui"])
    P.copy("dve", tauf[:], taui[:], ["taui"], ["tauf"])
    ones5 = P.sb([128, 512], F32)
    P.memset("pool", ones5[:], 1.0, ["ones5"])
    St = P.sb([128, 8, 512], BF16)
    Ct = P.sb([128, 8, 512], BF16)
    SL = P.sb([128, 8], F32)
    CL = P.sb([128, 8], F32)
    magT = P.sb([128, 8, 512], F32)
    ph = P.sb([128, 513], F32)
    ti = P.sb([128, 513], I32)
    for gp in range(8):
        for off, tab, last, nm in ((4.0, St, SL, "St"), (4.25, Ct, CL, "Ct")):
            P.ts("dve", ph[:], tauf[:], sm["angt"][:, gp:gp + 1], off, ALU.mult, ALU.add, ["tauf", K_("angt")], ["ph"])
            P.copy("dve", ti[:], ph[:], ["ph"], ["ti"])
            P.tt("dve", ph[:], ph[:], ti[:], ALU.subtract, ["ti"], ["ph"])
            P.act(tab[:, gp, :], ph[:, 0:512], AF.Sin, ["ph"], [(nm, gp)], scale=TWO_PI)
            P.act(last[:, gp:gp + 1], ph[:, 512:513], AF.Sin, ["ph"], [(nm + "L", gp)], scale=TWO_PI)
        P.ts("dve", magT[:, gp, :], ones5[:], sm["mag"][:, gp:gp + 1], None, ALU.mult, None, ["ones5", K_("mag")], [("magT", gp)])
        P.ts("dve", sm["nsT"][:, gp:gp + 1], SL[:, gp:gp + 1], -1.0, None, ALU.mult, None, [("StL", gp)], [("nsT", gp)])

    carry = [P.sb([128, 8], F32) for _ in range(2)]
    P.memset("dve", carry[0][:], 0.0, [("carry", 0, gp) for gp in range(8)])
    P.memset("dve", carry[1][:], 0.0, [("carry", 1, gp) for gp in range(8)])
    ND = 4
    bsb = [[P.sb([128, 512], BF16) for _ in range(2)] for _ in range(ND)]
    ptr_ = [[P.sb([128, 512], BF16) for _ in range(4)] for _ in range(2)]
    zin = [[P.sb([128, 512], BF16) for _ in range(2)] for _ in range(ND)]
    zz = [[P.sb([128, 512], F32) for _ in range(2)] for _ in range(ND)]
    zb = [[P.sb([128, 512], BF16) for _ in range(2)] for _ in range(ND)]
    dtr_ = [[P.sb([128, 512], BF16) for _ in range(4)] for _ in range(2)]
    yf = P.sb([128, 512], F32)
    g2 = P.sb([128, 512], F32)
    zg = [P.sb([128, 512], BF16) for _ in range(2)]
    sg = P.sb([128, 512], F32)
    stage = P.sb([128, 2, S], BF16)
    ctmp = P.sb([128, 2], F32)

    def unit(c, cs, i, ui):
        tsl = slice(c * 512, (c + 1) * 512)
        Y = banks[4 + cs]
        gp = 4 * cs + i
        rb = ui % ND
        pb = ui % 2
        Cg = Ct[:, gp, :]
        Sg = St[:, gp, :]
        kC, kS = ("Ct", gp), ("St", gp)
        for ri in range(2):
            P.mm(banks[2 * pb + ri][:], LB[ri][:, gp, :], uS[:, cs, tsl], True, True, [("LB", ri), ("uS", cs)], [("bank", 2 * pb + ri)])
            P.copy("act", bsb[rb][ri][:], banks[2 * pb + ri][:], [("bank", 2 * pb + ri)], [("bsb", rb, ri)])
        yield
        bre, bim = bsb[rb]
        pt = ptr_[ui % 2]
        P.tt("dve", pt[0][:], bre[:], Cg, ALU.mult, [("bsb", rb, 0), kC], [("pt", ui % 2, 0)])
        P.tt("dve", pt[1][:], bim[:], Sg, ALU.mult, [("bsb", rb, 1), kS], [("pt", ui % 2, 1)])
        P.tt("dve", pt[2][:], bim[:], Cg, ALU.mult, [("bsb", rb, 1), kC], [("pt", ui % 2, 2)])
        P.tt("dve", pt[3][:], bre[:], Sg, ALU.mult, [("bsb", rb, 0), kS], [("pt", ui % 2, 3)])
        P.tt("pool", zin[rb][0][:], pt[0][:], pt[1][:], ALU.add, [("pt", ui % 2, 0), ("pt", ui % 2, 1)], [("zin", rb, 0)])
        P.tt("pool", zin[rb][1][:], pt[2][:], pt[3][:], ALU.subtract, [("pt", ui % 2, 2), ("pt", ui % 2, 3)], [("zin", rb, 1)])
        yield
        for ri in range(2):
            P.op("dve", lambda e, o=zz[rb][ri], d1=zin[rb][ri], ini=carry[ri][:, gp:gp + 1], d0=magT[:, gp, :]:
                 e.tensor_tensor_scan(out=o[:], data0=d0, data1=d1[:], initial=ini, op0=ALU.mult, op1=ALU.add),
                 [("zin", rb, ri), ("magT", gp), ("carry", ri, gp)], [("zz", rb, ri)])
            P.copy("act", zb[rb][ri][:], zz[rb][ri][:], [("zz", rb, ri)], [("zb", rb, ri)])
        zr_, zi_ = zz[rb]
        cT = CL[:, gp:gp + 1]
        sT = SL[:, gp:gp + 1]
        kCL, kSL = ("CtL", gp), ("StL", gp)
        nsT = sm["nsT"][:, gp:gp + 1]
        P.ts("dve", ctmp[:, 0:1], zr_[:, 511:512], cT, None, ALU.mult, None, [("zz", rb, 0), kCL], [("ctmp", 0)])
        P.stt(carry[0][:, gp:gp + 1], zi_[:, 511:512], nsT, ctmp[:, 0:1], ALU.mult, ALU.add, [("zz", rb, 1), ("nsT", gp), ("ctmp", 0)], [("carry", 0, gp)])
        P.ts("dve", ctmp[:, 1:2], zr_[:, 511:512], sT, None, ALU.mult, None, [("zz", rb, 0), kSL], [("ctmp", 1)])
        P.stt(carry[1][:, gp:gp + 1], zi_[:, 511:512], cT, ctmp[:, 1:2], ALU.mult, ALU.add, [("zz", rb, 1), kCL, ("ctmp", 1)], [("carry", 1, gp)])
        yield
        zbr, zbi = zb[rb]
        dt = dtr_[ui % 2]
        dk = [("dt", ui % 2, q) for q in range(4)]
        P.tt("dve", dt[0][:], zbr[:], Cg, ALU.mult, [("zb", rb, 0), kC], [dk[0]])
        P.tt("dve", dt[1][:], zbi[:], Sg, ALU.mult, [("zb", rb, 1), kS], [dk[1]])
        P.tt("dve", dt[2][:], zbr[:], Sg, ALU.mult, [("zb", rb, 0), kS], [dk[2]])
        P.tt("dve", dt[3][:], zbi[:], Cg, ALU.mult, [("zb", rb, 1), kC], [dk[3]])
        for q, wsel in enumerate((0, 2, 1, 1)):
            P.mm(Y[:], LC[wsel][:, gp, :], dt[q][:], i == 0 and q == 0, i == 3 and q == 3, [("LC", wsel), dk[q]], [("bank", 4 + cs)])
        if i == 3:
            P.stt(yf[:], uS[:, cs, tsl], dT[:, cs:cs + 1], Y[:], ALU.mult, ALU.add, [("uS", cs), "dT", ("bank", 4 + cs)], ["yf"])
            P.tt("dve", g2[:], yf[:], yf[:], ALU.mult, ["yf"], ["g2"])
            P.ts("dve", g2[:], g2[:], 0.044715, 1.0, ALU.mult, ALU.add, [], ["g2"])
            P.tt("dve", g2[:], g2[:], yf[:], ALU.mult, ["yf"], ["g2"])
            P.act(g2[:], g2[:], AF.Sigmoid, [], ["g2"], scale=GELU_C)
            P.tt("dve", zg[cs][:], g2[:], yf[:], ALU.mult, ["g2", "yf"], [("zg", cs)])
            if cs == 1:
                for oc in range(2):
                    for k in range(2):
                        P.mm(banks[6][:], wv[:, k, oc * 128:(oc + 1) * 128], zg[k][:], k == 0, k == 1, ["wv", ("zg", k)], [("bank", 6)])
                    for k in range(2):
                        P.mm(banks[7][:], wg[:, k, oc * 128:(oc + 1) * 128], zg[k][:], k == 0, k == 1, ["wg", ("zg", k)], [("bank", 7)])
                    P.act(sg[:], banks[7][:], AF.Sigmoid, [("bank", 7)], ["sg"])
                    P.tt("dve", stage[:, oc, tsl], banks[6][:], sg[:], ALU.mult, [("bank", 6), "sg"], [("stage", oc, c)])
        yield

    units = []
    for c in range(8):
        for cs in range(2):
            for i in range(4):
                units.append(unit(c, cs, i, len(units)))
    pipeline(units, ND)
    for oc in range(2):
        P.dma("sp", brT_d[6 + oc], stage[:, oc, :], reads=[("stage", oc, c) for c in range(8)], writes=[("brT_d", 6 + oc)])


C0 = 0.6065306597126334
GN_EPS = 64e-5
NCK = 64
RB = 8
GL = 4
RWKV_DEBUG = int(os.environ.get('RWKV_DEBUG', '0'))


def phase_rwkv(P, zr_d, prm, brT_d):
    identb = make_ident(P, BF16)

    def colparam(name, ap):
        t = P.sb([128, 3], F32)
        P.dma("sp", t[:], ap.rearrange("(c p) -> p c", p=128), writes=[name], allow_slow_non_contiguous=True)
        return t
    w0T = colparam("w0T", prm["w0"])
    a0T = colparam("a0T", prm["a0"])
    kkT = colparam("kkT", prm["k_k"])
    kaT = colparam("kaT", prm["k_a"])
    lnwT = colparam("lnwT", prm["ln_w"])
    lnbT = colparam("lnbT", prm["ln_b"])
    rkT = colparam("rkT", prm["r_k"].rearrange("h j -> (h j)"))
    omka = P.sb([128, 3], F32)
    P.ts("dve", omka[:], kaT[:], -1.0, 1.0, ALU.mult, ALU.add, ["kaT"], ["omka"])
    w2b = P.sb([128, 384], BF16)
    a2b = P.sb([128, 384], BF16)
    g2b = P.sb([128, 384], BF16)
    P.memset("pool", w2b[:], 0.0, ["w2b"])
    P.memset("pool", a2b[:], 0.0, ["a2b"])
    P.dma("pool", w2b[0:64, :], prm["w2"], writes=["w2b"])
    P.dma("pool", a2b[64:128, :], prm["a2"], writes=["a2b"])
    P.dma("pool", g2b[:], prm["g2"], writes=["g2b"])
    blk1 = P.sb([128, 128], BF16)
    P.memset("pool", blk1[:], 0.0, ["blk1"])
    P.memset("pool", blk1[0:64, 0:64], 1.0, ["blk1"])
    P.memset("pool", blk1[64:128, 64:128], 1.0, ["blk1"])
    rst = P.sb([128, 512], F32)
    P.memset("pool", rst[:], 1.0, ["rst"])
    P.memset("pool", rst[:, 0:512:64], 0.0, ["rst"])
    onesb = P.sb([128, 128], BF16)
    P.memset("pool", onesb[:], 1.0, ["onesb"])
    mask2 = P.sb([128, 256], BF16)
    maskL = P.sb([128, 128], BF16)
    P.memset("pool", mask2[:], 0.0, ["mask2"])
    P.memset("pool", maskL[:], 0.0, ["maskL"])
    for half in range(2):
        rows = slice(64 * half, 64 * half + 64)
        for qb in range(2):
            cols = slice(128 * qb + 64 * half, 128 * qb + 64 * half + 64)
            P.op("pool", lambda e, rows=rows, cols=cols, qb=qb: e.affine_select(
                out=mask2[rows, cols], in_=onesb[rows, 0:64], pattern=[[1, 64]],
                compare_op=(ALU.is_gt if qb == 0 else ALU.is_ge), fill=0.0, base=0, channel_multiplier=-1), ["onesb"], ["mask2"])
        P.op("pool", lambda e, rows=rows: e.affine_select(
            out=maskL[rows, rows], in_=onesb[rows, 0:64], pattern=[[-1, 64]],
            compare_op=ALU.is_gt, fill=0.0, base=0, channel_multiplier=1), ["onesb"], ["maskL"])
    eps = P.sb([128, 1], F32)
    P.memset("pool", eps[:], GN_EPS, ["eps"])

    tx = P.sb([128, S], BF16)
    sgx = P.sb([128, S], BF16)
    zl = [P.sb([128, 512], F32) for _ in range(2)]
    for q in range(8):
        b = q % 2
        qs = slice(q * 512, (q + 1) * 512)
        P.dma("sp", zl[b][:], zr_d[9][:, qs], writes=[("zl", b)])
        P.act(tx[0:64, qs], zl[b][0:64, :], AF.Tanh, [("zl", b)], [("tx", q)])
        P.copy("dve", tx[64:128, qs], zl[b][64:128, :], [("zl", b)], [("tx", q)])
    for q in range(8):
        b = q % 2
        qs = slice(q * 512, (q + 1) * 512)
        P.dma("sp", zl[b][:], zr_d[10][:, qs], writes=[("zl", b)])
        P.act(sgx[:, qs], zl[b][:], AF.Sigmoid, [("zl", b)], [("sgx", q)])

    ARbd = P.sb([128, NCK, 2, 128], BF16)
    BKbd = P.sb([128, NCK, 2, 128], BF16)
    vbd = P.sb([128, NCK, 128], BF16)
    for q in range(4):
        cq = slice(q * 16, (q + 1) * 16)
        P.memset("pool", ARbd[:, cq], 0.0, [("ARz", q)])
        P.memset("pool", BKbd[:, cq], 0.0, [("BKz", q)])
    P.memset("pool", vbd[:], 0.0, ["vbz"])
    PCt = P.sb([128, NCK], F32)
    bonv = P.sb([128, S], BF16)
    gT = P.sb([128, S], BF16)
    yT = P.sb([128, S], BF16)
    TW = 256
    zin_s = [[P.sb([128, TW], F32) for _ in range(3)] for _ in range(2)]
    T1s = [{n: P.sb([128, TW], F32) for n in ("sig", "eta", "cum", "ex", "eP", "eN", "eX", "kkk", "rn", "kk", "fac", "kp", "b", "rk")} for _ in range(2)]
    T1bs = [{n: P.sb([128, TW], BF16) for n in ("sq", "rkr")} for _ in range(2)]
    Ab = [P.sb([128, 256], BF16) for _ in range(RB)]
    Ak = [P.sb([128, 256], BF16) for _ in range(RB)]
    Tt2 = [P.sb([128, 128], BF16) for _ in range(RB)]
    Btok = [P.sb([128, 128], BF16) for _ in range(RB)]
    Ktok = [P.sb([128, 128], BF16) for _ in range(RB)]
    Vw = [P.sb([128, 128], BF16) for _ in range(RB)]
    Uw = P.sb([128, 128], BF16)
    Mk = [[P.sb([128, 128], BF16) for _ in range(2)] for _ in range(GL)]
    Lk = [[P.sb([128, 128], BF16) for _ in range(2)] for _ in range(GL)]
    X2b = P.sb([128, 128], BF16)
    ST32 = P.sb([128, 128], F32)
    STb = P.sb([128, 128], BF16)
    tmpS = P.sb([128, 128], F32)
    ynw = [P.sb([128, 128], BF16) for _ in range(2)]
    gsum = [P.sb([128, 8], F32) for _ in range(2)]
    junk = P.sb([128, 128], BF16)
    Xps = P.ps([128, 512], F32)
    Ups = P.ps([128, 512], F32)
    Yps = P.ps([128, 512], F32)
    Dps = P.ps([128, 512], F32)
    pre = [P.ps([128, 512], F32) for _ in range(GL)]

    for hp in range(3):
        NT1 = S // TW
        CPT = TW // 64

        def step1(tt):
            sb = tt % 2
            T1 = T1s[sb]
            T1b = T1bs[sb]
            zb = zin_s[sb]
            K1 = lambda n: (n, sb)
            tsl = slice(tt * TW, (tt + 1) * TW)
            t8 = (tt * TW) // 512
            for qi, ch in enumerate((hp, 3 + hp, 6 + hp)):
                P.dma("sp", zb[qi][:], zr_d[ch][:, tsl], writes=[("zin", sb, qi)])
            kr, kk_, kv = [("zin", sb, qi) for qi in range(3)]
            r_, k_, v_ = zb
            ck = slice(tt * CPT, (tt + 1) * CPT)
            zq = [("ARz", q) for q in range(4)] + [("BKz", q) for q in range(4)] + ["vbz"]

            def v3h(t, h):
                return t[64 * h:64 * h + 64, :].rearrange("p (c e) -> p c e", e=64)

            def bd(T_, slot, h):
                return T_[64 * h:64 * h + 64, ck, slot, 64 * h:64 * h + 64]
            P.mm(Xps[:, 0:TW], w2b[:, hp * 128:(hp + 1) * 128], tx[:, tsl], True, True, ["w2b", ("tx", t8)], ["Xps"])
            P.mm(Ups[:, 0:TW], a2b[:, hp * 128:(hp + 1) * 128], tx[:, tsl], True, True, ["a2b", ("tx", t8)], ["Ups"])
            P.mm(Yps[:, 0:TW], g2b[:, hp * 128:(hp + 1) * 128], sgx[:, tsl], True, True, ["g2b", ("sgx", t8)], ["Yps"])
            P.act(T1["sig"][:], Xps[:, 0:TW], AF.Sigmoid, ["Xps", "w0T"], [K1("sig")], bias=w0T[:, hp:hp + 1])
            P.act(T1["eta"][:], Ups[:, 0:TW], AF.Sigmoid, ["Ups", "a0T"], [K1("eta")], bias=a0T[:, hp:hp + 1])
            P.copy("act", gT[:, tsl], Yps[:, 0:TW], ["Yps"], [("gT", tt)])
            P.act(T1["kkk"][:], k_[:], AF.Copy, [kk_, "kkT"], [K1("kkk")], scale=kkT[:, hp:hp + 1])
            P.act(T1b["sq"][:], k_[:], AF.Square, [kk_, "kkT"], [K1("sq")], scale=kkT[:, hp:hp + 1])
            yield
            P.op("dve", lambda e: e.tensor_tensor_scan(out=T1["cum"][:], data0=rst[:, 0:TW], data1=T1["sig"][:], initial=0.0, op0=ALU.mult, op1=ALU.add),
                 ["rst", K1("sig")], [K1("cum")])
            P.mm(Dps[:, 0:TW], blk1[:], T1b["sq"][:], True, True, ["blk1", K1("sq")], ["Dps"])
            P.ts("dve", T1["fac"][:], T1["eta"][:], kaT[:, hp:hp + 1], omka[:, hp:hp + 1], ALU.mult, ALU.add, [K1("eta"), "kaT", "omka"], [K1("fac")])
            P.tt("dve", T1["ex"][:], T1["cum"][:], T1["sig"][:], ALU.subtract, [K1("cum"), K1("sig")], [K1("ex")])
            yield
            P.act(T1["rn"][:], Dps[:, 0:TW], AF.Sqrt, ["Dps"], [K1("rn")])
            P.act(T1["eP"][:], T1["cum"][:], AF.Exp, [K1("cum")], [K1("eP")], scale=-C0)
            P.act(T1["eN"][:], T1["cum"][:], AF.Exp, [K1("cum")], [K1("eN")], scale=C0)
            P.act(T1["eX"][:], T1["ex"][:], AF.Exp, [K1("ex")], [K1("eX")], scale=-C0)
            P.tt("pool", T1["kp"][:], k_[:], T1["fac"][:], ALU.mult, [kk_, K1("fac")], [K1("kp")])
            yield
            P.ts("dve", T1["rn"][:], T1["rn"][:], 1e-12, None, ALU.max, None, [], [K1("rn")])
            P.op("dve", lambda e: e.reciprocal(out=T1["rn"][:], in_=T1["rn"][:]), [], [K1("rn")])
            P.copy("dve", PCt[:, tt * CPT:(tt + 1) * CPT], T1["eP"][:, 63:TW:64], [K1("eP")], [("PCt", tt)])
            P.tt("dve", T1["kk"][:], T1["kkk"][:], T1["rn"][:], ALU.mult, [K1("kkk"), K1("rn")], [K1("kk")])
            P.tt("dve", T1["b"][:], T1["kk"][:], T1["eta"][:], ALU.mult, [K1("kk"), K1("eta")], [K1("b")])
            for h in range(2):
                P.tt("pool", bd(ARbd, 1, h), v3h(r_, h), v3h(T1["eP"], h), ALU.mult, [kr, K1("eP")] + zq, [("AR", tt, 1, h)])
            P.tt("pool", T1["rk"][:], r_[:], T1["kp"][:], ALU.mult, [kr, K1("kp")], [K1("rk")])
            yield
            for h in range(2):
                P.tt("pool", bd(BKbd, 1, h), v3h(T1["kp"], h), v3h(T1["eN"], h), ALU.mult, [K1("kp"), K1("eN")] + zq, [("BK", tt, 1, h)])
                P.tt("dve", bd(BKbd, 0, h), v3h(T1["b"], h), v3h(T1["eN"], h), ALU.mult, [K1("b"), K1("eN")] + zq, [("BK", tt, 0, h)])
                P.stt(bd(ARbd, 0, h), v3h(T1["kk"], h), -1.0, v3h(T1["eX"], h), ALU.mult, ALU.mult, [K1("kk"), K1("eX")] + zq, [("AR", tt, 0, h)])
                P.copy("act", vbd[64 * h:64 * h + 64, ck, 64 * h:64 * h + 64], v3h(v_, h), [kv] + zq, [("vbd", tt, h)])
            P.act(T1b["rkr"][:], T1["rk"][:], AF.Copy, [K1("rk"), "rkT"], [K1("rkr")], scale=rkT[:, hp:hp + 1])
            yield
            P.mm(Dps[:, 0:TW], blk1[:], T1b["rkr"][:], True, True, ["blk1", K1("rkr")], ["Dps"])
            P.tt("dve", bonv[:, tsl], Dps[:, 0:TW], v_[:], ALU.mult, ["Dps", kv], [("bonv", tt)])
            yield

        pipeline([step1(tt) for tt in range(NT1)], 2)
        NT8 = NT1
        ARk = [("AR", tt, q, h) for tt in range(NT1) for q in range(2) for h in range(2)]
        BKk = [("BK", tt, q, h) for tt in range(NT1) for q in range(2) for h in range(2)]
        vbk = [("vbd", tt, h) for tt in range(NT1) for h in range(2)]
        PCk = [("PCt", tt) for tt in range(NT1)]
        P.memset("dve", ST32[:], 0.0, ["ST32"])
        P.memset("dve", STb[:], 0.0, ["STb"])
        if RWKV_DEBUG == 1:
            break

        def precompute(n, sl_):
            rb = n % RB
            Bk = pre[sl_]
            kA = ("pre", sl_)
            R0 = Bk[:, 0:256]
            R1 = Bk[:, 256:512]
            ARn = ARbd[:, n, :, :].rearrange("p a e -> p (a e)")
            Abd = ARbd[:, n, 0, :]
            Bbd = BKbd[:, n, 0, :]
            Kbd = BKbd[:, n, 1, :]
            P.mm(R0, Bbd, ARn, True, True, ARk + BKk, [kA])
            P.mm(R1, Kbd, ARn, True, True, ARk + BKk, [kA])
            yield
            P.tt("dve", Ab[rb][:], R0, mask2[:], ALU.mult, ["mask2"], [kA, ("Ab", rb)])
            P.tt("dve", Ak[rb][:], R1, mask2[:], ALU.mult, ["mask2"], [kA, ("Ak", rb)])
            P.mm(R0[:, 0:128], Abd, Bbd, True, True, ARk + BKk, [kA])
            Bv = Bk.bitcast(BF16)
            P.tr(Bv[:, 512:640], Bbd, identb[:], BKk + ["ident"], [kA])
            P.tr(Bv[:, 640:768], Kbd, identb[:], BKk + ["ident"], [kA])
            P.tr(Bv[:, 768:896], vbd[:, n, :], identb[:], vbk + ["ident"], [kA])
            yield
            P.tt("dve", Lk[sl_][0][:], R0[:, 0:128], maskL[:], ALU.mult, ["maskL"], [kA, ("Lk", sl_, 0)])
            P.tt("pool", Tt2[rb][:], Ab[rb][:, 0:128], identb[:], ALU.add, [("Ab", rb), "ident"], [("Tt2", rb)])
            P.copy("act", Btok[rb][:], Bv[:, 512:640], [], [kA, ("Btok", rb)])
            P.copy("act", Ktok[rb][:], Bv[:, 640:768], [], [kA, ("Ktok", rb)])
            P.copy("act", Vw[rb][:], Bv[:, 768:896], [], [kA, ("Vw", rb)])
            yield
            pp = 0
            for lev in range(1, 6):
                if lev == 1:
                    Mo, kMo = Ab[rb][:, 0:128], ("Ab", rb)
                else:
                    Mo, kMo = Mk[sl_][pp][:], ("Mk", sl_, pp)
                Lo = Lk[sl_][pp]
                Mn, Ln = Mk[sl_][1 - pp], Lk[sl_][1 - pp]
                if lev < 5:
                    P.mm(R0[:, 0:128], Lo[:], Mo, True, True, [("Lk", sl_, pp), kMo], [kA])
                P.mm(R1[:, 0:128], Mo, Lo[:], True, True, [("Lk", sl_, pp), kMo], [kA])
                yield
                if lev < 5:
                    P.copy("act", Mn[:], R0[:, 0:128], [], [kA, ("Mk", sl_, 1 - pp)])
                P.copy("act", Ln[:], R1[:, 0:128], [], [kA, ("Lk", sl_, 1 - pp)])
                P.mm(R0[:, 128:256], Ln[:], Tt2[rb][:], True, True, [("Lk", sl_, 1 - pp), ("Tt2", rb)], [kA])
                yield
                P.tt("dve", Tt2[rb][:], R0[:, 128:256], Tt2[rb][:], ALU.add, [], [kA, ("Tt2", rb)])
                pp = 1 - pp
            yield

        def chain(n):
            rb = n % RB
            t0 = n * 64
            X = Xps[:, 0:128]
            U = Ups[:, 0:128]
            Y = Yps[:, 0:128]
            Dd = Dps[:, 0:128]
            Abd = ARbd[:, n, 0, :]
            Rbd = ARbd[:, n, 1, :]
            P.mm(X, Abd, STb[:], True, False, ARk + ["STb"], ["Xps"])
            P.mm(X, Ak[rb][:, 0:128], Vw[rb][:], False, True, [("Ak", rb), ("Vw", rb)], ["Xps"])
            yield
            P.copy("dve", X2b[:], X, ["Xps"], ["X2b"])
            P.mm(U, Tt2[rb][:], X2b[:], True, True, [("Tt2", rb), "X2b"], ["Ups"])
            yield
            P.copy("dve", Uw[:], U, ["Ups"], ["Uw"])
            P.mm(Dd, Btok[rb][:], Uw[:], True, False, [("Btok", rb), "Uw"], ["Dps"])
            P.mm(Dd, Ktok[rb][:], Vw[rb][:], False, True, [("Ktok", rb), ("Vw", rb)], ["Dps"])
            P.mm(Y, Rbd, STb[:], True, False, ARk + ["STb"], ["Yps"])
            P.mm(Y, Ab[rb][:, 128:256], Uw[:], False, False, [("Ab", rb), "Uw"], ["Yps"])
            P.mm(Y, Ak[rb][:, 128:256], Vw[rb][:], False, True, [("Ak", rb), ("Vw", rb)], ["Yps"])
            yield
            P.tt("dve", tmpS[:], Dd, ST32[:], ALU.add, ["Dps", "ST32"], ["tmpS"])
            P.ts("dve", STb[:], tmpS[:], PCt[:, n:n + 1], None, ALU.mult, None, ["tmpS"] + PCk, ["STb"])
            P.act(ST32[:], tmpS[:], AF.Copy, ["tmpS"] + PCk, ["ST32"], scale=PCt[:, n:n + 1])
            yb = n % 2
            gs = gsum[yb]
            gk = ("gs", yb)
            P.op("dve", lambda e: e.tensor_reduce(out=gs[:, 0:1], in_=Y, axis=AX.X, op=ALU.add), [], ["Yps", gk])
            P.ts("dve", gs[:, 1:2], gs[:, 0:1], -1.0 / 64, None, ALU.mult, None, [], [gk])
            P.act(junk[:], Y, AF.Square, [], ["Yps", "junk", ("gq", yb)], accum_out=gs[:, 2:3])
            yield
            P.tt("dve", gs[:, 3:4], gs[:, 1:2], gs[:, 1:2], ALU.mult, [], [gk])
            P.ts("dve", gs[:, 4:5], gs[:, 3:4], -1.0, GN_EPS, ALU.mult, ALU.add, [], [gk])
            P.act(gs[:, 5:6], gs[:, 2:3], AF.Sqrt, [gk, ("gq", yb)], [("gr", yb)], scale=1.0 / 64, bias=gs[:, 4:5])
            P.op("dve", lambda e: e.reciprocal(out=gs[:, 6:7], in_=gs[:, 5:6]), [("gr", yb)], [("gi", yb)])
            P.ts("dve", ynw[yb][:], Y, gs[:, 1:2], gs[:, 6:7], ALU.add, ALU.mult, [gk, ("gi", yb)], ["Yps", ("ynw", yb)])
            yt_ps = Yps.bitcast(BF16)[:, 512 + 128 * yb:512 + 128 * (yb + 1)]
            P.tr(yt_ps, ynw[yb][:], identb[:], [("ynw", yb), "ident"], ["Yps"])
            P.copy("act", yT[0:64, t0:t0 + 64], yt_ps[0:64, 0:64], [], ["Yps", ("yT", n, 0)])
            P.copy("act", yT[64:128, t0:t0 + 64], yt_ps[64:128, 64:128], [], ["Yps", ("yT", n, 1)])
            yield

        def run(gens):
            for g_ in gens:
                for _ in g_:
                    pass

        def interleave(ga, gb):
            alive = [ga, gb]
            while alive:
                for g_ in list(alive):
                    try:
                        next(g_)
                    except StopIteration:
                        alive.remove(g_)

        def lockstep(gs_):
            alive = list(gs_)
            while alive:
                for g_ in list(alive):
                    try:
                        next(g_)
                    except StopIteration:
                        alive.remove(g_)
                yield

        def seq(gs_):
            for g_ in gs_:
                for _ in g_:
                    yield

        ngroups = NCK // GL
        run([lockstep([precompute(q, q) for q in range(GL)])])
        if RWKV_DEBUG == 2:
            break
        for gi in range(ngroups):
            ch = seq([chain(GL * gi + q) for q in range(GL)])
            if gi + 1 < ngroups:
                prg = lockstep([precompute(GL * (gi + 1) + q, q) for q in range(GL)])
                interleave(ch, prg)
            else:
                run([ch])
        yk = [("yT", n, s) for n in range(NCK) for s in range(2)]
        for q in range(4):
            qs = slice(q * 1024, (q + 1) * 1024)
            tq = list(range(q * (1024 // TW), (q + 1) * (1024 // TW)))
            P.ts("dve", yT[:, qs], yT[:, qs], lnwT[:, hp:hp + 1], lnbT[:, hp:hp + 1], ALU.mult, ALU.add, yk + ["lnwT", "lnbT"], [("yTf", q)])
            P.tt("pool", bonv[:, qs], yT[:, qs], bonv[:, qs], ALU.add, [("yTf", q)], [("bonv", t_) for t_ in tq])
            P.tt("pool", bonv[:, qs], bonv[:, qs], gT[:, qs], ALU.mult, [("gT", t_) for t_ in tq], [("bonv", t_) for t_ in tq])
        P.dma("sp", brT_d[3 + hp], bonv[:], reads=[("bonv", t_) for t_ in range(NT1)], writes=[("brT_d", 3 + hp)])


def phase_merge(P, brT_d, pT_d, wba, wbr, wbs, mT_d):
    brT = P.sb([128, 8, S], BF16)
    for t_ in range(8):
        P.dma("sp" if t_ % 2 == 0 else "act", brT[:, :, t_ * 512:(t_ + 1) * 512],
              brT_d[:, :, t_ * 512:(t_ + 1) * 512].rearrange("k p t -> p k t"), writes=[("brT", t_)])
    wall = P.sb([128, 8, 1024], BF16)
    P.dma("pool", wall[:, 0:3, :], wba.rearrange("(k p) c -> p k c", p=128), writes=[("wall", 0)])
    P.dma("pool", wall[:, 3:6, :], wbr.rearrange("(k p) c -> p k c", p=128), writes=[("wall", 1)])
    P.dma("pool", wall[:, 6:8, :], wbs.rearrange("(k p) c -> p k c", p=128), writes=[("wall", 2)])
    gt = [P.sb([128, 3, S], BF16) for _ in range(2)]
    stage = [P.sb([128, S], BF16) for _ in range(2)]
    NM = 4
    m = [P.sb([128, 3, 512], BF16) for _ in range(NM)]
    banks = [P.ps([128, 512], F32) for _ in range(8)]
    kr = ((0, 3), (3, 6), (6, 8))

    def unit(dc, tt, ui):
        gb = dc % 2
        mb = ui % NM
        tsl = slice(tt * 512, (tt + 1) * 512)
        if tt == 0:
            if dc == 0:
                for b3 in range(3):
                    P.dma("act", gt[0][:, b3, :], pT_d[22 + b3 * 8], writes=[("gt", 0, b3)])
            if dc + 1 < 8:
                ngb = (dc + 1) % 2
                for b3 in range(3):
                    P.dma("act", gt[ngb][:, b3, :], pT_d[22 + b3 * 8 + dc + 1], writes=[("gt", ngb, b3)])
        bks = [(3 * ui + b3) % 8 for b3 in range(3)]
        for b3 in range(3):
            k0, k1 = kr[b3]
            for k in range(k0, k1):
                P.mm(banks[bks[b3]][:], wall[:, k, dc * 128:(dc + 1) * 128], brT[:, k, tsl], k == k0, k == k1 - 1,
                     [("wall", b3), ("brT", tt)], [("bank", bks[b3])])
        yield
        for b3 in range(3):
            P.tt("dve", m[mb][:, b3, :], banks[bks[b3]][:], gt[gb][:, b3, tsl], ALU.mult, [("bank", bks[b3]), ("gt", gb, b3)], [("m", mb, b3)])
        yield
        P.tt("pool", m[mb][:, 0, :], m[mb][:, 0, :], m[mb][:, 1, :], ALU.add, [("m", mb, 1)], [("m", mb, 0)])
        yield
        P.tt("dve", stage[gb][:, tsl], m[mb][:, 0, :], m[mb][:, 2, :], ALU.add, [("m", mb, 0), ("m", mb, 2)], [("stage", gb, tt)])
        if tt == 7:
            P.dma("sp", mT_d[:, :, dc, :].rearrange("tt p t -> p tt t"), stage[gb][:].rearrange("p (tt t) -> p tt t", tt=8), reads=[("stage", gb, t_) for t_ in range(8)], writes=[("mT_d", dc)])
        yield

    units = []
    for dc in range(8):
        for tt in range(8):
            units.append(unit(dc, tt, len(units)))
    pipeline(units, 4)


def load_down_w(P, nk, w_ap):
    wsb = P.sb([128, nk, 1024], BF16)
    for k0 in range(0, nk, 4):
        k1 = min(nk, k0 + 4)
        P.dma("pool", wsb[:, k0:k1, :], w_ap[k0 * 128:k1 * 128, :].rearrange("(k p) c -> p k c", p=128), writes=[("wsb", k0)])
    return wsb


def with_prefetch(P, nk, w_ap, first, second):
    wsb = load_down_w(P, nk, w_ap)
    outer = P.stack
    with ExitStack() as st2:
        P.stack = st2
        first(P)
        P.emit()
    P.stack = outer
    second(P, wsb)


def phase_down(P, aT_d, nk, w_ap, hsrc, hdst, norm=None, final=None, wsb=None):
    if wsb is None:
        wsb = load_down_w(P, nk, w_ap)
        wk = [("wsb", k0) for k0 in range(0, nk, 4)]
    else:
        wk = []
    at = [P.sb([128, nk, 512], BF16) for _ in range(2)]
    hb = [P.sb([128, 1024], F32) for _ in range(3)]
    banks = [P.ps([128, 1024], F32) for _ in range(3)]
    eps = P.sb([128, 1], F32)
    P.memset("pool", eps[:], 1e-6, ["eps"])
    ssq = [P.sb([128, 2], F32) for _ in range(3)]
    junk = P.sb([128, 1024], BF16)
    if norm is not None:
        gain_ap, uT_d = norm
        ident = make_ident(P)
        gT = P.sb([128, 8], F32)
        P.dma("sp", gT[:], gain_ap.rearrange("(k p) -> p k", p=128), writes=["gT"], allow_slow_non_contiguous=True)
        ub = [P.sb([128, 1024], BF16) for _ in range(3)]
        pst2_ = [P.ps([128, 1024], BF16) for _ in range(2)]
        pst2 = [t[:].rearrange("p (k t) -> p k t", k=8) for t in pst2_]
        uTt = [P.sb([128, 8, 512], BF16) for _ in range(2)]
    if final is not None:
        gain_ap, y_d = final
        g1 = P.sb([1, 1024], F32)
        P.dma("sp", g1[:], gain_ap.rearrange("(o d) -> o d", o=1), writes=["g1"])
        ones = P.sb([1, 128], F32)
        P.memset("pool", ones[:], 1.0, ["ones"])
        gbt = P.sb([128, 1024], F32)
        for half in range(2):
            P.mm(banks[0][:, half * 512:(half + 1) * 512], ones[:], g1[:, half * 512:(half + 1) * 512], True, True, ["ones", "g1"], [("bank", 0)])
        P.copy("dve", gbt[:], banks[0][:], [("bank", 0)], ["gbt"])
        ob = [P.sb([128, 1024], F32) for _ in range(2)]
    def unit(tt, s, ui):
        ab = tt % 2
        t0 = tt * 512 + s * 128
        hbk = ui % 3
        bk = ui % 3
        ub_i = ui % 3
        if s == 0:
            if tt == 0:
                P.dma("sp", at[0][:], aT_d[0], writes=[("at", 0)])
            if tt + 1 < 8:
                nab = (tt + 1) % 2
                P.dma("sp", at[nab][:], aT_d[tt + 1], writes=[("at", nab)])
        P.dma("sp", hb[hbk][:], hsrc[t0:t0 + 128, :], writes=[("hb", hbk)])
        for half in range(2):
            for k in range(nk):
                P.mm(banks[bk][:, half * 512:(half + 1) * 512], at[ab][:, k, s * 128:(s + 1) * 128], wsb[:, k, half * 512:(half + 1) * 512],
                     k == 0, k == nk - 1, [("at", ab)] + wk, [("bank", bk)])
        yield
        P.tt("dve", hb[hbk][:], banks[bk][:], hb[hbk][:], ALU.add, [("bank", bk)], [("hb", hbk)])
        if final is None:
            P.dma("pool", hdst[t0:t0 + 128, :], hb[hbk][:], reads=[("hb", hbk)], writes=[("hdst", t0)])
        if norm is not None or final is not None:
            sq = ssq[hbk]
            P.act(junk[:], hb[hbk][:], AF.Square, [("hb", hbk)], ["junk", ("ssq", hbk)], accum_out=sq[:, 0:1])
            P.act(sq[:, 1:2], sq[:, 0:1], AF.Sqrt, ["eps"], [("ssq", hbk)], scale=1.0 / D, bias=eps[:])
            P.op("dve", lambda e, o=sq: e.reciprocal(out=o[:, 1:2], in_=o[:, 1:2]), [], [("ssq", hbk)])
        yield
        if norm is not None:
            P.ts("dve", ub[ub_i][:], hb[hbk][:], sq[:, 1:2], None, ALU.mult, None, [("hb", hbk), ("ssq", hbk)], [("ub", ub_i)])
        if final is not None:
            ob_i = ui % 2
            P.stt(ob[ob_i][:], hb[hbk][:], sq[:, 1:2], gbt[:], ALU.mult, ALU.mult, [("hb", hbk), ("ssq", hbk), "gbt"], [("ob", ob_i)])
            P.dma("pool", y_d[t0:t0 + 128, :], ob[ob_i][:], reads=[("ob", ob_i)], writes=[("y", t0)])
        yield
        if norm is not None:
            pst = pst2[ui % 2]
            pk = ("pst", ui % 2)
            for k in range(8):
                P.tr(pst[:, k, :], ub[ub_i][:, k * 128:(k + 1) * 128], ident[:], [("ub", ub_i), "ident"], [pk])
            yield
            for k in range(8):
                if ui % 2 == 0:
                    P.act(uTt[ab][:, k, s * 128:(s + 1) * 128], pst[:, k, :], AF.Copy, ["gT"], [pk, ("uTt", ab, s)], scale=gT[:, k:k + 1])
                else:
                    P.ts("dve", uTt[ab][:, k, s * 128:(s + 1) * 128], pst[:, k, :], gT[:, k:k + 1], None, ALU.mult, None, ["gT"], [pk, ("uTt", ab, s)])
            if s == 3:
                P.dma("pool", uT_d[tt], uTt[ab][:],
                      reads=[("uTt", ab, q) for q in range(4)], writes=[("uT_d", tt)])
        yield

    units = []
    for tt in range(8):
        for s in range(4):
            units.append(unit(tt, s, len(units)))
    pipeline(units, 3)


def phase_ffn_up(P, uT_d, wgu, hidT_d):
    uT = P.sb([128, 8, S], BF16)
    for t_ in range(8):
        P.dma("sp" if t_ % 2 == 0 else "act", uT[:, :, t_ * 512:(t_ + 1) * 512],
              uT_d[t_], writes=[("uT", t_)])
    wa = [P.sb([128, 8, 512], BF16) for _ in range(2)]
    wg = [P.sb([128, 8, 512], BF16) for _ in range(2)]
    stage = [P.sb([128, S], BF16) for _ in range(2)]
    sa = [P.sb([128, 512], F32) for _ in range(2)]
    banks = [P.ps([128, 512], F32) for _ in range(6)]
    nb = 0
    it = 0
    for jg in range(6):
        j0 = jg * 4
        nj = min(4, 22 - j0)
        wb_ = jg % 2
        P.dma("pool", wa[wb_][:, :, 0:nj * 128], wgu[:, j0 * 128:(j0 + nj) * 128].rearrange("(k p) c -> p k c", p=128), writes=[("wa", wb_)])
        P.dma("pool", wg[wb_][:, :, 0:nj * 128], wgu[:, FH + j0 * 128:FH + (j0 + nj) * 128].rearrange("(k p) c -> p k c", p=128), writes=[("wg", wb_)])
        for jj in range(nj):
            j = j0 + jj
            st = stage[j % 2]
            for tt in range(8):
                tsl = slice(tt * 512, (tt + 1) * 512)
                ba = nb % 6
                bb = (nb + 1) % 6
                nb += 2
                sb_ = it % 2
                it += 1
                for k in range(8):
                    P.mm(banks[ba][:], wa[wb_][:, k, jj * 128:(jj + 1) * 128], uT[:, k, tsl], k == 0, k == 7, [("wa", wb_), ("uT", tt)], [("bank", ba)])
                for k in range(8):
                    P.mm(banks[bb][:], wg[wb_][:, k, jj * 128:(jj + 1) * 128], uT[:, k, tsl], k == 0, k == 7, [("wg", wb_), ("uT", tt)], [("bank", bb)])
                P.act(sa[sb_][:], banks[ba][:], AF.Silu, [("bank", ba)], [("sa", sb_)])
                P.tt("dve", st[:, tsl], banks[bb][:], sa[sb_][:], ALU.mult, [("bank", bb), ("sa", sb_)], [("stage", j % 2, tt)])
            P.dma("sp", hidT_d[:, :, j, :].rearrange("tt p t -> p tt t"), st[:].rearrange("p (tt t) -> p tt t", tt=8), reads=[("stage", j % 2, tt) for tt in range(8)], writes=[("hidT_d", j)])


def phase_final(P, hsrc, gain_ap, y_d):
    g1 = P.sb([1, 1024], F32)
    P.dma("sp", g1[:], gain_ap.rearrange("(o d) -> o d", o=1), writes=["g1"])
    ones = P.sb([1, 128], F32)
    P.memset("dve", ones[:], 1.0, ["ones"])
    gps = P.ps([128, 1024], F32)
    gb = P.sb([128, 1024], F32)
    for half in range(2):
        P.mm(gps[:, half * 512:(half + 1) * 512], ones[:], g1[:, half * 512:(half + 1) * 512], True, True, ["ones", "g1"], [("gps", half)])
    P.copy("dve", gb[:], gps[:], [("gps", 0), ("gps", 1)], ["gb"])
    eps = P.sb([128, 1], F32)
    P.memset("dve", eps[:], 1e-6, ["eps"])
    hb = [P.sb([128, 1024], F32) for _ in range(2)]
    ob = [P.sb([128, 1024], F32) for _ in range(2)]
    junk = P.sb([128, 1024], BF16)
    ss = [P.sb([128, 1], F32) for _ in range(2)]
    for t in range(32):
        b = t % 2
        P.dma("sp", hb[b][:], hsrc[t * 128:(t + 1) * 128, :], writes=[("hb", b)])
        P.act(junk[:], hb[b][:], AF.Square, [("hb", b)], ["junk", ("ss", b)], accum_out=ss[b][:])
        P.act(ss[b][:], ss[b][:], AF.Sqrt, ["eps"], [("ss", b)], scale=1.0 / D, bias=eps[:])
        P.op("dve", lambda e, o=ss[b]: e.reciprocal(out=o[:], in_=o[:]), [], [("ss", b)])
        P.stt(ob[b][:], hb[b][:], ss[b][:, 0:1], gb[:], ALU.mult, ALU.mult, [("hb", b), ("ss", b), "gb"], [("ob", b)])
        P.dma("act", y_d[t * 128:(t + 1) * 128, :], ob[b][:], reads=[("ob", b)], writes=[("y", t)])


WEIGHT_SPECS = [
    ("norm_mix", [NL, D]), ("w_in", [NL, D, N_IN]), ("rwkv_shift_mix", [NL, 1408]), ("rwkv_w0", [NL, 384]),
    ("rwkv_w2", [NL, 64, 384]), ("rwkv_a0", [NL, 384]), ("rwkv_a2", [NL, 64, 384]), ("rwkv_g2", [NL, 128, 384]),
    ("rwkv_k_k", [NL, 384]), ("rwkv_k_a", [NL, 384]), ("rwkv_r_k", [NL, 6, 64]), ("rwkv_ln_w", [NL, 384]),
    ("rwkv_ln_b", [NL, 384]), ("ssm_a_re", [NL, 16, 64]), ("ssm_a_im", [NL, 16, 64]), ("ssm_log_step", [NL, 16]),
    ("ssm_b_re", [NL, 16, 64, 16]), ("ssm_b_im", [NL, 16, 64, 16]), ("ssm_c_re", [NL, 16, 16, 64]),
    ("ssm_c_im", [NL, 16, 16, 64]), ("ssm_d", [NL, 256]), ("ssm_glu_val", [NL, 256, 256]), ("ssm_glu_gate", [NL, 256, 256]),
    ("w_branch_attn", [NL, 384, D]), ("w_branch_rwkv", [NL, 384, D]), ("w_branch_ssm", [NL, 256, D]), ("w_out", [NL, D, D]),
    ("norm_ffn", [NL, D]), ("ffn_w_gate_up", [NL, D, 2 * FH]), ("ffn_w_down", [NL, FH, D]), ("norm_final", [D]),
]

SCRATCH_SPECS = [
    ("hb", [S, D], F32), ("uT", [8, 128, 8, 512], BF16), ("pT", [NCH, 128, S], BF16), ("zr", [11, 128, S], F32),
    ("ao", [3, S, 130], F32), ("brT", [8, 128, S], BF16), ("mT", [8, 128, 8, 512], BF16), ("hidT", [8, 128, 22, 512], BF16),
]


def build_program(phases=None, debug_out=()):
    nc = bass.Bass("TRN2", target_bir_lowering=False)
    T = {}
    T["x"] = nc.dram_tensor("x", [S, D], F32, kind="ExternalInput").ap()
    for name, shp in WEIGHT_SPECS:
        T[name] = nc.dram_tensor(name, shp, F32, kind="ExternalInput").ap()
    for name, shp, dt in SCRATCH_SPECS:
        kind = "ExternalOutput" if name in debug_out else "Internal"
        T[name] = nc.dram_tensor(name, shp, dt, kind=kind).ap()
    T["y"] = nc.dram_tensor("y", [S, D], F32, kind="ExternalOutput").ap()

    plist = []
    for l in range(NL):
        hsrc = T["x"] if l == 0 else T["hb"]
        if l == 0:
            plist.append(("norm%d" % l, lambda P, l=l, hsrc=hsrc: phase_norm(P, hsrc, T["norm_mix"][l], T["uT"])))
        plist.append(("proj%d" % l, lambda P, l=l: phase_proj(P, T["uT"], T["w_in"][l], T["rwkv_shift_mix"][l], T["pT"], T["zr"])))
        plist.append(("attn%d" % l, lambda P, l=l: phase_attn(P, T["pT"], T["ao"])))
        plist.append(("attnc%d" % l, lambda P, l=l: phase_attn_combine(P, T["ao"], T["brT"])))
        plist.append(("s5_%d" % l, lambda P, l=l: phase_s5(P, T["pT"], {
            "a_re": T["ssm_a_re"][l], "a_im": T["ssm_a_im"][l], "log_step": T["ssm_log_step"][l], "b_re": T["ssm_b_re"][l],
            "b_im": T["ssm_b_im"][l], "c_re": T["ssm_c_re"][l], "c_im": T["ssm_c_im"][l], "d": T["ssm_d"][l],
            "glu_val": T["ssm_glu_val"][l], "glu_gate": T["ssm_glu_gate"][l]}, T["brT"])))
        plist.append(("rwkv%d" % l, lambda P, l=l: phase_rwkv(P, T["zr"], {
            "w0": T["rwkv_w0"][l], "w2": T["rwkv_w2"][l], "a0": T["rwkv_a0"][l], "a2": T["rwkv_a2"][l], "g2": T["rwkv_g2"][l],
            "k_k": T["rwkv_k_k"][l], "k_a": T["rwkv_k_a"][l], "r_k": T["rwkv_r_k"][l], "ln_w": T["rwkv_ln_w"][l],
            "ln_b": T["rwkv_ln_b"][l]}, T["brT"])))
        def f_merge(P, l=l):
            phase_merge(P, T["brT"], T["pT"], T["w_branch_attn"][l], T["w_branch_rwkv"][l], T["w_branch_ssm"][l], T["mT"])

        def f_out(P, wsb, l=l, hsrc=hsrc):
            phase_down(P, T["mT"], 8, T["w_out"][l], hsrc, T["hb"], norm=(T["norm_ffn"][l], T["uT"]), wsb=wsb)

        def f_up(P, l=l):
            phase_ffn_up(P, T["uT"], T["ffn_w_gate_up"][l], T["hidT"])

        def f_dn(P, wsb, l=l):
            if l + 1 < NL:
                phase_down(P, T["hidT"], 22, T["ffn_w_down"][l], T["hb"], T["hb"], norm=(T["norm_mix"][l + 1], T["uT"]), wsb=wsb)
            else:
                phase_down(P, T["hidT"], 22, T["ffn_w_down"][l], T["hb"], T["hb"], final=(T["norm_final"], T["y"]), wsb=wsb)
        plist.append(("mergeout%d" % l, lambda P, l=l, a=f_merge, b=f_out: with_prefetch(P, 8, T["w_out"][l], a, b)))
        plist.append(("ffn%d" % l, lambda P, l=l, a=f_up, b=f_dn: with_prefetch(P, 22, T["ffn_w_down"][l], a, b)))

    with ExitStack() as st:
        P = Prog(nc, st)
        for name, fn in plist:
            if phases is not None and name not in phases:
                continue
            with ExitStack() as pst:
                P.stack = pst
                fn(P)
                P.emit()
        nins = P.nins
    return nc, nins


def kernel(**inputs):
    nc, _ = build_program()
    x = np.ascontiguousarray(inputs["x"], dtype=np.float32)
    wmap = {name: np.ascontiguousarray(inputs[name], dtype=np.float32) for name, _ in WEIGHT_SPECS}
    in_maps = []
    for c in range(8):
        m = dict(wmap)
        m["x"] = x[c]
        in_maps.append(m)
    res = run_bass_kernel_spmd(nc, in_maps, core_ids=list(range(8)))
    return np.stack([np.asarray(r["y"], dtype=np.float32) for r in res.results], axis=0)
```
